# Optimizing a Trainium2 kernel written in Bass

```python
import numpy as np
import jax
import jax.numpy as jnp
from jax import lax

D_MODEL = 2048
BATCH = 4
SEQ = 4096
DEPTH = 4

F32 = jnp.float32
HEAD_DIM = 64
D_MIX = D_MODEL
D_RWKV = (3 * D_MIX) // 8
D_POOL = D_MIX // 4
D_NSA = D_MIX - D_RWKV - D_POOL
RW_HEADS = D_RWKV // HEAD_DIM
RW_DECAY_LORA = 64
RW_A_LORA = 64
RW_GATE_LORA = 128
RW_GN_EPS = 64e-5
RW_COLS = 3 * D_RWKV + RW_DECAY_LORA + RW_A_LORA + RW_GATE_LORA
RW_SPLITS = (D_RWKV, 2 * D_RWKV, 3 * D_RWKV, 3 * D_RWKV + RW_DECAY_LORA,
             3 * D_RWKV + RW_DECAY_LORA + RW_A_LORA)
POOL_WINDOWS = (2, 4, 8, 16)
POOL_GROUP = D_POOL // 4
NSA_HEADS = D_NSA // HEAD_DIM
NSA_KV_HEADS = 3
NSA_GQA = NSA_HEADS // NSA_KV_HEADS
NSA_KV = NSA_KV_HEADS * HEAD_DIM
NSA_COLS = D_NSA + 6 * NSA_KV + 3 * NSA_HEADS
NSA_SPLITS = (D_NSA, D_NSA + NSA_KV, D_NSA + 2 * NSA_KV, D_NSA + 3 * NSA_KV,
              D_NSA + 4 * NSA_KV, D_NSA + 5 * NSA_KV, D_NSA + 6 * NSA_KV)
CMP_BLOCK = 32
CMP_STRIDE = 16
CMP_HIDDEN = 256
SEL_BLOCK = 64
SEL_TOPK = 16
FORCE_SCORE = 1e9
WINDOW = 512
Q_BLOCK = 128
SEL_Q_CHUNK = 64
ROPE_THETA = 500000.0
ROPE_DIM = HEAD_DIM // 4
D_FF = ((8 * D_MODEL // 3 + 127) // 128) * 128
IN_COLS = RW_COLS + D_POOL + NSA_COLS
ALPHA = (2 * DEPTH) ** 0.25
BETA = (8 * DEPTH) ** -0.25
LN_EPS = 1e-5

kernel_name = "hymba_rwkv7_pool_nsa_macaron_deepnorm"


def layer_norm(x, g, b):
    xf = x.astype(F32)
    mu = xf.mean(-1, keepdims=True)
    var = jnp.square(xf - mu).mean(-1, keepdims=True)
    return ((xf - mu) * lax.rsqrt(var + LN_EPS) * g + b).astype(x.dtype)


def swiglu(x, w_up, w_down):
    a, b = jnp.split(x @ w_up, 2, axis=-1)
    return (jax.nn.silu(a) * b) @ w_down


def token_shift(t):
    return jnp.pad(t[:, :-1], ((0, 0), (1, 0), (0, 0)))


def partial_rope(x, pos):
    half = ROPE_DIM // 2
    inv_freq = ROPE_THETA ** (-jnp.arange(half, dtype=F32) / half)
    ang = pos.astype(F32)[:, None] * inv_freq
    cos, sin = jnp.cos(ang)[:, None, :], jnp.sin(ang)[:, None, :]
    x1, x2, xp = x[..., :half], x[..., half:ROPE_DIM], x[..., ROPE_DIM:]
    rot = jnp.concatenate([x1 * cos - x2 * sin, x2 * cos + x1 * sin], axis=-1)
    return jnp.concatenate([rot.astype(x.dtype), xp], axis=-1)


def masked_softmax(s, mask):
    s = jnp.where(mask, s.astype(F32), -jnp.inf)
    m = jnp.max(s, axis=-1, keepdims=True)
    m = jnp.where(jnp.isfinite(m), m, 0.0)
    e = jnp.where(mask, jnp.exp(s - m), 0.0)
    return e / jnp.maximum(e.sum(-1, keepdims=True), jnp.finfo(F32).tiny)


def rwkv7_mix(p, mu, w0, w2, a0, a2, g2, k_k, k_a, r_k, gn_g, gn_b):
    B, S, _ = p.shape
    dt = p.dtype
    p = p + (token_shift(p) - p) * mu
    r, k, v, wl, al, gl = jnp.split(p, RW_SPLITS, axis=-1)
    w = -jax.nn.softplus(-(w0 + jnp.tanh(wl) @ w2).astype(F32)) - 0.5
    decay = jnp.exp(-jnp.exp(w))
    a = jax.nn.sigmoid((a0 + al @ a2).astype(F32))
    g = jax.nn.sigmoid(gl) @ g2

    def heads(t):
        return t.astype(F32).reshape(B, S, RW_HEADS, HEAD_DIM)

    kk = heads(k * k_k)
    kk = kk / jnp.maximum(jnp.linalg.norm(kk, axis=-1, keepdims=True), 1e-12)
    k_mod = k.astype(F32) * (1.0 + (a - 1.0) * k_a)
    r_h, k_h, v_h, w_h, a_h = heads(r), heads(k_mod), heads(v), heads(decay), heads(a)

    def step(state, inp):
        r_t, w_t, k_t, v_t, kk_t, a_t = inp
        sa = jnp.einsum('bhvk,bhk->bhv', state, kk_t)
        state = (state * w_t[:, :, None, :]
                 - sa[..., None] * (kk_t * a_t)[:, :, None, :]
                 + v_t[..., :, None] * k_t[..., None, :])
        return state, jnp.einsum('bhvk,bhk->bhv', state, r_t)

    xs = tuple(jnp.moveaxis(t, 1, 0) for t in (r_h, w_h, k_h, v_h, kk, a_h))
    state0 = jnp.zeros((B, RW_HEADS, HEAD_DIM, HEAD_DIM), F32)
    _, y = lax.scan(step, state0, xs)
    y = jnp.moveaxis(y, 0, 1)
    ym = y.mean(-1, keepdims=True)
    yv = jnp.square(y - ym).mean(-1, keepdims=True)
    y = ((y - ym) * lax.rsqrt(yv + RW_GN_EPS)).reshape(B, S, D_RWKV) * gn_g + gn_b
    bonus = (jnp.sum(r_h * k_h * r_k, axis=-1, keepdims=True) * v_h).reshape(B, S, D_RWKV)
    return ((y + bonus) * g).astype(dt)


def pool_mix(p, pool_w, pool_b, pool_scale):
    B, S, _ = p.shape
    n_g = len(POOL_WINDOWS)
    pf = p.astype(F32).reshape(B, S, n_g, POOL_GROUP)
    cs = jnp.cumsum(pf, axis=1)
    t1 = jnp.arange(1, S + 1, dtype=F32)
    pooled = []
    for gi, win in enumerate(POOL_WINDOWS):
        c = cs[:, :, gi]
        lagged = jnp.pad(c, ((0, 0), (win, 0), (0, 0)))[:, :S]
        count = jnp.minimum(t1, float(win))[None, :, None]
        pooled.append((c - lagged) / count - pf[:, :, gi])
    z = jnp.stack(pooled, axis=2)
    z = jnp.einsum('bsgc,gcd->bsgd', z, pool_w) + pool_b.reshape(n_g, POOL_GROUP)
    z = z * pool_scale.reshape(n_g, POOL_GROUP)
    return z.reshape(B, S, D_POOL).astype(p.dtype)


def compress(blocks, pe, w1, w2):
    h = jnp.einsum('bnlhd,ldf->bnhf', blocks + pe[:, None, :], w1)
    return jax.nn.gelu(h) @ w2


def nsa_mix(p, pos, cmp_pe_k, cmp_pe_v, cmp_k_w1, cmp_k_w2, cmp_v_w1, cmp_v_w2, gate_b):
    B, S, _ = p.shape
    dt = p.dtype
    q, kc, vc, ks, vs, kw, vw, gl = jnp.split(p, NSA_SPLITS, axis=-1)
    kv_shape = (B, S, NSA_KV_HEADS, HEAD_DIM)
    q = partial_rope(q.reshape(B, S, NSA_HEADS, HEAD_DIM), pos)
    q = q.reshape(B, S, NSA_KV_HEADS, NSA_GQA, HEAD_DIM) * (HEAD_DIM ** -0.5)
    kc, vc, vs, vw = (t.reshape(kv_shape) for t in (kc, vc, vs, vw))
    ks = partial_rope(ks.reshape(kv_shape), pos)
    kw = partial_rope(kw.reshape(kv_shape), pos)
    gates = jax.nn.sigmoid((gl + gate_b).astype(F32)).reshape(B, S, NSA_KV_HEADS, NSA_GQA, 3)

    n_cmp = (S - CMP_BLOCK) // CMP_STRIDE + 1
    cmp_start = np.arange(n_cmp) * CMP_STRIDE
    cmp_idx = cmp_start[:, None] + np.arange(CMP_BLOCK)[None, :]
    cmp_end = cmp_start + CMP_BLOCK - 1
    k_cmp = compress(kc[:, cmp_idx], cmp_pe_k, cmp_k_w1, cmp_k_w2)
    v_cmp = compress(vc[:, cmp_idx], cmp_pe_v, cmp_v_w1, cmp_v_w2)
    k_cmp = partial_rope(k_cmp, jnp.asarray(cmp_end))
    s_cmp = jnp.einsum('bshgd,bnhd->bshgn', q, k_cmp)
    mask_cmp = (jnp.asarray(cmp_end)[None, :] <= pos[:, None])[None, :, None, None, :]
    p_cmp = masked_softmax(s_cmp, mask_cmp)
    o_cmp = jnp.einsum('bshgn,bnhd->bshgd', p_cmp.astype(dt), v_cmp)

    n_sel = S // SEL_BLOCK
    k_sel = min(SEL_TOPK, n_sel)
    sel_start = np.arange(n_sel) * SEL_BLOCK
    overlap = np.clip(np.minimum(cmp_start[:, None] + CMP_BLOCK, sel_start[None, :] + SEL_BLOCK)
                      - np.maximum(cmp_start[:, None], sel_start[None, :]), 0, None) / CMP_BLOCK
    imp = jnp.einsum('bshgn,nj->bshj', p_cmp, jnp.asarray(overlap, F32))
    blk = jnp.arange(n_sel)[None, :]
    cur = (pos // SEL_BLOCK)[:, None]
    forced = (blk == 0) | (blk == cur) | (blk == cur - 1)
    score = jnp.where(forced[None, :, None, :], FORCE_SCORE, imp)
    score = jnp.where((blk <= cur)[None, :, None, :], score, -jnp.inf)
    top_val, top_idx = lax.top_k(score, k_sel)
    top_ok = top_val > -jnp.inf
    ks_blk = ks.reshape(B, n_sel, SEL_BLOCK, NSA_KV_HEADS, HEAD_DIM).transpose(0, 3, 1, 2, 4)
    vs_blk = vs.reshape(B, n_sel, SEL_BLOCK, NSA_KV_HEADS, HEAD_DIM).transpose(0, 3, 1, 2, 4)
    nc = S // SEL_Q_CHUNK
    b_ix = jnp.arange(B)[:, None, None, None]
    h_ix = jnp.arange(NSA_KV_HEADS)[None, None, :, None]
    tok = jnp.arange(SEL_BLOCK)

    def chunk_first(t):
        return jnp.moveaxis(t.reshape(B, nc, SEL_Q_CHUNK, *t.shape[2:]), 1, 0)

    def sel_chunk(args):
        qc, idx, ok, qpos = args
        kg = ks_blk[b_ix, h_ix, idx]
        vg = vs_blk[b_ix, h_ix, idx]
        kpos = idx[..., None] * SEL_BLOCK + tok
        mask = ok[..., None] & (kpos <= qpos[None, :, None, None, None])
        s = jnp.einsum('bchgd,bchkld->bchgkl', qc, kg)
        pr = masked_softmax(s.reshape(*s.shape[:4], -1), mask.reshape(*mask.shape[:3], 1, -1))
        return jnp.einsum('bchgn,bchnd->bchgd', pr.astype(dt),
                          vg.reshape(*vg.shape[:3], -1, HEAD_DIM))

    o_sel = lax.map(sel_chunk, (chunk_first(q), chunk_first(top_idx), chunk_first(top_ok),
                                pos.reshape(nc, SEL_Q_CHUNK)))
    o_sel = jnp.moveaxis(o_sel, 0, 1).reshape(B, S, NSA_KV_HEADS, NSA_GQA, HEAD_DIM)

    n_qb = S // Q_BLOCK
    span = WINDOW + Q_BLOCK
    win_idx = np.arange(n_qb)[:, None] * Q_BLOCK + np.arange(span)[None, :]
    pad = ((0, 0), (WINDOW, 0), (0, 0), (0, 0))
    kb = jnp.pad(kw, pad)[:, win_idx]
    vb = jnp.pad(vw, pad)[:, win_idx]
    kpos = (win_idx - WINDOW)[:, None, :]
    qpos = np.arange(S).reshape(n_qb, Q_BLOCK)[:, :, None]
    mask_w = (kpos >= 0) & (kpos <= qpos) & (kpos > qpos - WINDOW)
    qb = q.reshape(B, n_qb, Q_BLOCK, NSA_KV_HEADS, NSA_GQA, HEAD_DIM)
    s_w = jnp.einsum('bnqhgd,bnkhd->bnhgqk', qb, kb)
    p_w = masked_softmax(s_w, jnp.asarray(mask_w)[None, :, None, None])
    o_win = jnp.einsum('bnhgqk,bnkhd->bnqhgd', p_w.astype(dt), vb)
    o_win = o_win.reshape(B, S, NSA_KV_HEADS, NSA_GQA, HEAD_DIM)

    o = gates[..., 0:1] * o_cmp + gates[..., 1:2] * o_sel + gates[..., 2:3] * o_win
    return o.reshape(B, S, D_NSA).astype(dt)


def setup_inputs(seed: int = 0) -> dict:
    key = jax.random.key(seed)
    keys = iter(jax.random.split(key, 48))
    L = DEPTH

    def nrm(shape, scale):
        return scale * jax.random.normal(next(keys), shape, F32)

    def unif(shape, lo, hi):
        return jax.random.uniform(next(keys), shape, F32, lo, hi)

    return {
        "x": nrm((BATCH, SEQ, D_MODEL), 1.0),
        "ffn1_w_up": nrm((L, D_MODEL, 2 * D_FF), D_MODEL ** -0.5),
        "ffn1_w_down": nrm((L, D_FF, D_MODEL), BETA * D_FF ** -0.5),
        "ln1_g": 1.0 + nrm((L, D_MODEL), 0.02),
        "ln1_b": nrm((L, D_MODEL), 0.02),
        "w_in": nrm((L, D_MODEL, IN_COLS), D_MODEL ** -0.5),
        "rw_mu": unif((L, RW_COLS), 0.0, 1.0),
        "rw_w0": unif((L, D_RWKV), -6.0, -1.0),
        "rw_w2": nrm((L, RW_DECAY_LORA, D_RWKV), RW_DECAY_LORA ** -0.5),
        "rw_a0": nrm((L, D_RWKV), 0.1),
        "rw_a2": nrm((L, RW_A_LORA, D_RWKV), RW_A_LORA ** -0.5),
        "rw_g2": nrm((L, RW_GATE_LORA, D_RWKV), RW_GATE_LORA ** -0.5),
        "rw_k_k": 0.85 + nrm((L, D_RWKV), 0.02),
        "rw_k_a": 1.0 + nrm((L, D_RWKV), 0.02),
        "rw_r_k": nrm((L, RW_HEADS, HEAD_DIM), 0.1),
        "rw_gn_g": 1.0 + nrm((L, D_RWKV), 0.02),
        "rw_gn_b": nrm((L, D_RWKV), 0.02),
        "pool_w": nrm((L, len(POOL_WINDOWS), POOL_GROUP, POOL_GROUP), POOL_GROUP ** -0.5),
        "pool_b": nrm((L, D_POOL), 0.02),
        "pool_scale": 1.0 + nrm((L, D_POOL), 0.1),
        "nsa_cmp_pe_k": nrm((L, CMP_BLOCK, HEAD_DIM), 0.02),
        "nsa_cmp_pe_v": nrm((L, CMP_BLOCK, HEAD_DIM), 0.02),
        "nsa_cmp_k_w1": nrm((L, CMP_BLOCK, HEAD_DIM, CMP_HIDDEN), (CMP_BLOCK * HEAD_DIM) ** -0.5),
        "nsa_cmp_k_w2": nrm((L, CMP_HIDDEN, HEAD_DIM), CMP_HIDDEN ** -0.5),
        "nsa_cmp_v_w1": nrm((L, CMP_BLOCK, HEAD_DIM, CMP_HIDDEN), (CMP_BLOCK * HEAD_DIM) ** -0.5),
        "nsa_cmp_v_w2": nrm((L, CMP_HIDDEN, HEAD_DIM), CMP_HIDDEN ** -0.5),
        "nsa_gate_b": nrm((L, 3 * NSA_HEADS), 0.1),
        "w_out": nrm((L, D_MIX, D_MODEL), BETA * D_MIX ** -0.5),
        "ln2_g": 1.0 + nrm((L, D_MODEL), 0.02),
        "ln2_b": nrm((L, D_MODEL), 0.02),
        "ffn2_w_up": nrm((L, D_MODEL, 2 * D_FF), D_MODEL ** -0.5),
        "ffn2_w_down": nrm((L, D_FF, D_MODEL), BETA * D_FF ** -0.5),
        "ln3_g": 1.0 + nrm((L, D_MODEL), 0.02),
        "ln3_b": nrm((L, D_MODEL), 0.02),
    }


def reference(x, ffn1_w_up, ffn1_w_down, ln1_g, ln1_b, w_in, rw_mu, rw_w0, rw_w2, rw_a0,
              rw_a2, rw_g2, rw_k_k, rw_k_a, rw_r_k, rw_gn_g, rw_gn_b, pool_w, pool_b,
              pool_scale, nsa_cmp_pe_k, nsa_cmp_pe_v, nsa_cmp_k_w1, nsa_cmp_k_w2,
              nsa_cmp_v_w1, nsa_cmp_v_w2, nsa_gate_b, w_out, ln2_g, ln2_b, ffn2_w_up,
              ffn2_w_down, ln3_g, ln3_b):
    pos = jnp.arange(x.shape[1])
    for l in range(DEPTH):
        x = layer_norm(ALPHA * x + 0.5 * swiglu(x, ffn1_w_up[l], ffn1_w_down[l]), ln1_g[l], ln1_b[l])
        p = x @ w_in[l]
        p_rw, p_pool, p_nsa = jnp.split(p, (RW_COLS, RW_COLS + D_POOL), axis=-1)
        y_rw = rwkv7_mix(p_rw, rw_mu[l], rw_w0[l], rw_w2[l], rw_a0[l], rw_a2[l], rw_g2[l],
                         rw_k_k[l], rw_k_a[l], rw_r_k[l], rw_gn_g[l], rw_gn_b[l])
        y_pool = pool_mix(p_pool, pool_w[l], pool_b[l], pool_scale[l])
        y_nsa = nsa_mix(p_nsa, pos, nsa_cmp_pe_k[l], nsa_cmp_pe_v[l], nsa_cmp_k_w1[l],
                        nsa_cmp_k_w2[l], nsa_cmp_v_w1[l], nsa_cmp_v_w2[l], nsa_gate_b[l])
        y = jnp.concatenate([y_rw, y_pool, y_nsa], axis=-1) @ w_out[l]
        x = layer_norm(ALPHA * x + y, ln2_g[l], ln2_b[l])
        x = layer_norm(ALPHA * x + 0.5 * swiglu(x, ffn2_w_up[l], ffn2_w_down[l]), ln3_g[l], ln3_b[l])
    return x
```

```python
from contextlib import ExitStack
import numpy as np
import ml_dtypes
import concourse.bass as bass
import concourse.mybir as mybir
from concourse.bass_utils import run_bass_kernel_spmd

F32 = mybir.dt.float32
BF16 = mybir.dt.bfloat16
AF = mybir.ActivationFunctionType
ALU = mybir.AluOpType
AX = mybir.AxisListType

D = 2048
SEQ = 4096
NLAYER = 4
DFF = 5504
NF = 43
ALPHA = float((2 * NLAYER) ** 0.25)
LN_EPS = 1e-5
RWC = 2560
NFM = 27
NTM = 1572
CDEC = float(np.exp(-0.5))
GN_EPS = 64e-5
NEG = -30000.0


class Key:
    __slots__ = ("name", "w", "r")

    def __init__(self, name=""):
        self.name = name
        self.w = None
        self.r = []


class Buf:
    def __init__(self, ap, key):
        self.ap = ap
        self.key = key


class Sched:
    EPOCH = 30000
    NDS = 40

    def __init__(self, nc):
        self.nc = nc
        self.engs = {"pe": nc.tensor, "act": nc.scalar, "dve": nc.vector, "pool": nc.gpsimd, "sp": nc.sync}
        self.cnt = {e: 0 for e in self.engs}
        self.sems = {e: [] for e in self.engs}
        self.known = {e: {} for e in self.engs}
        self.dsems = [nc.alloc_semaphore("dq%d" % i) for i in range(self.NDS)]
        self.dcum = [0] * self.NDS
        self.dnext = 0
        self.banks = []
        for i in range(8):
            t = nc.alloc_psum_tensor("psb%d" % i, [128, 512], F32).ap()
            self.banks.append(Buf(t, Key("ps%d" % i)))
        self.bnext = 0
        self.ninst = 0
        self.pooltok = []

    def psum(self):
        b = self.banks[self.bnext]
        self.bnext = (self.bnext + 1) % 8
        return b

    def _esem(self, e, n):
        i = n // self.EPOCH
        while len(self.sems[e]) <= i:
            self.sems[e].append(self.nc.alloc_semaphore("s_%s_%d" % (e, len(self.sems[e]))))
        return self.sems[e][i], n % self.EPOCH + 1

    def _wait(self, e, tok):
        if tok[0] == "E":
            _, src, n = tok
            if self.known[e].get(("E", src), -1) >= n:
                return
            if src == e and e == "pe":
                return
            sem, val = self._esem(src, n)
            self.engs[e].wait_ge(sem, val)
            self.known[e][("E", src)] = n
        else:
            _, s, val = tok
            if self.known[e].get(("D", s), 0) >= val:
                return
            self.engs[e].wait_ge(self.dsems[s], val)
            self.known[e][("D", s)] = val
        self.ninst += 1

    def _deps(self, e, reads, writes, is_dma):
        deps = []
        for k in reads:
            if k.w is not None:
                deps.append(k.w)
        for k in writes:
            if k.w is not None:
                deps.append(k.w)
            for t in k.r:
                if (not is_dma) and t[0] == "E" and t[1] == e:
                    continue
                deps.append(t)
        for t in deps:
            self._wait(e, t)

    def _commit(self, e, tok, reads, writes):
        for k in reads:
            if tok[0] == "E":
                k.r = [t for t in k.r if not (t[0] == "E" and t[1] == tok[1])]
            k.r.append(tok)
        for k in writes:
            k.w = tok
            k.r = []

    def op(self, e, fn, reads=(), writes=()):
        self._deps(e, reads, writes, False)
        ins = fn(self.engs[e])
        n = self.cnt[e]
        self.cnt[e] += 1
        sem, _ = self._esem(e, n)
        ins.then_inc(sem, 1)
        self._commit(e, ("E", e, n), reads, writes)
        self.ninst += 1
        return ins

    def dma(self, q, out, in_, reads=(), writes=()):
        s = self.dnext
        self.dnext = (self.dnext + 1) % self.NDS
        if self.dcum[s] > 0:
            self._wait(q, ("D", s, self.dcum[s]))
        if q == "pool":
            if len(self.pooltok) >= 4:
                self._wait(q, self.pooltok[-4])
        self._deps(q, reads, writes, True)
        ins = self.engs[q].dma_start(out=out, in_=in_)
        self.dcum[s] += 16
        ins.then_inc(self.dsems[s], 16)
        self._commit(q, ("D", s, self.dcum[s]), reads, writes)
        if q == "pool":
            self.pooltok.append(("D", s, self.dcum[s]))
            self.pooltok = self.pooltok[-8:]
        self.ninst += 1
        return ins

    def barrier(self):
        for e in self.engs:
            for src in ("pe", "act", "dve", "pool"):
                if self.cnt[src] > 0:
                    n = self.cnt[src] - 1
                    if src == e and e == "pe":
                        continue
                    if self.known[e].get(("E", src), -1) < n:
                        sem, val = self._esem(src, n)
                        self.engs[e].wait_ge(sem, val)
                        self.known[e][("E", src)] = n
            for s in range(self.NDS):
                if self.dcum[s] > 0:
                    self._wait(e, ("D", s, self.dcum[s]))


class Pool:
    def __init__(self, nc, name, shape, dtype, n):
        self.bufs = [sb(nc, "%s%d" % (name, i), shape, dtype) for i in range(n)]
        self.i = 0

    def next(self):
        b = self.bufs[self.i]
        self.i = (self.i + 1) % len(self.bufs)
        return b


_STACK = [None]


_CNT = [0]


def sb(nc, name, shape, dtype):
    _CNT[0] += 1
    name = "%s_u%d" % (name, _CNT[0])
    h = _STACK[0].enter_context(nc.sbuf_tensor(name, list(shape), dtype))
    return Buf(h.ap(), Key(name))


class Prog:
    def __init__(self, nlayer=NLAYER, trun=SEQ, debug=False, stop_after=None):
        self.nlayer = nlayer
        self.T = trun
        self.NTB = trun // 512
        self.debug = debug
        self.stop_after = stop_after
        nc = bass.Bass("TRN2", target_bir_lowering=False)
        self.nc = nc
        self.S = Sched(nc)
        self.din = {}
        self._declare_io()
        self.build()

    def inp(self, name, shape, dtype=F32):
        t = nc_t = self.nc.dram_tensor(name, list(shape), dtype, kind="ExternalInput").ap()
        self.din[name] = (tuple(shape), dtype)
        return Buf(t, Key(name))

    def scratch(self, name, shape, dtype=F32):
        kind = "ExternalOutput" if self.debug else "Internal"
        t = self.nc.dram_tensor(name, list(shape), dtype, kind=kind).ap()
        return Buf(t, Key(name))

    def _declare_io(self):
        L = self.nlayer
        T = self.T
        self.x_in = self.inp("x", [SEQ, D])
        self.out = Buf(self.nc.dram_tensor("out", [SEQ, D], F32, kind="ExternalOutput").ap(), Key("out"))
        self.wup = [self.inp("wup%d" % i, [L, NF, 128, 4096]) for i in (1, 2)]
        self.wdn = [self.inp("wdn%d" % i, [L, 4, 11, 128, 2048]) for i in (1, 2)]
        self.lng = [self.inp("ln%dg" % i, [L, D]) for i in (1, 2, 3)]
        self.lnb = [self.inp("ln%db" % i, [L, D]) for i in (1, 2, 3)]
        self.winfm = self.inp("winfm", [L, NFM, 128, 2048])
        self.wintm = self.inp("wintm", [L, 4, 4, 128, 2048])
        self.wout = self.inp("wout", [L, 4, 4, 128, 2048])
        self.mu = self.inp("rwmu", [L, 128, 20])
        self.gateb = self.inp("gateb", [L, 36])
        self.rw_w2 = self.inp("rw_w2", [L, 64, 768])
        self.rw_a2 = self.inp("rw_a2", [L, 64, 768])
        self.rw_g2 = self.inp("rw_g2", [L, 128, 768])
        self.rwvec = self.inp("rwvec", [L, 64, 7, 12])
        self.poolw = self.inp("poolw", [L, 4, 128, 128])
        self.poolv = self.inp("poolv", [L, 128, 8])
        self.cw1 = self.inp("cw1", [L, 2, 64, 32 * 256])
        self.cw2 = self.inp("cw2", [L, 2, 128, 2, 64])
        self.cpe = self.inp("cpe", [L, 2, 64, 32])
        self.c_ropec = self.inp("c_ropec", [256, 16])
        self.c_ovl = self.inp("c_ovl", [2, 128, 65])
        self.c_expand = self.inp("c_expand", [64, 32, 128])
        self.c_caus = self.inp("c_caus", [2, 128, 128])
        self.c_force = self.inp("c_force", [32, 128, 64])
        self.c_valid = self.inp("c_valid", [32, 128, 64])
        self.c_cmask = self.inp("c_cmask", [32, 2, 128, 128])
        self.c_masks = self.inp("c_masks", [3, 128, 128])
        self.c_rcnt = self.inp("c_rcnt", [4, SEQ])
        self.c_ident = self.inp("c_ident", [128, 128])
        self.c_rope = self.inp("c_rope", [SEQ, 16])
        self.X1 = self.scratch("X1", [SEQ, D])
        self.XN = self.scratch("XN", [SEQ, D])
        self.FMT = self.scratch("FMT", [NFM * 128, SEQ])
        self.QKT = self.scratch("QKT", [1152, SEQ], BF16)
        self.VSW = self.scratch("VSW", [SEQ, 384], BF16)
        self.GL = self.scratch("GL", [SEQ, 36])
        self.YT = self.scratch("YT", [D, SEQ], BF16)

        def wscr(name, shape):
            t = self.nc.dram_tensor(name, list(shape), BF16, kind="Internal").ap()
            return t, [Key("%s_%d" % (name, i)) for i in range(shape[0])]
        self.wupB = [wscr("wupB%d" % i, [NF, 128, 4096]) for i in (1, 2)]
        self.wdnB = [wscr("wdnB%d" % i, [44, 128, 2048]) for i in (1, 2)]
        self.winfmB = wscr("winfmB", [NFM, 128, 2048])
        self.wintmB = wscr("wintmB", [16, 128, 2048])
        self.woutB = wscr("woutB", [16, 128, 2048])

    def _consts(self):
        nc, S = self.nc, self.S
        self.ident = sb(nc, "ident", [128, 128], F32)
        S.dma("sp", self.ident.ap, self.c_ident.ap, reads=[self.c_ident.key], writes=[self.ident.key])
        self.epsc = sb(nc, "epsc", [128, 1], F32)
        S.op("dve", lambda e: e.memset(self.epsc.ap, LN_EPS), writes=[self.epsc.key])

    def mm(self, ps, lhsT, rhs, start, stop, reads):
        self.S.op("pe", lambda e: e.matmul(ps.ap if isinstance(ps, Buf) else ps, lhsT=lhsT, rhs=rhs, start=start, stop=stop),
                  reads=reads, writes=[ps.key] if isinstance(ps, Buf) else [])

    def make_xT(self, xtok, xT, tog):
        S = self.S
        for dc in range(16):
            ps = S.psum()
            for tt in range(4):
                S.op("pe", lambda e, tt=tt: e.transpose(out=ps.ap[:, tt * 128:(tt + 1) * 128],
                                                         in_=xtok.ap[:, tt, dc * 128:(dc + 1) * 128],
                                                         identity=self.ident.ap),
                     reads=[self.kx[tt], self.ident.key], writes=[ps.key])
            if (dc + tog) % 2 == 0:
                S.op("act", lambda e: e.activation(out=xT.ap[:, dc, :], in_=ps.ap, func=AF.Copy),
                     reads=[ps.key], writes=[xT.key])
            else:
                S.op("dve", lambda e: e.tensor_copy(out=xT.ap[:, dc, :], in_=ps.ap), reads=[ps.key], writes=[xT.key])

    def tokmm(self, lhsT, nk, wsrc, ncb, G, evac, ncols=None):
        S = self.S
        ng = (nk + G - 1) // G
        for cb in range(ncb):
            nco = 512 if ncols is None else ncols[cb]
            banks = [S.psum() for _ in range(4)]
            for g in range(ng):
                w = self.wpool.next()
                wap, wkeys = wsrc
                S.dma(self.wq(), w.ap[:, 0:G * 512], wap[cb * ng + g], reads=[wkeys[cb * ng + g]], writes=[w.key])
                for tt in range(4):
                    for fi in range(G):
                        f = g * G + fi
                        if f >= nk:
                            continue
                        self.S.op("pe", lambda e, tt=tt, f=f, fi=fi: e.matmul(
                            banks[tt].ap[:, 0:nco], lhsT=lhsT.ap[:, f, tt * 128:(tt + 1) * 128],
                            rhs=w.ap[:, fi * 512:fi * 512 + nco], start=(f == 0), stop=(f == nk - 1)),
                            reads=[lhsT.key, w.key], writes=[banks[tt].key])
            for tt in range(4):
                evac(tt, cb, banks[tt], nco)

    def convert(self, l, which):
        S = self.S

        def cv(dstp, src_ap, src_key):
            dst, keys = dstp
            for i in range(len(keys)):
                S.dma("pool", dst[i], src_ap[i], reads=[src_key], writes=[keys[i]])
        if which == 0:
            cv(self.wupB[0], self.wup[0].ap[l], self.wup[0].key)
            cv(self.wdnB[0], self.wdn[0].ap[l].rearrange("a b p c -> (a b) p c"), self.wdn[0].key)
            cv(self.winfmB, self.winfm.ap[l], self.winfm.key)
            cv(self.wintmB, self.wintm.ap[l].rearrange("a b p c -> (a b) p c"), self.wintm.key)
        else:
            cv(self.woutB, self.wout.ap[l].rearrange("a b p c -> (a b) p c"), self.wout.key)
            cv(self.wupB[1], self.wup[1].ap[l], self.wup[1].key)
            cv(self.wdnB[1], self.wdn[1].ap[l].rearrange("a b p c -> (a b) p c"), self.wdn[1].key)

    def wq(self):
        self._wq = getattr(self, "_wq", 0) + 1
        return "pool"

    def layer_consts(self, l, first=True):
        nc, S = self.nc, self.S
        for i in ((0,) if first else (1, 2)):
            S.dma("sp", self.lnG[i].ap, self.lng[i].ap[l].partition_broadcast(128), reads=[self.lng[i].key],
                  writes=[self.lnG[i].key])
            S.dma("sp", self.lnB[i].ap, self.lnb[i].ap[l].partition_broadcast(128), reads=[self.lnb[i].key],
                  writes=[self.lnB[i].key])
        if not first:
            return
        S.dma("sp", self.MU.ap, self.mu.ap[l], reads=[self.mu.key], writes=[self.MU.key])
        S.dma("sp", self.GATEB.ap, self.gateb.ap[l].partition_broadcast(128), reads=[self.gateb.key],
              writes=[self.GATEB.key])

    def ffn_ln(self, l, which, xtok, xT, hT):
        S = self.S
        wupB, wdnB = self.wupB[which], self.wdnB[which]
        lni = 0 if which == 0 else 2
        for tt in range(4):
            S.op("pool", lambda e, tt=tt: e.tensor_scalar_mul(out=xtok.ap[:, tt, :], in0=xtok.ap[:, tt, :], scalar1=ALPHA),
                 reads=[self.kx[tt]], writes=[self.kx[tt]])
        for f in range(NF):
            w = self.wpool.next()
            S.dma(self.wq(), w.ap, wupB[0][f], reads=[wupB[1][f]], writes=[w.key])
            pa, pb = S.psum(), S.psum()
            for kc in range(16):
                self.mm(pa, w.ap[:, kc * 128:(kc + 1) * 128], xT.ap[:, kc, :], kc == 0, kc == 15, [w.key, xT.key])
            for kc in range(16):
                self.mm(pb, w.ap[:, (16 + kc) * 128:(17 + kc) * 128], xT.ap[:, kc, :], kc == 0, kc == 15,
                        [w.key, xT.key])
            sa = self.sapool.next()
            S.op("act", lambda e: e.activation(out=sa.ap, in_=pa.ap, func=AF.Silu), reads=[pa.key], writes=[sa.key])
            S.op("dve", lambda e: e.tensor_tensor(out=hT.ap[:, f, :], in0=pb.ap, in1=sa.ap, op=ALU.mult),
                 reads=[pb.key, sa.key], writes=[hT.key])

        def evac(tt, cb, ps, nco):
            S.op("dve", lambda e: e.scalar_tensor_tensor(out=xtok.ap[:, tt, cb * 512:(cb + 1) * 512], in0=ps.ap,
                                                         scalar=0.5, in1=xtok.ap[:, tt, cb * 512:(cb + 1) * 512],
                                                         op0=ALU.mult, op1=ALU.add),
                 reads=[ps.key, self.kx[tt]], writes=[self.kx[tt]])
        self.tokmm(hT, NF, wdnB, 4, 4, evac)
        for tt in range(4):
            self.layernorm(xtok, tt, lni)

    def layernorm(self, xtok, tt, lni):
        S = self.S
        st, mv, rs = self.lnst, self.lnmv, self.lnrs
        for j in range(4):
            S.op("dve", lambda e, j=j: e.bn_stats(out=st.ap[:, j, :], in_=xtok.ap[:, tt, j * 512:(j + 1) * 512]),
                 reads=[self.kx[tt]], writes=[st.key])
        S.op("dve", lambda e: e.bn_aggr(out=mv.ap, in_=st.ap.rearrange("p a b -> p (a b)")), reads=[st.key], writes=[mv.key])
        S.op("act", lambda e: e.activation(out=rs.ap, in_=mv.ap[:, 1:2], func=AF.Sqrt, bias=self.epsc.ap, scale=1.0),
             reads=[mv.key, self.epsc.key], writes=[rs.key])
        S.op("dve", lambda e: e.reciprocal(out=rs.ap, in_=rs.ap), reads=[rs.key], writes=[rs.key])
        S.op("dve", lambda e: e.tensor_scalar(out=xtok.ap[:, tt, :], in0=xtok.ap[:, tt, :], scalar1=mv.ap[:, 0:1],
                                              scalar2=rs.ap[:, 0:1], op0=ALU.subtract, op1=ALU.mult),
             reads=[self.kx[tt], mv.key, rs.key], writes=[self.kx[tt]])
        S.op("pool", lambda e: e.tensor_tensor(out=xtok.ap[:, tt, :], in0=xtok.ap[:, tt, :], in1=self.lnG[lni].ap, op=ALU.mult),
             reads=[self.kx[tt], self.lnG[lni].key], writes=[self.kx[tt]])
        S.op("pool", lambda e: e.tensor_tensor(out=xtok.ap[:, tt, :], in0=xtok.ap[:, tt, :], in1=self.lnB[lni].ap, op=ALU.add),
             reads=[self.kx[tt], self.lnB[lni].key], writes=[self.kx[tt]])

    def alloc_loop(self):
        nc = self.nc
        self.xtok = sb(nc, "xtok", [128, 4, D], F32)
        self.kx = [Key("xtok%d" % i) for i in range(4)]
        self.xT = sb(nc, "xT", [128, 16, 512], BF16)
        self.hT = sb(nc, "hT", [128, NF, 512], BF16)
        self.wpool = Pool(nc, "wp", [128, 4096], BF16, 6)
        self.sapool = Pool(nc, "sa", [128, 512], F32, 2)
        self.lnst = sb(nc, "lnst", [128, 4, 6], F32)
        self.lnmv = sb(nc, "lnmv", [128, 2], F32)
        self.lnrs = sb(nc, "lnrs", [128, 1], F32)
        self.lnG = [sb(nc, "lnG%d" % i, [128, D], F32) for i in range(2)]
        self.lnB = [sb(nc, "lnB%d" % i, [128, D], F32) for i in range(2)]
        self.lnG.append(self.lnG[0])
        self.lnB.append(self.lnB[0])
        self.MU = sb(nc, "MU", [128, 20], F32)
        self.GATEB = sb(nc, "GATEB", [128, 36], F32)
        self.halo = sb(nc, "halo", [128, 20], F32)
        self.stage = Pool(nc, "stg", [128, 513], F32, 2)
        self.ost = Pool(nc, "ost", [128, 512], F32, 3)
        self.dtmp = sb(nc, "dtmp", [128, 512], F32)
        tmv = self.hT.ap.rearrange("p a b -> p (a b)").bitcast(F32)[:, 0:4 * NTM].rearrange("p (a c) -> p a c", a=4)
        self.TM = Buf(tmv, self.hT.key)
        self.CS = sb(nc, "CS", [128, 4, 16], F32)
        self.rtmp = sb(nc, "rtmp", [128, 4, 18, 8], F32)
        self.bst = Pool(nc, "bst", [128, 512], BF16, 3)
        self.vst = sb(nc, "vst", [128, 4, 384], BF16)
        self.gst = sb(nc, "gst", [128, 4, 36], F32)

    def loop1(self, l, src):
        S = self.S
        xtok, xT, hT = self.xtok, self.xT, self.hT
        S.op("dve", lambda e: e.memset(self.halo.ap, 0.0), writes=[self.halo.key])
        for tb in range(self.NTB):
            t0 = tb * 512
            for tt in range(4):
                S.dma("sp", xtok.ap[:, tt, :], src.ap[t0 + tt * 128:t0 + (tt + 1) * 128, :], reads=[src.key],
                      writes=[self.kx[tt]])
            S.dma("sp", self.CS.ap, self.c_rope.ap[t0:t0 + 512, :].rearrange("(a p) c -> p a c", p=128),
                  reads=[self.c_rope.key], writes=[self.CS.key])
            self.make_xT(xtok, xT, 0)
            self.ffn_ln(l, 0, xtok, xT, hT)
            for tt in range(4):
                S.dma("sp", self.X1.ap[t0 + tt * 128:t0 + (tt + 1) * 128, :], xtok.ap[:, tt, :], reads=[self.kx[tt]],
                      writes=[self.X1.key])
            self.make_xT(xtok, xT, 1)
            self.win_proj(l, t0, xT)

    def win_proj(self, l, t0, xT):
        S = self.S
        for ch in range(NFM):
            w = self.wpool.next()
            S.dma(self.wq(), w.ap[:, 0:2048], self.winfmB[0][ch], reads=[self.winfmB[1][ch]], writes=[w.key])
            ps = S.psum()
            for kc in range(16):
                self.mm(ps, w.ap[:, kc * 128:(kc + 1) * 128], xT.ap[:, kc, :], kc == 0, kc == 15, [w.key, xT.key])
            o = self.ost.next()
            if ch < 20:
                st = self.stage.next()
                d = self.dtmp
                S.op("act", lambda e: e.activation(out=st.ap[:, 1:513], in_=ps.ap, func=AF.Copy), reads=[ps.key],
                     writes=[st.key])
                S.op("dve", lambda e: e.tensor_copy(out=st.ap[:, 0:1], in_=self.halo.ap[:, ch:ch + 1]),
                     reads=[self.halo.key], writes=[st.key])
                S.op("dve", lambda e: e.tensor_sub(out=d.ap, in0=st.ap[:, 0:512], in1=st.ap[:, 1:513]), reads=[st.key],
                     writes=[d.key])
                S.op("dve", lambda e: e.tensor_copy(out=self.halo.ap[:, ch:ch + 1], in_=st.ap[:, 512:513]),
                     reads=[st.key], writes=[self.halo.key])
                S.op("dve", lambda e: e.scalar_tensor_tensor(out=o.ap, in0=d.ap, scalar=self.MU.ap[:, ch:ch + 1],
                                                             in1=st.ap[:, 1:513], op0=ALU.mult, op1=ALU.add),
                     reads=[d.key, st.key, self.MU.key], writes=[o.key])
            else:
                S.op("act", lambda e: e.activation(out=o.ap, in_=ps.ap, func=AF.Copy), reads=[ps.key], writes=[o.key])
            S.dma("sp", self.FMT.ap[ch * 128:(ch + 1) * 128, t0:t0 + 512], o.ap, reads=[o.key], writes=[self.FMT.key])
        TM = self.TM

        def evac(tt, cb, ps, nco):
            if (tt + cb) % 2 == 0:
                S.op("act", lambda e: e.activation(out=TM.ap[:, tt, cb * 512:cb * 512 + nco], in_=ps.ap[:, 0:nco], func=AF.Copy),
                     reads=[ps.key], writes=[TM.key])
            else:
                S.op("dve", lambda e: e.tensor_copy(out=TM.ap[:, tt, cb * 512:cb * 512 + nco], in_=ps.ap[:, 0:nco]),
                     reads=[ps.key], writes=[TM.key])
        self.tokmm(xT, 16, self.wintmB, 4, 4, evac, ncols=[512, 512, 512, 36])
        rt = self.rtmp
        for tt in range(4):
            v = TM.ap[:, tt, 0:1152].rearrange("p (h c) -> p h c", c=64)
            x1, x2 = v[:, :, 0:8], v[:, :, 8:16]
            cosb = self.CS.ap[:, tt, 0:8].unsqueeze(1).broadcast_to([128, 18, 8])
            sinb = self.CS.ap[:, tt, 8:16].unsqueeze(1).broadcast_to([128, 18, 8])
            rk = [TM.key, self.CS.key]
            S.op("dve", lambda e: e.tensor_tensor(out=rt.ap[:, 0], in0=x1, in1=cosb, op=ALU.mult), reads=rk, writes=[rt.key])
            S.op("dve", lambda e: e.tensor_tensor(out=rt.ap[:, 1], in0=x2, in1=sinb, op=ALU.mult), reads=rk, writes=[rt.key])
            S.op("dve", lambda e: e.tensor_tensor(out=rt.ap[:, 2], in0=x2, in1=cosb, op=ALU.mult), reads=rk, writes=[rt.key])
            S.op("dve", lambda e: e.tensor_tensor(out=rt.ap[:, 3], in0=x1, in1=sinb, op=ALU.mult), reads=rk, writes=[rt.key])
            S.op("dve", lambda e: e.tensor_tensor(out=x1, in0=rt.ap[:, 0], in1=rt.ap[:, 1], op=ALU.subtract),
                 reads=[rt.key], writes=[TM.key])
            S.op("dve", lambda e: e.tensor_tensor(out=x2, in0=rt.ap[:, 2], in1=rt.ap[:, 3], op=ALU.add),
                 reads=[rt.key], writes=[TM.key])
        for ch in range(9):
            ps = S.psum()
            for tt in range(4):
                S.op("pe", lambda e, tt=tt: e.transpose(out=ps.ap[:, tt * 128:(tt + 1) * 128],
                                                         in_=TM.ap[:, tt, ch * 128:(ch + 1) * 128], identity=self.ident.ap),
                     reads=[TM.key, self.ident.key], writes=[ps.key])
            o = self.bst.next()
            S.op("act", lambda e: e.mul(o.ap, ps.ap, (0.125 if ch < 6 else 1.0)), reads=[ps.key], writes=[o.key])
            S.dma("sp", self.QKT.ap[ch * 128:(ch + 1) * 128, t0:t0 + 512], o.ap, reads=[o.key], writes=[self.QKT.key])
        for tt in range(4):
            S.op("act", lambda e, tt=tt: e.activation(out=self.vst.ap[:, tt, :], in_=TM.ap[:, tt, 1152:1536], func=AF.Copy),
                 reads=[TM.key], writes=[self.vst.key])
            S.op("dve", lambda e, tt=tt: e.tensor_tensor(out=self.gst.ap[:, tt, :], in0=TM.ap[:, tt, 1536:1572],
                                                         in1=self.GATEB.ap, op=ALU.add),
                 reads=[TM.key, self.GATEB.key], writes=[self.gst.key])
        S.op("act", lambda e: e.activation(out=self.gst.ap, in_=self.gst.ap, func=AF.Sigmoid), reads=[self.gst.key],
             writes=[self.gst.key])
        S.dma("sp", self.VSW.ap[t0:t0 + 512, :].rearrange("(a p) c -> p a c", p=128), self.vst.ap, reads=[self.vst.key],
              writes=[self.VSW.key])
        S.dma("sp", self.GL.ap[t0:t0 + 512, :].rearrange("(a p) c -> p a c", p=128), self.gst.ap, reads=[self.gst.key],
              writes=[self.GL.key])

    def build(self):
        S = self.S
        self.gstack = ExitStack()
        _STACK[0] = self.gstack
        self._consts()
        src = self.x_in
        sa = self.stop_after
        self.convert(0, 0)
        for l in range(self.nlayer):
            last = (l == self.nlayer - 1)
            with ExitStack() as st:
                _STACK[0] = st
                self.alloc_loop()
                self.layer_consts(l, True)
                self.loop1(l, src)
                S.barrier()
            if sa == "loop1":
                break
            with ExitStack() as st:
                _STACK[0] = st
                self.convert(l, 1)
                if not last:
                    self.convert(l + 1, 0)
                self.rwkv(l)
                S.barrier()
            if sa == "rwkv":
                break
            with ExitStack() as st:
                _STACK[0] = st
                self.poolmix(l)
                S.barrier()
            if sa == "pool":
                break
            with ExitStack() as st:
                _STACK[0] = st
                self.nsa(l)
                S.barrier()
            if sa == "nsa":
                break
            with ExitStack() as st:
                _STACK[0] = st
                self.alloc_loop()
                self.layer_consts(l, False)
                self.loop3(l, self.out if last else self.XN)
                S.barrier()
            src = self.XN
        S.barrier()


    def rwkv(self, l):
        nc, S = self.nc, self.S
        T = self.T
        NCH = T // 128
        ident = self.ident

        def W(name, shape, dt=F32):
            return sb(nc, "rk%d_" % l + name, shape, dt)
        w2 = W("w2", [64, 768]); a2 = W("a2", [64, 768]); g2 = W("g2", [128, 768])
        vec = W("vec", [64, 7, 12])
        omka = W("omka", [64, 12])
        ones = W("ones", [64, 64])
        msk = W("msk", [128, 3, 128])
        rmask = W("rmask", [64, 4, 128])
        gneps = W("gneps", [128, 1])
        S.dma("sp", w2.ap, self.rw_w2.ap[l], reads=[self.rw_w2.key], writes=[w2.key])
        S.dma("sp", a2.ap, self.rw_a2.ap[l], reads=[self.rw_a2.key], writes=[a2.key])
        S.dma("sp", g2.ap, self.rw_g2.ap[l], reads=[self.rw_g2.key], writes=[g2.key])
        S.dma("sp", vec.ap, self.rwvec.ap[l], reads=[self.rwvec.key], writes=[vec.key])
        S.dma("sp", msk.ap, self.c_masks.ap.rearrange("m p c -> p m c"), reads=[self.c_masks.key], writes=[msk.key])
        S.op("dve", lambda e: e.memset(ones.ap, 1.0), writes=[ones.key])
        S.op("dve", lambda e: e.memset(gneps.ap, GN_EPS), writes=[gneps.key])
        S.op("dve", lambda e: e.memset(rmask.ap, 1.0), writes=[rmask.key])
        S.op("dve", lambda e: e.memset(rmask.ap[:, :, 0:1], 0.0), writes=[rmask.key])
        S.op("dve", lambda e: e.tensor_scalar(out=omka.ap, in0=vec.ap[:, 3, :], scalar1=-1.0, scalar2=1.0, op0=ALU.mult,
                                              op1=ALU.add), reads=[vec.key], writes=[omka.key])
        R = W("R", [64, 4, 128]); K = W("K", [64, 4, 128]); V = W("V", [64, 4, 128])
        SG = W("SG", [64, 4, 128]); A = W("A", [64, 4, 128]); GT = W("GT", [64, 4, 128])
        KM = W("KM", [64, 4, 128]); CUM = W("CUM", [64, 4, 128]); E1 = W("E1", [64, 4, 128])
        E2 = W("E2", [64, 4, 128]); BT = W("BT", [64, 4, 128]); BON = W("BON", [64, 4, 128])
        t1 = W("t1", [64, 4, 128]); t2 = W("t2", [64, 4, 128])
        WL = W("WL", [64, 128]); AL = W("AL", [64, 128]); GLr = W("GLr", [128, 128])
        tw = W("tw", [64, 128]); sg = W("sg", [128, 128])
        VT = W("VT", [128, 4, 64]); AHT = W("AHT", [128, 4, 64]); BPT = W("BPT", [128, 4, 64]); KPT = W("KPT", [128, 4, 64])
        AtT = W("AtT", [128, 4, 64]); WT = W("WT", [128, 4, 64]); Yk = W("Yk", [128, 4, 64]); Yd = W("Yd", [128, 4, 64])
        Ysq = W("Ysq", [128, 4, 64])
        Nm = W("Nm", [128, 4, 128]); NTm = W("NTm", [128, 4, 128]); MTm = W("MTm", [128, 4, 128])
        Pm = W("Pm", [128, 4, 128]); Qm = W("Qm", [128, 4, 128]); Pb = W("Pb", [128, 4, 128]); PbT = W("PbT", [128, 4, 128])
        Ta = W("Ta", [128, 4, 128]); Tb = W("Tb", [128, 4, 128]); X1 = W("X1", [128, 4, 128]); Zm = W("Zm", [128, 4, 128])
        Gs = W("Gs", [64, 4, 64]); DG = W("DG", [64, 4, 64]); Rt = W("Rt", [64, 4, 128])
        STs = [[W("ST%d_%d" % (g, i), [64, 4, 64]) for i in range(2)] for g in range(3)]
        sm = W("sm", [128, 4]); sv = W("sv", [128, 4])
        OB = W("OB", [64, 4, 128], BF16)
        OF = W("OF", [64, 4, 128])

        def bc_h(v, n):
            return v.unsqueeze(2).broadcast_to([64, 4, n])

        def dve(fn, reads, writes):
            S.op("dve", fn, reads=[b.key for b in reads], writes=[b.key for b in writes])

        def act(fn, reads, writes):
            S.op("act", fn, reads=[b.key for b in reads], writes=[b.key for b in writes])

        def tt_(out, in0, in1, op, reads, writes):
            dve(lambda e: e.tensor_tensor(out=out, in0=in0, in1=in1, op=op), reads, writes)

        def pemm(ps, sl, lhsT, rhs, reads, start=True, stop=True):
            S.op("pe", lambda e: e.matmul(sl, lhsT=lhsT, rhs=rhs, start=start, stop=stop),
                 reads=[b.key for b in reads], writes=[ps.key])

        fl = lambda b: b.ap.rearrange("p a b -> p (a b)")
        mU, mUi, mL = msk.ap[:, 0, :], msk.ap[:, 1, :], msk.ap[:, 2, :]
        b4 = lambda m: m.unsqueeze(1).broadcast_to([128, 4, 128])
        idb = ident.ap.unsqueeze(1).broadcast_to([128, 4, 128])
        id64b = ident.ap[0:64, 0:64].unsqueeze(1).broadcast_to([64, 4, 64])

        for hg in range(3):
            h0 = hg * 4
            vh = lambda i: vec.ap[:, i, h0:h0 + 4]
            S.op("dve", lambda e: e.memset(STs[hg][0].ap, 0.0), writes=[STs[hg][0].key])
            for c in range(NCH):
                t0 = c * 128
                STc, STn = STs[hg][c % 2], STs[hg][(c + 1) % 2]
                fm = self.FMT
                for (buf, base) in ((R, 0), (K, 768), (V, 1536)):
                    S.dma("sp", buf.ap, fm.ap[base + h0 * 64:base + (h0 + 4) * 64, t0:t0 + 128].rearrange("(h c) t -> c h t", c=64),
                          reads=[fm.key], writes=[buf.key])
                S.dma("sp", WL.ap, fm.ap[2304:2368, t0:t0 + 128], reads=[fm.key], writes=[WL.key])
                S.dma("sp", AL.ap, fm.ap[2368:2432, t0:t0 + 128], reads=[fm.key], writes=[AL.key])
                S.dma("sp", GLr.ap, fm.ap[2432:2560, t0:t0 + 128], reads=[fm.key], writes=[GLr.key])
                act(lambda e: e.activation(out=tw.ap, in_=WL.ap, func=AF.Tanh), [WL], [tw])
                act(lambda e: e.activation(out=sg.ap, in_=GLr.ap, func=AF.Sigmoid), [GLr], [sg])
                p1, p2, p3 = S.psum(), S.psum(), S.psum()
                for j in range(4):
                    h = h0 + j
                    pemm(p1, p1.ap[0:64, j * 128:(j + 1) * 128], w2.ap[:, h * 64:(h + 1) * 64], tw.ap, [w2, tw])
                    pemm(p2, p2.ap[0:64, j * 128:(j + 1) * 128], a2.ap[:, h * 64:(h + 1) * 64], AL.ap, [a2, AL])
                    pemm(p3, p3.ap[0:64, j * 128:(j + 1) * 128], g2.ap[:, h * 64:(h + 1) * 64], sg.ap, [g2, sg])
                v3 = lambda p: p.ap[0:64, :].rearrange("p (a b) -> p a b", a=4)
                tt_(t1.ap, v3(p1), bc_h(vh(0), 128), ALU.add, [p1, vec], [t1])
                act(lambda e: e.activation(out=SG.ap, in_=t1.ap, func=AF.Sigmoid), [t1], [SG])
                tt_(t2.ap, v3(p2), bc_h(vh(1), 128), ALU.add, [p2, vec], [t2])
                act(lambda e: e.activation(out=A.ap, in_=t2.ap, func=AF.Sigmoid), [t2], [A])
                act(lambda e: e.activation(out=GT.ap, in_=v3(p3), func=AF.Copy), [p3], [GT])
                tt_(t1.ap, A.ap, bc_h(vh(3), 128), ALU.mult, [A, vec], [t1])
                tt_(t1.ap, t1.ap, bc_h(omka.ap[:, h0:h0 + 4], 128), ALU.add, [t1, omka], [t1])
                tt_(KM.ap, K.ap, t1.ap, ALU.mult, [K, t1], [KM])
                tt_(K.ap, K.ap, bc_h(vh(2), 128), ALU.mult, [K, vec], [K])
                tt_(t1.ap, K.ap, K.ap, ALU.mult, [K], [t1])
                pn = S.psum()
                pemm(pn, pn.ap[0:64, :], ones.ap, fl(t1), [ones, t1])
                act(lambda e: e.activation(out=fl(t1), in_=pn.ap[0:64, :], func=AF.Sqrt), [pn], [t1])
                dve(lambda e: e.tensor_scalar(out=t1.ap, in0=t1.ap, scalar1=1e-12, scalar2=None, op0=ALU.max), [t1], [t1])
                dve(lambda e: e.reciprocal(out=t1.ap, in_=t1.ap), [t1], [t1])
                tt_(K.ap, K.ap, t1.ap, ALU.mult, [K, t1], [K])
                tt_(A.ap, K.ap, A.ap, ALU.mult, [K, A], [A])
                tt_(t2.ap, R.ap, KM.ap, ALU.mult, [R, KM], [t2])
                tt_(t2.ap, t2.ap, bc_h(vh(4), 128), ALU.mult, [t2, vec], [t2])
                pb_ = S.psum()
                pemm(pb_, pb_.ap[0:64, :], ones.ap, fl(t2), [ones, t2])
                tt_(BON.ap, v3(pb_), V.ap, ALU.mult, [pb_, V], [BON])
                dve(lambda e: e.tensor_tensor_scan(out=fl(CUM), data0=fl(rmask), data1=fl(SG), initial=0.0, op0=ALU.mult,
                                                   op1=ALU.add), [rmask, SG], [CUM])
                act(lambda e: e.activation(out=E1.ap, in_=CUM.ap, func=AF.Exp, scale=-CDEC), [CUM], [E1])
                act(lambda e: e.activation(out=E2.ap, in_=CUM.ap, func=AF.Exp, scale=CDEC), [CUM], [E2])
                tt_(SG.ap, CUM.ap, SG.ap, ALU.subtract, [CUM, SG], [SG])
                act(lambda e: e.activation(out=SG.ap, in_=SG.ap, func=AF.Exp, scale=-CDEC), [SG], [SG])
                tt_(CUM.ap, CUM.ap[:, :, 127:128].broadcast_to([64, 4, 128]), CUM.ap, ALU.subtract, [CUM], [CUM])
                act(lambda e: e.activation(out=CUM.ap, in_=CUM.ap, func=AF.Exp, scale=-CDEC), [CUM], [CUM])
                dve(lambda e: e.scalar_tensor_tensor(out=SG.ap, in0=K.ap, scalar=-1.0, in1=SG.ap, op0=ALU.mult, op1=ALU.mult),
                    [K, SG], [SG])
                tt_(R.ap, R.ap, E1.ap, ALU.mult, [R, E1], [R])
                tt_(BT.ap, A.ap, E2.ap, ALU.mult, [A, E2], [BT])
                tt_(E2.ap, KM.ap, E2.ap, ALU.mult, [KM, E2], [E2])
                tt_(A.ap, A.ap, CUM.ap, ALU.mult, [A, CUM], [A])
                tt_(KM.ap, KM.ap, CUM.ap, ALU.mult, [KM, CUM], [KM])
                AH, RH, KT, BP, KP = SG, R, E2, A, KM
                for (src_, dst_) in ((V, VT), (AH, AHT), (BP, BPT), (KP, KPT)):
                    ps = S.psum()
                    for j in range(4):
                        S.op("pe", lambda e, j=j: e.transpose(out=ps.ap[:, j * 64:(j + 1) * 64], in_=src_.ap[:, j, :],
                                                               identity=ident.ap[0:64, 0:64]),
                             reads=[src_.key, ident.key], writes=[ps.key])
                    act(lambda e: e.activation(out=fl(dst_), in_=ps.ap[:, 0:256], func=AF.Copy), [ps], [dst_])
                def stage(dst_, lh, rh, mask):
                    ps = S.psum()
                    for j in range(4):
                        pemm(ps, ps.ap[:, j * 128:(j + 1) * 128], lh.ap[:, j, :], rh.ap[:, j, :], [lh, rh])
                    tt_(dst_.ap, ps.ap.rearrange("p (a b) -> p a b", a=4), b4(mask), ALU.mult, [ps, msk], [dst_])
                stage(Nm, BT, AH, mU)
                stage(NTm, AH, BT, mL)
                stage(MTm, AH, KT, mL)
                stage(Pm, BT, RH, mUi)
                stage(Qm, KT, RH, mUi)
                tt_(Ta.ap, Nm.ap, idb, ALU.add, [Nm, ident], [Ta])
                Pp, PpT, Pn, PnT = Nm, NTm, Pb, PbT
                Tp, Tn = Ta, Tb
                for k in range(1, 7):
                    if k < 6:
                        ps = S.psum()
                        for j in range(4):
                            pemm(ps, ps.ap[:, j * 128:(j + 1) * 128], PpT.ap[:, j, :], Pp.ap[:, j, :], [PpT, Pp])
                        act(lambda e: e.activation(out=fl(Pn), in_=ps.ap, func=AF.Copy), [ps], [Pn])
                    ps2 = S.psum()
                    for j in range(4):
                        pemm(ps2, ps2.ap[:, j * 128:(j + 1) * 128], Pp.ap[:, j, :], PpT.ap[:, j, :], [Pp, PpT])
                    dve(lambda e: e.tensor_copy(out=fl(PnT), in_=ps2.ap), [ps2], [PnT])
                    ps3 = S.psum()
                    for j in range(4):
                        pemm(ps3, ps3.ap[:, j * 128:(j + 1) * 128], PnT.ap[:, j, :], Tp.ap[:, j, :], [PnT, Tp])
                    tt_(fl(Tn), ps3.ap, fl(Tp), ALU.add, [ps3, Tp], [Tn])
                    Pp, PpT, Pn, PnT = Pn, PnT, Pp, PpT
                    Tp, Tn = Tn, Tp
                Tf = Tp
                ps = S.psum()
                for j in range(4):
                    pemm(ps, ps.ap[:, j * 64:(j + 1) * 64], Tf.ap[:, j, :], AHT.ap[:, j, :], [Tf, AHT])
                act(lambda e: e.activation(out=fl(AtT), in_=ps.ap[:, 0:256], func=AF.Copy), [ps], [AtT])
                ps = S.psum()
                for j in range(4):
                    pemm(ps, ps.ap[:, j * 128:(j + 1) * 128], Tf.ap[:, j, :], MTm.ap[:, j, :], [Tf, MTm])
                dve(lambda e: e.tensor_copy(out=fl(X1), in_=ps.ap), [ps], [X1])
                ps = S.psum()
                for j in range(4):
                    pemm(ps, ps.ap[0:64, j * 64:(j + 1) * 64], AtT.ap[:, j, :], BPT.ap[:, j, :], [AtT, BPT])
                tt_(DG.ap, id64b, E1.ap[:, :, 127:128].broadcast_to([64, 4, 64]), ALU.mult, [ident, E1], [DG])
                tt_(fl(Gs), ps.ap[0:64, 0:256], fl(DG), ALU.add, [ps, DG], [Gs])
                ps = S.psum()
                for j in range(4):
                    pemm(ps, ps.ap[:, j * 64:(j + 1) * 64], X1.ap[:, j, :], BPT.ap[:, j, :], [X1, BPT])
                tt_(fl(WT), ps.ap[:, 0:256], fl(KPT), ALU.add, [ps, KPT], [WT])
                ps = S.psum()
                for j in range(4):
                    pemm(ps, ps.ap[0:64, j * 128:(j + 1) * 128], AtT.ap[:, j, :], Pm.ap[:, j, :], [AtT, Pm])
                tt_(fl(Rt), ps.ap[0:64, :], fl(RH), ALU.add, [ps, RH], [Rt])
                ps = S.psum()
                for j in range(4):
                    pemm(ps, ps.ap[:, j * 128:(j + 1) * 128], X1.ap[:, j, :], Pm.ap[:, j, :], [X1, Pm])
                tt_(fl(Zm), ps.ap, fl(Qm), ALU.add, [ps, Qm], [Zm])
                py = S.psum()
                for j in range(4):
                    pemm(py, py.ap[:, j * 64:(j + 1) * 64], Rt.ap[:, j, :], STc.ap[:, j, :], [Rt, STc], True, False)
                    pemm(py, py.ap[:, j * 64:(j + 1) * 64], Zm.ap[:, j, :], VT.ap[:, j, :], [Zm, VT], False, True)
                pst = S.psum()
                for j in range(4):
                    pemm(pst, pst.ap[0:64, j * 64:(j + 1) * 64], Gs.ap[:, j, :], STc.ap[:, j, :], [Gs, STc], True, False)
                    pemm(pst, pst.ap[0:64, j * 64:(j + 1) * 64], WT.ap[:, j, :], VT.ap[:, j, :], [WT, VT], False, True)
                act(lambda e: e.activation(out=fl(STn), in_=pst.ap[0:64, 0:256], func=AF.Copy), [pst], [STn])
                act(lambda e: e.activation(out=fl(Yk), in_=py.ap[:, 0:256], func=AF.Copy), [py], [Yk])
                dve(lambda e: e.tensor_reduce(out=sm.ap, in_=Yk.ap, axis=AX.X, op=ALU.add), [Yk], [sm])
                dve(lambda e: e.tensor_scalar(out=sm.ap, in0=sm.ap, scalar1=-1.0 / 64, scalar2=None, op0=ALU.mult), [sm], [sm])
                tt_(Yd.ap, Yk.ap, sm.ap.unsqueeze(2).broadcast_to([128, 4, 64]), ALU.add, [Yk, sm], [Yd])
                tt_(Ysq.ap, Yd.ap, Yd.ap, ALU.mult, [Yd], [Ysq])
                dve(lambda e: e.tensor_reduce(out=sv.ap, in_=Ysq.ap, axis=AX.X, op=ALU.add), [Ysq], [sv])
                act(lambda e: e.activation(out=sv.ap, in_=sv.ap, func=AF.Sqrt, bias=gneps.ap, scale=1.0 / 64), [sv, gneps], [sv])
                dve(lambda e: e.reciprocal(out=sv.ap, in_=sv.ap), [sv], [sv])
                tt_(Yd.ap, Yd.ap, sv.ap.unsqueeze(2).broadcast_to([128, 4, 64]), ALU.mult, [Yd, sv], [Yd])
                po = S.psum()
                for j in range(4):
                    S.op("pe", lambda e, j=j: e.transpose(out=po.ap[0:64, j * 128:(j + 1) * 128], in_=Yd.ap[:, j, :],
                                                           identity=ident.ap), reads=[Yd.key, ident.key], writes=[po.key])
                tt_(OF.ap, v3(po), bc_h(vh(5), 128), ALU.mult, [po, vec], [OF])
                tt_(OF.ap, OF.ap, bc_h(vh(6), 128), ALU.add, [OF, vec], [OF])
                tt_(OF.ap, OF.ap, BON.ap, ALU.add, [OF, BON], [OF])
                tt_(OB.ap, OF.ap, GT.ap, ALU.mult, [OF, GT], [OB])
                S.dma("sp", self.YT.ap[h0 * 64:(h0 + 4) * 64, t0:t0 + 128].rearrange("(h c) t -> c h t", c=64), OB.ap,
                      reads=[OB.key], writes=[self.YT.key])


    def nsa(self, l):
        nc, S = self.nc, self.S
        T = self.T
        NQT = T // 128
        NC = T // 16 - 1
        NT = (NC + 127) // 128
        ident = self.ident
        rot = {"i": 0, "a": 0}

        def psr():
            b = S.banks[rot["i"] % 5]
            rot["i"] += 1
            return b

        def psa():
            b = S.banks[5 + rot["a"] % 3]
            rot["a"] += 1
            return b

        def W(name, shape, dt=F32):
            return sb(nc, "ns%d_" % l + name, shape, dt)

        def dve(fn, reads, writes):
            S.op("dve", fn, reads=[b.key for b in reads], writes=[b.key for b in writes])

        def act(fn, reads, writes):
            S.op("act", fn, reads=[b.key for b in reads], writes=[b.key for b in writes])

        def pemm(ps, sl, lhsT, rhs, reads, start=True, stop=True):
            S.op("pe", lambda e: e.matmul(sl, lhsT=lhsT, rhs=rhs, start=start, stop=stop),
                 reads=[b.key for b in reads], writes=[ps.key])

        KCMPT = W("KCMPT", [64, 3, 256], BF16)
        VCMP = W("VCMP", [128, 2, 3, 65], BF16)
        OVL = W("OVL", [128, 2, 65], BF16)
        KST = W("KST", [64, 3, T], BF16)
        KWT = W("KWT", [64, 3, T], BF16)
        VS = W("VS", [128, NQT, 3, 65], BF16)
        VW = W("VW", [128, NQT, 3, 65], BF16)
        EXPD = W("EXPD", [64, 32, 128], BF16)
        identb = W("identb", [128, 128], BF16)
        CAUS = W("CAUS", [128, 4, 128], BF16)
        BAND = W("BAND", [128, 4, 128], BF16)
        S.op("dve", lambda e: e.memset(KCMPT.ap, 0.0), writes=[KCMPT.key])
        S.op("dve", lambda e: e.memset(VCMP.ap, 0.0), writes=[VCMP.key])
        S.op("dve", lambda e: e.memset(VCMP.ap[:, :, :, 64:65], 1.0), writes=[VCMP.key])
        S.op("dve", lambda e: e.memset(VS.ap[:, :, :, 64:65], 1.0), writes=[VS.key])
        S.op("dve", lambda e: e.memset(VW.ap[:, :, :, 64:65], 1.0), writes=[VW.key])
        S.op("dve", lambda e: e.tensor_copy(out=identb.ap, in_=ident.ap), reads=[ident.key], writes=[identb.key])
        S.dma("pool", OVL.ap, self.c_ovl.ap.rearrange("a p c -> p a c"), reads=[self.c_ovl.key], writes=[OVL.key])
        S.dma("pool", EXPD.ap, self.c_expand.ap, reads=[self.c_expand.key], writes=[EXPD.key])
        S.dma("pool", CAUS.ap, self.c_caus.ap[0].unsqueeze(1).broadcast_to([128, 4, 128]), reads=[self.c_caus.key],
              writes=[CAUS.key])
        S.dma("pool", BAND.ap, self.c_caus.ap[1].unsqueeze(1).broadcast_to([128, 4, 128]), reads=[self.c_caus.key],
              writes=[BAND.key])
        S.dma("sp", KST.ap, self.QKT.ap[768:960, 0:T].rearrange("(h c) t -> c h t", c=64), reads=[self.QKT.key],
              writes=[KST.key])
        S.dma("sp", KWT.ap, self.QKT.ap[960:1152, 0:T].rearrange("(h c) t -> c h t", c=64), reads=[self.QKT.key],
              writes=[KWT.key])
        for k0 in range(0, NQT, 4):
            for h in range(3):
                S.dma("sp", VS.ap[:, k0:k0 + 4, h, 0:64],
                      self.VSW.ap[k0 * 128:(k0 + 4) * 128, h * 64:(h + 1) * 64].rearrange("(k p) c -> p k c", p=128),
                      reads=[self.VSW.key], writes=[VS.key])
                S.dma("sp", VW.ap[:, k0:k0 + 4, h, 0:64],
                      self.VSW.ap[k0 * 128:(k0 + 4) * 128, 192 + h * 64:192 + (h + 1) * 64].rearrange("(k p) c -> p k c", p=128),
                      reads=[self.VSW.key], writes=[VW.key])

        w1 = W("w1", [64, 32, 256]); w2c = W("w2c", [128, 2, 64]); pe = W("pe", [64, 32])
        biasc = W("biasc", [128, 2]); XC = W("XC", [64, T]); Hc = W("Hc", [128, 2, 256])
        hx = W("hx", [128, 256]); hu = W("hu", [128, 256])
        KCt = W("KCt", [128, 2, 3, 64]); CSC = W("CSC", [128, 2, 16]); rtc = W("rtc", [128, 4, 6, 8])
        S.op("dve", lambda e: e.memset(KCt.ap, 0.0), writes=[KCt.key])
        S.dma("sp", CSC.ap, self.c_ropec.ap.rearrange("(a p) c -> p a c", p=128), reads=[self.c_ropec.key], writes=[CSC.key])
        XC3 = XC.ap.rearrange("p (n b) -> p n b", b=16)
        for kv in range(2):
            S.dma("sp", w1.ap, self.cw1.ap[l, kv].rearrange("d (a f) -> d a f", f=256), reads=[self.cw1.key], writes=[w1.key])
            S.dma("sp", w2c.ap, self.cw2.ap[l, kv], reads=[self.cw2.key], writes=[w2c.key])
            S.dma("sp", pe.ap, self.cpe.ap[l, kv], reads=[self.cpe.key], writes=[pe.key])
            for fh in range(2):
                ps = psr()
                for l_ in range(32):
                    pemm(ps, ps.ap[:, 0:1], w1.ap[:, l_, fh * 128:(fh + 1) * 128], pe.ap[:, l_:l_ + 1], [w1, pe], l_ == 0, l_ == 31)
                dve(lambda e: e.tensor_copy(out=biasc.ap[:, fh:fh + 1], in_=ps.ap[:, 0:1]), [ps], [biasc])
            for h in range(3):
                r0 = 3072 + kv * 192 + h * 64
                S.dma("sp", XC.ap, self.FMT.ap[r0:r0 + 64, 0:T], reads=[self.FMT.key], writes=[XC.key])
                for fh in range(2):
                    ps = psr()
                    for l_ in range(32):
                        a_, b_ = l_ // 16, l_ % 16
                        pemm(ps, ps.ap[:, 0:NC], w1.ap[:, l_, fh * 128:(fh + 1) * 128], XC3[:, a_:a_ + NC, b_], [w1, XC],
                             l_ == 0, l_ == 31)
                    x_, u_ = hx.ap[:, 0:NC], hu.ap[:, 0:NC]
                    dve(lambda e: e.tensor_scalar(out=x_, in0=ps.ap[:, 0:NC], scalar1=biasc.ap[:, fh:fh + 1], scalar2=None,
                                                  op0=ALU.add), [ps, biasc], [hx])
                    dve(lambda e: e.tensor_tensor(out=u_, in0=x_, in1=x_, op=ALU.mult), [hx], [hu])
                    dve(lambda e: e.tensor_scalar(out=u_, in0=u_, scalar1=0.044715, scalar2=1.0, op0=ALU.mult, op1=ALU.add),
                        [hu], [hu])
                    dve(lambda e: e.tensor_tensor(out=u_, in0=u_, in1=x_, op=ALU.mult), [hu, hx], [hu])
                    act(lambda e: e.activation(out=u_, in_=u_, func=AF.Tanh, scale=0.7978845608028654), [hu], [hu])
                    dve(lambda e: e.scalar_tensor_tensor(out=u_, in0=u_, scalar=1.0, in1=x_, op0=ALU.add, op1=ALU.mult),
                        [hu, hx], [hu])
                    dve(lambda e: e.tensor_scalar(out=Hc.ap[:, fh, 0:NC], in0=u_, scalar1=0.5, scalar2=None, op0=ALU.mult),
                        [hu], [Hc])
                for nt in range(NT):
                    n0 = nt * 128
                    nn = min(128, NC - n0)
                    ps = psr()
                    for fh in range(2):
                        pemm(ps, ps.ap[0:nn, 0:64], Hc.ap[:, fh, n0:n0 + nn], w2c.ap[:, fh, :], [Hc, w2c], fh == 0, fh == 1)
                    if kv == 1:
                        act(lambda e: e.activation(out=VCMP.ap[0:nn, nt, h, 0:64], in_=ps.ap[0:nn, 0:64], func=AF.Copy),
                            [ps], [VCMP])
                    else:
                        act(lambda e: e.activation(out=KCt.ap[0:nn, nt, h, :], in_=ps.ap[0:nn, 0:64], func=AF.Copy),
                            [ps], [KCt])
            if kv == 0:
                v = KCt.ap.rearrange("p a h c -> p (a h) c")
                x1, x2 = v[:, :, 0:8], v[:, :, 8:16]
                cosb = CSC.ap[:, :, 0:8].unsqueeze(2).broadcast_to([128, 2, 3, 8]).rearrange("p a h c -> p (a h) c") \
                    if False else None
                for nt in range(NT):
                    vv = KCt.ap[:, nt]
                    x1, x2 = vv[:, :, 0:8], vv[:, :, 8:16]
                    cb_ = CSC.ap[:, nt, 0:8].unsqueeze(1).broadcast_to([128, 3, 8])
                    sb_ = CSC.ap[:, nt, 8:16].unsqueeze(1).broadcast_to([128, 3, 8])
                    r_ = rtc.ap
                    dve(lambda e: e.tensor_tensor(out=r_[:, 0, 0:3], in0=x1, in1=cb_, op=ALU.mult), [KCt, CSC], [rtc])
                    dve(lambda e: e.tensor_tensor(out=r_[:, 1, 0:3], in0=x2, in1=sb_, op=ALU.mult), [KCt, CSC], [rtc])
                    dve(lambda e: e.tensor_tensor(out=r_[:, 2, 0:3], in0=x2, in1=cb_, op=ALU.mult), [KCt, CSC], [rtc])
                    dve(lambda e: e.tensor_tensor(out=r_[:, 3, 0:3], in0=x1, in1=sb_, op=ALU.mult), [KCt, CSC], [rtc])
                    dve(lambda e: e.tensor_tensor(out=x1, in0=r_[:, 0, 0:3], in1=r_[:, 1, 0:3], op=ALU.subtract), [rtc], [KCt])
                    dve(lambda e: e.tensor_tensor(out=x2, in0=r_[:, 2, 0:3], in1=r_[:, 3, 0:3], op=ALU.add), [rtc], [KCt])
                    for h in range(3):
                        ps = psr()
                        S.op("pe", lambda e: e.transpose(out=ps.ap[0:64, 0:128], in_=KCt.ap[:, nt, h, :], identity=ident.ap),
                             reads=[KCt.key, ident.key], writes=[ps.key])
                        act(lambda e: e.activation(out=KCMPT.ap[:, h, nt * 128:(nt + 1) * 128], in_=ps.ap[0:64, 0:128],
                                                   func=AF.Copy), [ps], [KCMPT])

        Gt = W("Gt", [128, 36]); FORCE = W("FORCE", [128, 64]); VALID = W("VALID", [128, 64])
        CM = W("CM", [128, 2, 4, 128], BF16)
        QTt = Pool(nc, "ns%dQT" % l, [64, 4, 128], BF16, 2)
        expool = Pool(nc, "ns%dex" % l, [128, 512], BF16, 4)
        rz = W("rz", [128, 4]); coef = W("coef", [128, 4]); imp = W("imp", [128, 64]); sc2 = W("sc2", [128, 64])
        m16 = W("m16", [128, 16]); negm = W("negm", [128, 64]); NEGT = W("NEGT", [64, 4, 128], BF16)
        YN = W("YN", [128, 4, 64]); ytmp = W("ytmp", [128, 4, 64]); OBn = W("OBn", [64, 4, 128], BF16)
        fl2 = lambda b: b.ap.rearrange("p a b -> p (a b)")

        def combine(bank, h, br, first):
            bv = bank.ap[:, 0:260].rearrange("p (g c) -> p g c", g=4)
            dve(lambda e: e.tensor_scalar(out=rz.ap, in0=bv[:, :, 64], scalar1=1e-30, scalar2=None, op0=ALU.max), [bank], [rz])
            dve(lambda e: e.reciprocal(out=rz.ap, in_=rz.ap), [rz], [rz])
            gv = Gt.ap[:, h * 12:(h + 1) * 12].rearrange("p (g b) -> p g b", b=3)[:, :, br]
            dve(lambda e: e.tensor_tensor(out=coef.ap, in0=rz.ap, in1=gv, op=ALU.mult), [rz, Gt], [coef])
            cb_ = coef.ap.unsqueeze(2).broadcast_to([128, 4, 64])
            if first:
                dve(lambda e: e.tensor_tensor(out=YN.ap, in0=bv[:, :, 0:64], in1=cb_, op=ALU.mult), [bank, coef], [YN])
            else:
                dve(lambda e: e.tensor_tensor(out=ytmp.ap, in0=bv[:, :, 0:64], in1=cb_, op=ALU.mult), [bank, coef], [ytmp])
                dve(lambda e: e.tensor_tensor(out=YN.ap, in0=YN.ap, in1=ytmp.ap, op=ALU.add), [YN, ytmp], [YN])

        for qt in range(NQT):
            q0 = qt * 128
            S.dma("sp", Gt.ap, self.GL.ap[q0:q0 + 128, :], reads=[self.GL.key], writes=[Gt.key])
            S.dma("sp", FORCE.ap, self.c_force.ap[qt], reads=[self.c_force.key], writes=[FORCE.key])
            S.dma("sp", VALID.ap, self.c_valid.ap[qt], reads=[self.c_valid.key], writes=[VALID.key])
            nts = [nt for nt in range(NT) if 16 * (nt * 128) + 31 <= q0 + 127]
            for nt in nts:
                S.dma("pool", CM.ap[:, nt], self.c_cmask.ap[qt, nt].unsqueeze(1).broadcast_to([128, 4, 128]),
                      reads=[self.c_cmask.key], writes=[CM.key])
            for h in range(3):
                qb = QTt.next()
                S.dma("sp", qb.ap, self.QKT.ap[h * 256:(h + 1) * 256, q0:q0 + 128].rearrange("(g c) t -> c g t", c=64),
                      reads=[self.QKT.key], writes=[qb.key])
                q2 = fl2(qb)
                po, pi = psa(), psa()
                for ii, nt in enumerate(nts):
                    ps = psr()
                    pemm(ps, ps.ap, KCMPT.ap[:, h, nt * 128:(nt + 1) * 128], q2, [KCMPT, qb], True, False)
                    pemm(ps, ps.ap, identb.ap, CM.ap[:, nt].rearrange("p a b -> p (a b)"), [identb, CM], False, True)
                    ex = expool.next()
                    act(lambda e: e.activation(out=ex.ap, in_=ps.ap, func=AF.Exp), [ps], [ex])
                    for g in range(4):
                        pemm(po, po.ap[:, g * 65:(g + 1) * 65], ex.ap[:, g * 128:(g + 1) * 128], VCMP.ap[:, nt, h, :], [ex, VCMP],
                             ii == 0 and g == 0, ii == len(nts) - 1 and g == 3)
                    for g in range(4):
                        pemm(pi, pi.ap[:, g * 65:(g + 1) * 65], ex.ap[:, g * 128:(g + 1) * 128], OVL.ap[:, nt, :], [ex, OVL],
                             ii == 0 and g == 0, ii == len(nts) - 1 and g == 3)
                piv = pi.ap[:, 0:260].rearrange("p (g c) -> p g c", g=4)
                dve(lambda e: e.tensor_scalar(out=rz.ap, in0=piv[:, :, 64], scalar1=1e-30, scalar2=None, op0=ALU.max), [pi], [rz])
                dve(lambda e: e.reciprocal(out=rz.ap, in_=rz.ap), [rz], [rz])
                dve(lambda e: e.tensor_scalar(out=imp.ap, in0=piv[:, 0, 0:64], scalar1=rz.ap[:, 0:1], scalar2=None, op0=ALU.mult),
                    [pi, rz], [imp])
                for g in range(1, 4):
                    dve(lambda e: e.scalar_tensor_tensor(out=imp.ap, in0=piv[:, g, 0:64], scalar=rz.ap[:, g:g + 1], in1=imp.ap,
                                                         op0=ALU.mult, op1=ALU.add), [pi, rz, imp], [imp])
                dve(lambda e: e.tensor_tensor(out=imp.ap, in0=imp.ap, in1=FORCE.ap, op=ALU.max), [imp, FORCE], [imp])
                dve(lambda e: e.tensor_tensor(out=imp.ap, in0=imp.ap, in1=VALID.ap, op=ALU.add), [imp, VALID], [imp])
                dve(lambda e: e.max(out=m16.ap[:, 0:8], in_=imp.ap), [imp], [m16])
                dve(lambda e: e.match_replace(out=sc2.ap, in_to_replace=m16.ap[:, 0:8], in_values=imp.ap, imm_value=-2e30),
                    [imp, m16], [sc2])
                dve(lambda e: e.max(out=m16.ap[:, 8:16], in_=sc2.ap), [sc2], [m16])
                dve(lambda e: e.tensor_scalar(out=negm.ap, in0=imp.ap, scalar1=m16.ap[:, 15:16], scalar2=None, op0=ALU.is_ge),
                    [imp, m16], [negm])
                dve(lambda e: e.tensor_scalar(out=negm.ap, in0=negm.ap, scalar1=-1.0, scalar2=-NEG, op0=ALU.add, op1=ALU.mult),
                    [negm], [negm])
                pT = psr()
                S.op("pe", lambda e: e.transpose(out=pT.ap[0:64, 0:128], in_=negm.ap, identity=ident.ap),
                     reads=[negm.key, ident.key], writes=[pT.key])
                dve(lambda e: e.tensor_copy(out=NEGT.ap, in_=pT.ap[0:64, 0:128].unsqueeze(1).broadcast_to([64, 4, 128])),
                    [pT], [NEGT])
                combine(po, h, 0, True)
                pss = psa()
                for kt in range(qt + 1):
                    ps = psr()
                    pemm(ps, ps.ap, KST.ap[:, h, kt * 128:(kt + 1) * 128], q2, [KST, qb], True, False)
                    pemm(ps, ps.ap, EXPD.ap[:, kt, :], fl2(NEGT), [EXPD, NEGT], False, kt != qt)
                    if kt == qt:
                        pemm(ps, ps.ap, identb.ap, fl2(CAUS), [identb, CAUS], False, True)
                    ex = expool.next()
                    act(lambda e: e.activation(out=ex.ap, in_=ps.ap, func=AF.Exp), [ps], [ex])
                    for g in range(4):
                        pemm(pss, pss.ap[:, g * 65:(g + 1) * 65], ex.ap[:, g * 128:(g + 1) * 128], VS.ap[:, kt, h, :], [ex, VS],
                             kt == 0 and g == 0, kt == qt and g == 3)
                combine(pss, h, 1, False)
                psw = psa()
                kts = list(range(max(0, qt - 4), qt + 1))
                for kt in kts:
                    ps = psr()
                    masked = (kt == qt) or (kt == qt - 4)
                    pemm(ps, ps.ap, KWT.ap[:, h, kt * 128:(kt + 1) * 128], q2, [KWT, qb], True, not masked)
                    if kt == qt:
                        pemm(ps, ps.ap, identb.ap, fl2(CAUS), [identb, CAUS], False, True)
                    elif kt == qt - 4:
                        pemm(ps, ps.ap, identb.ap, fl2(BAND), [identb, BAND], False, True)
                    ex = expool.next()
                    act(lambda e: e.activation(out=ex.ap, in_=ps.ap, func=AF.Exp), [ps], [ex])
                    for g in range(4):
                        pemm(psw, psw.ap[:, g * 65:(g + 1) * 65], ex.ap[:, g * 128:(g + 1) * 128], VW.ap[:, kt, h, :], [ex, VW],
                             kt == kts[0] and g == 0, kt == qt and g == 3)
                combine(psw, h, 2, False)
                pso = psr()
                for g in range(4):
                    S.op("pe", lambda e, g=g: e.transpose(out=pso.ap[0:64, g * 128:(g + 1) * 128], in_=YN.ap[:, g, :],
                                                           identity=ident.ap), reads=[YN.key, ident.key], writes=[pso.key])
                act(lambda e: e.activation(out=fl2(OBn), in_=pso.ap[0:64, :], func=AF.Copy), [pso], [OBn])
                S.dma("sp", self.YT.ap[1280 + h * 256:1280 + (h + 1) * 256, q0:q0 + 128].rearrange("(g c) t -> c g t", c=64),
                      OBn.ap, reads=[OBn.key], writes=[self.YT.key])

    def loop3(self, l, dst):
        S = self.S
        xtok, xT, hT = self.xtok, self.xT, self.hT
        for tb in range(self.NTB):
            t0 = tb * 512
            for tt in range(4):
                S.dma("sp", xtok.ap[:, tt, :], self.X1.ap[t0 + tt * 128:t0 + (tt + 1) * 128, :], reads=[self.X1.key],
                      writes=[self.kx[tt]])
            S.dma("sp", xT.ap, self.YT.ap[:, t0:t0 + 512].rearrange("(k p) t -> p k t", p=128), reads=[self.YT.key],
                  writes=[xT.key])
            for tt in range(4):
                S.op("pool", lambda e, tt=tt: e.tensor_scalar_mul(out=xtok.ap[:, tt, :], in0=xtok.ap[:, tt, :], scalar1=ALPHA),
                     reads=[self.kx[tt]], writes=[self.kx[tt]])

            def evac(tt, cb, ps, nco):
                S.op("dve", lambda e: e.tensor_tensor(out=xtok.ap[:, tt, cb * 512:(cb + 1) * 512], in0=ps.ap,
                                                      in1=xtok.ap[:, tt, cb * 512:(cb + 1) * 512], op=ALU.add),
                     reads=[ps.key, self.kx[tt]], writes=[self.kx[tt]])
            self.tokmm(xT, 16, self.woutB, 4, 4, evac)
            for tt in range(4):
                self.layernorm(xtok, tt, 1)
            self.make_xT(xtok, xT, 0)
            self.ffn_ln(l, 1, xtok, xT, hT)
            for tt in range(4):
                S.dma("sp", dst.ap[t0 + tt * 128:t0 + (tt + 1) * 128, :], xtok.ap[:, tt, :], reads=[self.kx[tt]],
                      writes=[dst.key])

    def poolmix(self, l):
        nc, S = self.nc, self.S
        T = self.T
        PP = sb(nc, "pPP%d" % l, [128, 16 + T], F32)
        A = sb(nc, "pA%d" % l, [128, 16 + T], F32)
        Bq = sb(nc, "pB%d" % l, [128, 16 + T], F32)
        RC = sb(nc, "pRC%d" % l, [128, T], F32)
        PW = sb(nc, "pPW%d" % l, [128, 4, 128], F32)
        PV = sb(nc, "pPV%d" % l, [128, 8], F32)
        ob = Pool(nc, "pob%d_" % l, [128, 512], BF16, 2)
        S.dma("sp", PW.ap, self.poolw.ap[l].rearrange("g c d -> c g d"), reads=[self.poolw.key], writes=[PW.key])
        S.dma("sp", PV.ap, self.poolv.ap[l], reads=[self.poolv.key], writes=[PV.key])
        for b_ in (PP, A, Bq):
            S.op("dve", lambda e: e.memset(b_.ap[:, 0:16], 0.0), writes=[b_.key])
        for gi, win in enumerate((2, 4, 8, 16)):
            S.dma("sp", PP.ap[:, 16:16 + T], self.FMT.ap[2560 + gi * 128:2560 + (gi + 1) * 128, 0:T], reads=[self.FMT.key],
                  writes=[PP.key])
            S.dma("sp", RC.ap, self.c_rcnt.ap[gi, 0:T].partition_broadcast(128), reads=[self.c_rcnt.key], writes=[RC.key])
            cur, nxt = PP, A
            sh = 1
            while sh < win:
                S.op("dve", lambda e: e.tensor_tensor(out=nxt.ap[:, 16:16 + T], in0=cur.ap[:, 16:16 + T],
                                                      in1=cur.ap[:, 16 - sh:16 - sh + T], op=ALU.add),
                     reads=[cur.key], writes=[nxt.key])
                cur = nxt
                nxt = Bq if cur is A else A
                sh *= 2
            S.op("dve", lambda e: e.tensor_tensor(out=nxt.ap[:, 16:16 + T], in0=cur.ap[:, 16:16 + T], in1=RC.ap, op=ALU.mult),
                 reads=[cur.key, RC.key], writes=[nxt.key])
            z = nxt
            S.op("dve", lambda e: e.tensor_tensor(out=z.ap[:, 16:16 + T], in0=z.ap[:, 16:16 + T], in1=PP.ap[:, 16:16 + T],
                                                  op=ALU.subtract), reads=[z.key, PP.key], writes=[z.key])
            for tb in range(T // 512):
                ps = S.psum()
                S.op("pe", lambda e: e.matmul(ps.ap, lhsT=PW.ap[:, gi, :], rhs=z.ap[:, 16 + tb * 512:16 + (tb + 1) * 512],
                                              start=True, stop=True), reads=[PW.key, z.key], writes=[ps.key])
                o = ob.next()
                S.op("dve", lambda e: e.tensor_scalar(out=o.ap, in0=ps.ap, scalar1=PV.ap[:, gi:gi + 1],
                                                      scalar2=PV.ap[:, 4 + gi:5 + gi], op0=ALU.add, op1=ALU.mult),
                     reads=[ps.key, PV.key], writes=[o.key])
                S.dma("sp", self.YT.ap[768 + gi * 128:768 + (gi + 1) * 128, tb * 512:(tb + 1) * 512], o.ap, reads=[o.key],
                      writes=[self.YT.key])

def wdn_view(wdn, l):
    return Buf(wdn.ap[l], wdn.key)


def tok_layout(w, G):
    K, N = w.shape
    nk = (K + 127) // 128
    ng = (nk + G - 1) // G
    ncb = (N + 511) // 512
    wp = np.zeros((ng * G * 128, ncb * 512), np.float32)
    wp[:K, :N] = w
    wp = wp.reshape(ng, G, 128, ncb, 512).transpose(3, 0, 2, 1, 4)
    return np.ascontiguousarray(wp).reshape(ncb, ng, 128, G * 512)


def host_layout(inp, nlayer):
    L = nlayer
    o = {}
    for i, nm in ((1, "ffn1"), (2, "ffn2")):
        wu = inp[nm + "_w_up"][:L]
        wu = wu.reshape(L, 16, 128, 2, NF, 128).transpose(0, 4, 2, 3, 1, 5)
        o["wup%d" % i] = np.ascontiguousarray(wu).reshape(L, NF, 128, 4096)
        o["wdn%d" % i] = np.stack([tok_layout(inp[nm + "_w_down"][l], 4) for l in range(L)])
    lns = {1: (inp["ln1_g"], inp["ln1_b"]), 2: (inp["ln2_g"], inp["ln2_b"]), 3: (inp["ln3_g"], inp["ln3_b"])}
    for i in (1, 2, 3):
        o["ln%dg" % i] = np.ascontiguousarray(lns[i][0][:L])
        o["ln%db" % i] = np.ascontiguousarray(lns[i][1][:L])
    win = inp["w_in"][:L]
    nb = 3072
    fm_cols = np.concatenate([np.arange(0, 3072), nb + np.arange(768, 1152)])
    tm_cols = np.concatenate([nb + np.arange(0, 768), nb + np.arange(1152, 1344), nb + np.arange(1536, 1728),
                              nb + np.arange(1344, 1536), nb + np.arange(1728, 1920), nb + np.arange(1920, 1956)])
    wfm = win[:, :, fm_cols].reshape(L, 16, 128, NFM, 128).transpose(0, 3, 2, 1, 4)
    o["winfm"] = np.ascontiguousarray(wfm).reshape(L, NFM, 128, 2048)
    o["wintm"] = np.stack([tok_layout(win[l][:, tm_cols], 4) for l in range(L)])
    o["wout"] = np.stack([tok_layout(inp["w_out"][l], 4) for l in range(L)])
    o["rwmu"] = np.ascontiguousarray(inp["rw_mu"][:L].reshape(L, 20, 128).transpose(0, 2, 1))
    o["gateb"] = np.ascontiguousarray(inp["nsa_gate_b"][:L])
    o["rw_w2"] = np.ascontiguousarray(inp["rw_w2"][:L])
    o["rw_a2"] = np.ascontiguousarray(inp["rw_a2"][:L])
    o["rw_g2"] = np.ascontiguousarray(inp["rw_g2"][:L])
    vecs = [inp["rw_w0"], inp["rw_a0"], inp["rw_k_k"], inp["rw_k_a"], inp["rw_r_k"].reshape(-1, 768), inp["rw_gn_g"],
            inp["rw_gn_b"]]
    o["rwvec"] = np.ascontiguousarray(np.stack([v[:L].reshape(L, 12, 64) for v in vecs], axis=1).transpose(0, 3, 1, 2))
    o["poolw"] = np.ascontiguousarray(inp["pool_w"][:L])
    w1s = np.stack([inp["nsa_cmp_k_w1"][:L], inp["nsa_cmp_v_w1"][:L]], axis=1)
    o["cw1"] = np.ascontiguousarray(w1s.transpose(0, 1, 3, 2, 4)).reshape(L, 2, 64, 32 * 256)
    w2s = np.stack([inp["nsa_cmp_k_w2"][:L], inp["nsa_cmp_v_w2"][:L]], axis=1)
    o["cw2"] = np.ascontiguousarray(w2s.reshape(L, 2, 2, 128, 64).transpose(0, 1, 3, 2, 4))
    pes = np.stack([inp["nsa_cmp_pe_k"][:L], inp["nsa_cmp_pe_v"][:L]], axis=1)
    o["cpe"] = np.ascontiguousarray(pes.transpose(0, 1, 3, 2))
    pv = np.concatenate([inp["pool_b"][:L].reshape(L, 4, 128), inp["pool_scale"][:L].reshape(L, 4, 128)], axis=1)
    o["poolv"] = np.ascontiguousarray(pv.transpose(0, 2, 1))
    return o


def host_consts():
    c = {}
    inv_freq = (500000.0 ** (-np.arange(8, dtype=np.float32) / 8)).astype(np.float32)
    c["c_ident"] = np.eye(128, dtype=np.float32)
    ii = np.arange(128)
    c["c_masks"] = np.stack([(ii[:, None] < ii[None, :]), (ii[:, None] <= ii[None, :]), (ii[:, None] > ii[None, :])]).astype(np.float32)
    angc = (np.arange(256, dtype=np.float32) * 16 + 31)[:, None] * inv_freq
    c["c_ropec"] = np.concatenate([np.cos(angc), np.sin(angc)], axis=1).astype(np.float32)
    n_cmp = 255
    cmp_start = np.arange(256) * 16
    sel_start = np.arange(64) * 64
    ov = np.clip(np.minimum(cmp_start[:, None] + 32, sel_start[None, :] + 64) - np.maximum(cmp_start[:, None], sel_start[None, :]),
                 0, None) / 32.0
    ovl = np.concatenate([ov, np.ones((256, 1))], axis=1).astype(np.float32)
    ovl[255] = 0.0
    c["c_ovl"] = ovl.reshape(2, 128, 65)
    jj = np.arange(64)[:, None, None]
    c["c_expand"] = (jj == (2 * np.arange(32)[None, :, None] + np.arange(128)[None, None, :] // 64)).astype(np.float32)
    kk_, qq_ = np.arange(128)[:, None], np.arange(128)[None, :]
    c["c_caus"] = np.stack([np.where(kk_ > qq_, NEG, 0.0), np.where(kk_ <= qq_, NEG, 0.0)]).astype(np.float32)
    qabs = np.arange(SEQ).reshape(32, 128)
    cur = qabs // 64
    jb = np.arange(64)[None, None, :]
    forced = (jb == 0) | (jb == cur[:, :, None]) | (jb == cur[:, :, None] - 1)
    c["c_force"] = np.where(forced, 1e9, 0.0).astype(np.float32)
    c["c_valid"] = np.where(jb > cur[:, :, None], -1e30, 0.0).astype(np.float32)
    nabs = np.arange(256).reshape(2, 128)
    cend = nabs * 16 + 31
    ok = (cend[None, :, :, None] <= qabs[:, None, None, :]) & (nabs[None, :, :, None] < n_cmp)
    c["c_cmask"] = np.where(ok, 0.0, NEG).astype(np.float32)
    t1 = np.arange(1, SEQ + 1, dtype=np.float32)
    c["c_rcnt"] = np.stack([1.0 / np.minimum(t1, float(w)) for w in (2, 4, 8, 16)]).astype(np.float32)
    half = 8
    inv_freq = (500000.0 ** (-np.arange(half, dtype=np.float32) / half)).astype(np.float32)
    ang = np.arange(SEQ, dtype=np.float32)[:, None] * inv_freq
    c["c_rope"] = np.concatenate([np.cos(ang), np.sin(ang)], axis=1).astype(np.float32)
    return c


_CACHE = {}


def run(inputs, nlayer=NLAYER, trun=SEQ, debug=False, stop_after=None, trace=False):
    key = (nlayer, trun, debug, stop_after)
    if key not in _CACHE:
        _CACHE[key] = Prog(nlayer, trun, debug, stop_after)
    prog = _CACHE[key]
    shared = host_layout(inputs, nlayer)
    shared.update(host_consts())
    in_maps = []
    for c in range(8):
        m = dict(shared)
        m["x"] = np.ascontiguousarray(inputs["x"][c % 4])
        m = {k: v for k, v in m.items() if k in prog.din}
        in_maps.append(m)
    res = run_bass_kernel_spmd(prog.nc, in_maps, core_ids=list(range(8)), trace=trace)
    return prog, res


def kernel(**inputs):
    inputs = {k: np.asarray(v) for k, v in inputs.items()}
    prog, res = run(inputs)
    out = np.stack([res.results[b]["out"] for b in range(4)], axis=0)
    return out.astype(np.float32)
```

```python
from contextlib import ExitStack
import numpy as np
import ml_dtypes
import concourse.bass as bass
import concourse.mybir as mybir
from concourse.bass_utils import run_bass_kernel_spmd

F32 = mybir.dt.float32
BF16 = mybir.dt.bfloat16
AF = mybir.ActivationFunctionType
ALU = mybir.AluOpType
AX = mybir.AxisListType

D = 2048
SEQ = 4096
NLAYER = 4
DFF = 5504
NF = 43
ALPHA = float((2 * NLAYER) ** 0.25)
LN_EPS = 1e-5
RWC = 2560
NFM = 27
NTM = 1572
CDEC = float(np.exp(-0.5))
GN_EPS = 64e-5
NEG = -30000.0


class Key:
    __slots__ = ("name", "w", "r")

    def __init__(self, name=""):
        self.name = name
        self.w = None
        self.r = []


class Buf:
    def __init__(self, ap, key):
        self.ap = ap
        self.key = key


class Sched:
    EPOCH = 30000
    NDS = 40

    def __init__(self, nc):
        self.nc = nc
        self.engs = {"pe": nc.tensor, "act": nc.scalar, "dve": nc.vector, "pool": nc.gpsimd, "sp": nc.sync}
        self.cnt = {e: 0 for e in self.engs}
        self.sems = {e: [] for e in self.engs}
        self.known = {e: {} for e in self.engs}
        self.dsems = [nc.alloc_semaphore("dq%d" % i) for i in range(self.NDS)]
        self.dcum = [0] * self.NDS
        self.dnext = 0
        self.banks = []
        for i in range(8):
            t = nc.alloc_psum_tensor("psb%d" % i, [128, 512], F32).ap()
            self.banks.append(Buf(t, Key("ps%d" % i)))
        self.bnext = 0
        self.ninst = 0
        self.pooltok = []

    def psum(self):
        b = self.banks[self.bnext]
        self.bnext = (self.bnext + 1) % 8
        return b

    def _esem(self, e, n):
        i = n // self.EPOCH
        while len(self.sems[e]) <= i:
            self.sems[e].append(self.nc.alloc_semaphore("s_%s_%d" % (e, len(self.sems[e]))))
        return self.sems[e][i], n % self.EPOCH + 1

    def _wait(self, e, tok):
        if tok[0] == "E":
            _, src, n = tok
            if self.known[e].get(("E", src), -1) >= n:
                return
            if src == e and e == "pe":
                return
            sem, val = self._esem(src, n)
            self.engs[e].wait_ge(sem, val)
            self.known[e][("E", src)] = n
        else:
            _, s, val = tok
            if self.known[e].get(("D", s), 0) >= val:
                return
            self.engs[e].wait_ge(self.dsems[s], val)
            self.known[e][("D", s)] = val
        self.ninst += 1

    def _deps(self, e, reads, writes, is_dma):
        deps = []
        for k in reads:
            if k.w is not None:
                deps.append(k.w)
        for k in writes:
            if k.w is not None:
                deps.append(k.w)
            for t in k.r:
                if (not is_dma) and t[0] == "E" and t[1] == e:
                    continue
                deps.append(t)
        for t in deps:
            self._wait(e, t)

    def _commit(self, e, tok, reads, writes):
        for k in reads:
            if tok[0] == "E":
                k.r = [t for t in k.r if not (t[0] == "E" and t[1] == tok[1])]
            k.r.append(tok)
        for k in writes:
            k.w = tok
            k.r = []

    def op(self, e, fn, reads=(), writes=()):
        self._deps(e, reads, writes, False)
        ins = fn(self.engs[e])
        n = self.cnt[e]
        self.cnt[e] += 1
        sem, _ = self._esem(e, n)
        ins.then_inc(sem, 1)
        self._commit(e, ("E", e, n), reads, writes)
        self.ninst += 1
        return ins

    def dma(self, q, out, in_, reads=(), writes=()):
        s = self.dnext
        self.dnext = (self.dnext + 1) % self.NDS
        if self.dcum[s] > 0:
            self._wait(q, ("D", s, self.dcum[s]))
        if q == "pool":
            if len(self.pooltok) >= 4:
                self._wait(q, self.pooltok[-4])
        self._deps(q, reads, writes, True)
        ins = self.engs[q].dma_start(out=out, in_=in_)
        self.dcum[s] += 16
        ins.then_inc(self.dsems[s], 16)
        self._commit(q, ("D", s, self.dcum[s]), reads, writes)
        if q == "pool":
            self.pooltok.append(("D", s, self.dcum[s]))
            self.pooltok = self.pooltok[-8:]
        self.ninst += 1
        return ins

    def barrier(self):
        for e in self.engs:
            for src in ("pe", "act", "dve", "pool"):
                if self.cnt[src] > 0:
                    n = self.cnt[src] - 1
                    if src == e and e == "pe":
                        continue
                    if self.known[e].get(("E", src), -1) < n:
                        sem, val = self._esem(src, n)
                        self.engs[e].wait_ge(sem, val)
                        self.known[e][("E", src)] = n
            for s in range(self.NDS):
                if self.dcum[s] > 0:
                    self._wait(e, ("D", s, self.dcum[s]))


class Pool:
    def __init__(self, nc, name, shape, dtype, n):
        self.bufs = [sb(nc, "%s%d" % (name, i), shape, dtype) for i in range(n)]
        self.i = 0

    def next(self):
        b = self.bufs[self.i]
        self.i = (self.i + 1) % len(self.bufs)
        return b


_STACK = [None]


_CNT = [0]


def sb(nc, name, shape, dtype):
    _CNT[0] += 1
    name = "%s_u%d" % (name, _CNT[0])
    h = _STACK[0].enter_context(nc.sbuf_tensor(name, list(shape), dtype))
    return Buf(h.ap(), Key(name))


class Prog:
    def __init__(self, nlayer=NLAYER, trun=SEQ, debug=False, stop_after=None):
        self.nlayer = nlayer
        self.T = trun
        self.NTB = trun // 512
        self.debug = debug
        self.stop_after = stop_after
        nc = bass.Bass("TRN2", target_bir_lowering=False)
        self.nc = nc
        self.S = Sched(nc)
        self.din = {}
        self._declare_io()
        self.build()

    def inp(self, name, shape, dtype=F32):
        t = nc_t = self.nc.dram_tensor(name, list(shape), dtype, kind="ExternalInput").ap()
        self.din[name] = (tuple(shape), dtype)
        return Buf(t, Key(name))

    def scratch(self, name, shape, dtype=F32):
        kind = "ExternalOutput" if self.debug else "Internal"
        t = self.nc.dram_tensor(name, list(shape), dtype, kind=kind).ap()
        return Buf(t, Key(name))

    def _declare_io(self):
        L = self.nlayer
        T = self.T
        self.x_in = self.inp("x", [SEQ, D])
        self.out = Buf(self.nc.dram_tensor("out", [SEQ, D], F32, kind="ExternalOutput").ap(), Key("out"))
        self.wup = [self.inp("wup%d" % i, [L, NF, 128, 4096]) for i in (1, 2)]
        self.wdn = [self.inp("wdn%d" % i, [L, 4, 11, 128, 2048]) for i in (1, 2)]
        self.lng = [self.inp("ln%dg" % i, [L, D]) for i in (1, 2, 3)]
        self.lnb = [self.inp("ln%db" % i, [L, D]) for i in (1, 2, 3)]
        self.winfm = self.inp("winfm", [L, NFM, 128, 2048])
        self.wintm = self.inp("wintm", [L, 4, 4, 128, 2048])
        self.wout = self.inp("wout", [L, 4, 4, 128, 2048])
        self.mu = self.inp("rwmu", [L, 128, 20])
        self.gateb = self.inp("gateb", [L, 36])
        self.rw_w2 = self.inp("rw_w2", [L, 64, 768])
        self.rw_a2 = self.inp("rw_a2", [L, 64, 768])
        self.rw_g2 = self.inp("rw_g2", [L, 128, 768])
        self.rwvec = self.inp("rwvec", [L, 64, 7, 12])
        self.poolw = self.inp("poolw", [L, 4, 128, 128])
        self.poolv = self.inp("poolv", [L, 128, 8])
        self.cw1 = self.inp("cw1", [L, 2, 64, 32 * 256])
        self.cw2 = self.inp("cw2", [L, 2, 128, 2, 64])
        self.cpe = self.inp("cpe", [L, 2, 64, 32])
        self.c_ropec = self.inp("c_ropec", [256, 16])
        self.c_ovl = self.inp("c_ovl", [2, 128, 65])
        self.c_expand = self.inp("c_expand", [64, 32, 128])
        self.c_caus = self.inp("c_caus", [2, 128, 128])
        self.c_force = self.inp("c_force", [32, 128, 64])
        self.c_valid = self.inp("c_valid", [32, 128, 64])
        self.c_cmask = self.inp("c_cmask", [32, 2, 128, 128])
        self.c_masks = self.inp("c_masks", [3, 128, 128])
        self.c_rcnt = self.inp("c_rcnt", [4, SEQ])
        self.c_ident = self.inp("c_ident", [128, 128])
        self.c_rope = self.inp("c_rope", [SEQ, 16])
        self.X1 = self.scratch("X1", [SEQ, D])
        self.XN = self.scratch("XN", [SEQ, D])
        self.FMT = self.scratch("FMT", [NFM * 128, SEQ])
        self.QKT = self.scratch("QKT", [1152, SEQ], BF16)
        self.VSW = self.scratch("VSW", [SEQ, 384], BF16)
        self.GL = self.scratch("GL", [SEQ, 36])
        self.YT = self.scratch("YT", [D, SEQ], BF16)

        def wscr(name, shape):
            t = self.nc.dram_tensor(name, list(shape), BF16, kind="Internal").ap()
            return t, [Key("%s_%d" % (name, i)) for i in range(shape[0])]
        self.wupB = [wscr("wupB%d" % i, [NF, 128, 4096]) for i in (1, 2)]
        self.wdnB = [wscr("wdnB%d" % i, [44, 128, 2048]) for i in (1, 2)]
        self.winfmB = wscr("winfmB", [NFM, 128, 2048])
        self.wintmB = wscr("wintmB", [16, 128, 2048])
        self.woutB = wscr("woutB", [16, 128, 2048])

    def _consts(self):
        nc, S = self.nc, self.S
        self.ident = sb(nc, "ident", [128, 128], F32)
        S.dma("sp", self.ident.ap, self.c_ident.ap, reads=[self.c_ident.key], writes=[self.ident.key])
        self.epsc = sb(nc, "epsc", [128, 1], F32)
        S.op("dve", lambda e: e.memset(self.epsc.ap, LN_EPS), writes=[self.epsc.key])

    def mm(self, ps, lhsT, rhs, start, stop, reads):
        self.S.op("pe", lambda e: e.matmul(ps.ap if isinstance(ps, Buf) else ps, lhsT=lhsT, rhs=rhs, start=start, stop=stop),
                  reads=reads, writes=[ps.key] if isinstance(ps, Buf) else [])

    def make_xT(self, xtok, xT, tog):
        S = self.S
        for dc in range(16):
            ps = S.psum()
            for tt in range(4):
                S.op("pe", lambda e, tt=tt: e.transpose(out=ps.ap[:, tt * 128:(tt + 1) * 128],
                                                         in_=xtok.ap[:, tt, dc * 128:(dc + 1) * 128],
                                                         identity=self.ident.ap),
                     reads=[self.kx[tt], self.ident.key], writes=[ps.key])
            if (dc + tog) % 2 == 0:
                S.op("act", lambda e: e.activation(out=xT.ap[:, dc, :], in_=ps.ap, func=AF.Copy),
                     reads=[ps.key], writes=[xT.key])
            else:
                S.op("dve", lambda e: e.tensor_copy(out=xT.ap[:, dc, :], in_=ps.ap), reads=[ps.key], writes=[xT.key])

    def tokmm(self, lhsT, nk, wsrc, ncb, G, evac, ncols=None):
        S = self.S
        ng = (nk + G - 1) // G
        for cb in range(ncb):
            nco = 512 if ncols is None else ncols[cb]
            banks = [S.psum() for _ in range(4)]
            for g in range(ng):
                w = self.wpool.next()
                wap, wkeys = wsrc
                S.dma(self.wq(), w.ap[:, 0:G * 512], wap[cb * ng + g], reads=[wkeys[cb * ng + g]], writes=[w.key])
                for tt in range(4):
                    for fi in range(G):
                        f = g * G + fi
                        if f >= nk:
                            continue
                        self.S.op("pe", lambda e, tt=tt, f=f, fi=fi: e.matmul(
                            banks[tt].ap[:, 0:nco], lhsT=lhsT.ap[:, f, tt * 128:(tt + 1) * 128],
                            rhs=w.ap[:, fi * 512:fi * 512 + nco], start=(f == 0), stop=(f == nk - 1)),
                            reads=[lhsT.key, w.key], writes=[banks[tt].key])
            for tt in range(4):
                evac(tt, cb, banks[tt], nco)

    def convert(self, l, which):
        S = self.S

        def cv(dstp, src_ap, src_key):
            dst, keys = dstp
            for i in range(len(keys)):
                S.dma("pool", dst[i], src_ap[i], reads=[src_key], writes=[keys[i]])
        if which == 0:
            cv(self.wupB[0], self.wup[0].ap[l], self.wup[0].key)
            cv(self.wdnB[0], self.wdn[0].ap[l].rearrange("a b p c -> (a b) p c"), self.wdn[0].key)
            cv(self.winfmB, self.winfm.ap[l], self.winfm.key)
            cv(self.wintmB, self.wintm.ap[l].rearrange("a b p c -> (a b) p c"), self.wintm.key)
        else:
            cv(self.woutB, self.wout.ap[l].rearrange("a b p c -> (a b) p c"), self.wout.key)
            cv(self.wupB[1], self.wup[1].ap[l], self.wup[1].key)
            cv(self.wdnB[1], self.wdn[1].ap[l].rearrange("a b p c -> (a b) p c"), self.wdn[1].key)

    def wq(self):
        self._wq = getattr(self, "_wq", 0) + 1
        return "pool"

    def layer_consts(self, l, first=True):
        nc, S = self.nc, self.S
        for i in ((0,) if first else (1, 2)):
            S.dma("sp", self.lnG[i].ap, self.lng[i].ap[l].partition_broadcast(128), reads=[self.lng[i].key],
                  writes=[self.lnG[i].key])
            S.dma("sp", self.lnB[i].ap, self.lnb[i].ap[l].partition_broadcast(128), reads=[self.lnb[i].key],
                  writes=[self.lnB[i].key])
        if not first:
            return
        S.dma("sp", self.MU.ap, self.mu.ap[l], reads=[self.mu.key], writes=[self.MU.key])
        S.dma("sp", self.GATEB.ap, self.gateb.ap[l].partition_broadcast(128), reads=[self.gateb.key],
              writes=[self.GATEB.key])

    def ffn_ln(self, l, which, xtok, xT, hT):
        S = self.S
        wupB, wdnB = self.wupB[which], self.wdnB[which]
        lni = 0 if which == 0 else 2
        for tt in range(4):
            S.op("pool", lambda e, tt=tt: e.tensor_scalar_mul(out=xtok.ap[:, tt, :], in0=xtok.ap[:, tt, :], scalar1=ALPHA),
                 reads=[self.kx[tt]], writes=[self.kx[tt]])
        for f in range(NF):
            w = self.wpool.next()
            S.dma(self.wq(), w.ap, wupB[0][f], reads=[wupB[1][f]], writes=[w.key])
            pa, pb = S.psum(), S.psum()
            for kc in range(16):
                self.mm(pa, w.ap[:, kc * 128:(kc + 1) * 128], xT.ap[:, kc, :], kc == 0, kc == 15, [w.key, xT.key])
            for kc in range(16):
                self.mm(pb, w.ap[:, (16 + kc) * 128:(17 + kc) * 128], xT.ap[:, kc, :], kc == 0, kc == 15,
                        [w.key, xT.key])
            sa = self.sapool.next()
            S.op("act", lambda e: e.activation(out=sa.ap, in_=pa.ap, func=AF.Silu), reads=[pa.key], writes=[sa.key])
            S.op("dve", lambda e: e.tensor_tensor(out=hT.ap[:, f, :], in0=pb.ap, in1=sa.ap, op=ALU.mult),
                 reads=[pb.key, sa.key], writes=[hT.key])

        def evac(tt, cb, ps, nco):
            S.op("dve", lambda e: e.scalar_tensor_tensor(out=xtok.ap[:, tt, cb * 512:(cb + 1) * 512], in0=ps.ap,
                                                         scalar=0.5, in1=xtok.ap[:, tt, cb * 512:(cb + 1) * 512],
                                                         op0=ALU.mult, op1=ALU.add),
                 reads=[ps.key, self.kx[tt]], writes=[self.kx[tt]])
        self.tokmm(hT, NF, wdnB, 4, 4, evac)
        for tt in range(4):
            self.layernorm(xtok, tt, lni)

    def layernorm(self, xtok, tt, lni):
        S = self.S
        st, mv, rs = self.lnst, self.lnmv, self.lnrs
        for j in range(4):
            S.op("dve", lambda e, j=j: e.bn_stats(out=st.ap[:, j, :], in_=xtok.ap[:, tt, j * 512:(j + 1) * 512]),
                 reads=[self.kx[tt]], writes=[st.key])
        S.op("dve", lambda e: e.bn_aggr(out=mv.ap, in_=st.ap.rearrange("p a b -> p (a b)")), reads=[st.key], writes=[mv.key])
        S.op("act", lambda e: e.activation(out=rs.ap, in_=mv.ap[:, 1:2], func=AF.Sqrt, bias=self.epsc.ap, scale=1.0),
             reads=[mv.key, self.epsc.key], writes=[rs.key])
        S.op("dve", lambda e: e.reciprocal(out=rs.ap, in_=rs.ap), reads=[rs.key], writes=[rs.key])
        S.op("dve", lambda e: e.tensor_scalar(out=xtok.ap[:, tt, :], in0=xtok.ap[:, tt, :], scalar1=mv.ap[:, 0:1],
                                              scalar2=rs.ap[:, 0:1], op0=ALU.subtract, op1=ALU.mult),
             reads=[self.kx[tt], mv.key, rs.key], writes=[self.kx[tt]])
        S.op("pool", lambda e: e.tensor_tensor(out=xtok.ap[:, tt, :], in0=xtok.ap[:, tt, :], in1=self.lnG[lni].ap, op=ALU.mult),
             reads=[self.kx[tt], self.lnG[lni].key], writes=[self.kx[tt]])
        S.op("pool", lambda e: e.tensor_tensor(out=xtok.ap[:, tt, :], in0=xtok.ap[:, tt, :], in1=self.lnB[lni].ap, op=ALU.add),
             reads=[self.kx[tt], self.lnB[lni].key], writes=[self.kx[tt]])

    def alloc_loop(self):
        nc = self.nc
        self.xtok = sb(nc, "xtok", [128, 4, D], F32)
        self.kx = [Key("xtok%d" % i) for i in range(4)]
        self.xT = sb(nc, "xT", [128, 16, 512], BF16)
        self.hT = sb(nc, "hT", [128, NF, 512], BF16)
        self.wpool = Pool(nc, "wp", [128, 4096], BF16, 6)
        self.sapool = Pool(nc, "sa", [128, 512], F32, 2)
        self.lnst = sb(nc, "lnst", [128, 4, 6], F32)
        self.lnmv = sb(nc, "lnmv", [128, 2], F32)
        self.lnrs = sb(nc, "lnrs", [128, 1], F32)
        self.lnG = [sb(nc, "lnG%d" % i, [128, D], F32) for i in range(2)]
        self.lnB = [sb(nc, "lnB%d" % i, [128, D], F32) for i in range(2)]
        self.lnG.append(self.lnG[0])
        self.lnB.append(self.lnB[0])
        self.MU = sb(nc, "MU", [128, 20], F32)
        self.GATEB = sb(nc, "GATEB", [128, 36], F32)
        self.halo = sb(nc, "halo", [128, 20], F32)
        self.stage = Pool(nc, "stg", [128, 513], F32, 2)
        self.ost = Pool(nc, "ost", [128, 512], F32, 3)
        self.dtmp = sb(nc, "dtmp", [128, 512], F32)
        tmv = self.hT.ap.rearrange("p a b -> p (a b)").bitcast(F32)[:, 0:4 * NTM].rearrange("p (a c) -> p a c", a=4)
        self.TM = Buf(tmv, self.hT.key)
        self.CS = sb(nc, "CS", [128, 4, 16], F32)
        self.rtmp = sb(nc, "rtmp", [128, 4, 18, 8], F32)
        self.bst = Pool(nc, "bst", [128, 512], BF16, 3)
        self.vst = sb(nc, "vst", [128, 4, 384], BF16)
        self.gst = sb(nc, "gst", [128, 4, 36], F32)

    def loop1(self, l, src):
        S = self.S
        xtok, xT, hT = self.xtok, self.xT, self.hT
        S.op("dve", lambda e: e.memset(self.halo.ap, 0.0), writes=[self.halo.key])
        for tb in range(self.NTB):
            t0 = tb * 512
            for tt in range(4):
                S.dma("sp", xtok.ap[:, tt, :], src.ap[t0 + tt * 128:t0 + (tt + 1) * 128, :], reads=[src.key],
                      writes=[self.kx[tt]])
            S.dma("sp", self.CS.ap, self.c_rope.ap[t0:t0 + 512, :].rearrange("(a p) c -> p a c", p=128),
                  reads=[self.c_rope.key], writes=[self.CS.key])
            self.make_xT(xtok, xT, 0)
            self.ffn_ln(l, 0, xtok, xT, hT)
            for tt in range(4):
                S.dma("sp", self.X1.ap[t0 + tt * 128:t0 + (tt + 1) * 128, :], xtok.ap[:, tt, :], reads=[self.kx[tt]],
                      writes=[self.X1.key])
            self.make_xT(xtok, xT, 1)
            self.win_proj(l, t0, xT)

    def win_proj(self, l, t0, xT):
        S = self.S
        for ch in range(NFM):
            w = self.wpool.next()
            S.dma(self.wq(), w.ap[:, 0:2048], self.winfmB[0][ch], reads=[self.winfmB[1][ch]], writes=[w.key])
            ps = S.psum()
            for kc in range(16):
                self.mm(ps, w.ap[:, kc * 128:(kc + 1) * 128], xT.ap[:, kc, :], kc == 0, kc == 15, [w.key, xT.key])
            o = self.ost.next()
            if ch < 20:
                st = self.stage.next()
                d = self.dtmp
                S.op("act", lambda e: e.activation(out=st.ap[:, 1:513], in_=ps.ap, func=AF.Copy), reads=[ps.key],
                     writes=[st.key])
                S.op("dve", lambda e: e.tensor_copy(out=st.ap[:, 0:1], in_=self.halo.ap[:, ch:ch + 1]),
                     reads=[self.halo.key], writes=[st.key])
                S.op("dve", lambda e: e.tensor_sub(out=d.ap, in0=st.ap[:, 0:512], in1=st.ap[:, 1:513]), reads=[st.key],
                     writes=[d.key])
                S.op("dve", lambda e: e.tensor_copy(out=self.halo.ap[:, ch:ch + 1], in_=st.ap[:, 512:513]),
                     reads=[st.key], writes=[self.halo.key])
                S.op("dve", lambda e: e.scalar_tensor_tensor(out=o.ap, in0=d.ap, scalar=self.MU.ap[:, ch:ch + 1],
                                                             in1=st.ap[:, 1:513], op0=ALU.mult, op1=ALU.add),
                     reads=[d.key, st.key, self.MU.key], writes=[o.key])
            else:
                S.op("act", lambda e: e.activation(out=o.ap, in_=ps.ap, func=AF.Copy), reads=[ps.key], writes=[o.key])
            S.dma("sp", self.FMT.ap[ch * 128:(ch + 1) * 128, t0:t0 + 512], o.ap, reads=[o.key], writes=[self.FMT.key])
        TM = self.TM

        def evac(tt, cb, ps, nco):
            if (tt + cb) % 2 == 0:
                S.op("act", lambda e: e.activation(out=TM.ap[:, tt, cb * 512:cb * 512 + nco], in_=ps.ap[:, 0:nco], func=AF.Copy),
                     reads=[ps.key], writes=[TM.key])
            else:
                S.op("dve", lambda e: e.tensor_copy(out=TM.ap[:, tt, cb * 512:cb * 512 + nco], in_=ps.ap[:, 0:nco]),
                     reads=[ps.key], writes=[TM.key])
        self.tokmm(xT, 16, self.wintmB, 4, 4, evac, ncols=[512, 512, 512, 36])
        rt = self.rtmp
        for tt in range(4):
            v = TM.ap[:, tt, 0:1152].rearrange("p (h c) -> p h c", c=64)
            x1, x2 = v[:, :, 0:8], v[:, :, 8:16]
            cosb = self.CS.ap[:, tt, 0:8].unsqueeze(1).broadcast_to([128, 18, 8])
            sinb = self.CS.ap[:, tt, 8:16].unsqueeze(1).broadcast_to([128, 18, 8])
            rk = [TM.key, self.CS.key]
            S.op("dve", lambda e: e.tensor_tensor(out=rt.ap[:, 0], in0=x1, in1=cosb, op=ALU.mult), reads=rk, writes=[rt.key])
            S.op("dve", lambda e: e.tensor_tensor(out=rt.ap[:, 1], in0=x2, in1=sinb, op=ALU.mult), reads=rk, writes=[rt.key])
            S.op("dve", lambda e: e.tensor_tensor(out=rt.ap[:, 2], in0=x2, in1=cosb, op=ALU.mult), reads=rk, writes=[rt.key])
            S.op("dve", lambda e: e.tensor_tensor(out=rt.ap[:, 3], in0=x1, in1=sinb, op=ALU.mult), reads=rk, writes=[rt.key])
            S.op("dve", lambda e: e.tensor_tensor(out=x1, in0=rt.ap[:, 0], in1=rt.ap[:, 1], op=ALU.subtract),
                 reads=[rt.key], writes=[TM.key])
            S.op("dve", lambda e: e.tensor_tensor(out=x2, in0=rt.ap[:, 2], in1=rt.ap[:, 3], op=ALU.add),
                 reads=[rt.key], writes=[TM.key])
        for ch in range(9):
            ps = S.psum()
            for tt in range(4):
                S.op("pe", lambda e, tt=tt: e.transpose(out=ps.ap[:, tt * 128:(tt + 1) * 128],
                                                         in_=TM.ap[:, tt, ch * 128:(ch + 1) * 128], identity=self.ident.ap),
                     reads=[TM.key, self.ident.key], writes=[ps.key])
            o = self.bst.next()
            S.op("act", lambda e: e.mul(o.ap, ps.ap, (0.125 if ch < 6 else 1.0)), reads=[ps.key], writes=[o.key])
            S.dma("sp", self.QKT.ap[ch * 128:(ch + 1) * 128, t0:t0 + 512], o.ap, reads=[o.key], writes=[self.QKT.key])
        for tt in range(4):
            S.op("act", lambda e, tt=tt: e.activation(out=self.vst.ap[:, tt, :], in_=TM.ap[:, tt, 1152:1536], func=AF.Copy),
                 reads=[TM.key], writes=[self.vst.key])
            S.op("dve", lambda e, tt=tt: e.tensor_tensor(out=self.gst.ap[:, tt, :], in0=TM.ap[:, tt, 1536:1572],
                                                         in1=self.GATEB.ap, op=ALU.add),
                 reads=[TM.key, self.GATEB.key], writes=[self.gst.key])
        S.op("act", lambda e: e.activation(out=self.gst.ap, in_=self.gst.ap, func=AF.Sigmoid), reads=[self.gst.key],
             writes=[self.gst.key])
        S.dma("sp", self.VSW.ap[t0:t0 + 512, :].rearrange("(a p) c -> p a c", p=128), self.vst.ap, reads=[self.vst.key],
              writes=[self.VSW.key])
        S.dma("sp", self.GL.ap[t0:t0 + 512, :].rearrange("(a p) c -> p a c", p=128), self.gst.ap, reads=[self.gst.key],
              writes=[self.GL.key])

    def build(self):
        S = self.S
        self.gstack = ExitStack()
        _STACK[0] = self.gstack
        self._consts()
        src = self.x_in
        sa = self.stop_after
        self.convert(0, 0)
        for l in range(self.nlayer):
            last = (l == self.nlayer - 1)
            with ExitStack() as st:
                _STACK[0] = st
                self.alloc_loop()
                self.layer_consts(l, True)
                self.loop1(l, src)
                S.barrier()
            if sa == "loop1":
                break
            with ExitStack() as st:
                _STACK[0] = st
                self.convert(l, 1)
                if not last:
                    self.convert(l + 1, 0)
                self.rwkv(l)
                S.barrier()
            if sa == "rwkv":
                break
            with ExitStack() as st:
                _STACK[0] = st
                self.poolmix(l)
                S.barrier()
            if sa == "pool":
                break
            with ExitStack() as st:
                _STACK[0] = st
                self.nsa(l)
                S.barrier()
            if sa == "nsa":
                break
            with ExitStack() as st:
                _STACK[0] = st
                self.alloc_loop()
                self.layer_consts(l, False)
                self.loop3(l, self.out if last else self.XN)
                S.barrier()
            src = self.XN
        S.barrier()


    def rwkv(self, l):
        nc, S = self.nc, self.S
        T = self.T
        NCH = T // 128
        ident = self.ident

        def W(name, shape, dt=F32):
            return sb(nc, "rk%d_" % l + name, shape, dt)
        w2 = W("w2", [64, 768]); a2 = W("a2", [64, 768]); g2 = W("g2", [128, 768])
        vec = W("vec", [64, 7, 12])
        omka = W("omka", [64, 12])
        ones = W("ones", [64, 64])
        msk = W("msk", [128, 3, 128])
        rmask = W("rmask", [64, 4, 128])
        gneps = W("gneps", [128, 1])
        S.dma("sp", w2.ap, self.rw_w2.ap[l], reads=[self.rw_w2.key], writes=[w2.key])
        S.dma("sp", a2.ap, self.rw_a2.ap[l], reads=[self.rw_a2.key], writes=[a2.key])
        S.dma("sp", g2.ap, self.rw_g2.ap[l], reads=[self.rw_g2.key], writes=[g2.key])
        S.dma("sp", vec.ap, self.rwvec.ap[l], reads=[self.rwvec.key], writes=[vec.key])
        S.dma("sp", msk.ap, self.c_masks.ap.rearrange("m p c -> p m c"), reads=[self.c_masks.key], writes=[msk.key])
        S.op("dve", lambda e: e.memset(ones.ap, 1.0), writes=[ones.key])
        S.op("dve", lambda e: e.memset(gneps.ap, GN_EPS), writes=[gneps.key])
        S.op("dve", lambda e: e.memset(rmask.ap, 1.0), writes=[rmask.key])
        S.op("dve", lambda e: e.memset(rmask.ap[:, :, 0:1], 0.0), writes=[rmask.key])
        S.op("dve", lambda e: e.tensor_scalar(out=omka.ap, in0=vec.ap[:, 3, :], scalar1=-1.0, scalar2=1.0, op0=ALU.mult,
                                              op1=ALU.add), reads=[vec.key], writes=[omka.key])
        def mkset(si):
            W_ = lambda name, shape, dt=F32: W("s%d_" % si + name, shape, dt)
            R = W_("R", [64, 4, 128]); K = W_("K", [64, 4, 128]); V = W_("V", [64, 4, 128])
            SG = W_("SG", [64, 4, 128]); A = W_("A", [64, 4, 128]); GT = W_("GT", [64, 4, 128])
            KM = W_("KM", [64, 4, 128]); CUM = W_("CUM", [64, 4, 128]); E1 = W_("E1", [64, 4, 128])
            E2 = W_("E2", [64, 4, 128]); BT = W_("BT", [64, 4, 128]); BON = W_("BON", [64, 4, 128])
            t1 = W_("t1", [64, 4, 128]); t2 = W_("t2", [64, 4, 128])
            WL = W_("WL", [64, 128]); AL = W_("AL", [64, 128]); GLr = W_("GLr", [128, 128])
            tw = W_("tw", [64, 128]); sg = W_("sg", [128, 128])
            VT = W_("VT", [128, 4, 64]); AHT = W_("AHT", [128, 4, 64]); BPT = W_("BPT", [128, 4, 64]); KPT = W_("KPT", [128, 4, 64])
            AtT = W_("AtT", [128, 4, 64]); WT = W_("WT", [128, 4, 64]); Yk = W_("Yk", [128, 4, 64]); Yd = W_("Yd", [128, 4, 64])
            Ysq = W_("Ysq", [128, 4, 64])
            Nm = W_("Nm", [128, 4, 128]); NTm = W_("NTm", [128, 4, 128]); MTm = W_("MTm", [128, 4, 128])
            Pm = W_("Pm", [128, 4, 128]); Qm = W_("Qm", [128, 4, 128]); Pb = W_("Pb", [128, 4, 128]); PbT = W_("PbT", [128, 4, 128])
            Ta = W_("Ta", [128, 4, 128]); Tb = W_("Tb", [128, 4, 128]); X1 = W_("X1", [128, 4, 128]); Zm = W_("Zm", [128, 4, 128])
            Gs = W_("Gs", [64, 4, 64]); DG = W_("DG", [64, 4, 64]); Rt = W_("Rt", [64, 4, 128])
            sm = W_("sm", [128, 4]); sv = W_("sv", [128, 4])
            OB = W_("OB", [64, 4, 128], BF16)
            OF = W_("OF", [64, 4, 128])


            return dict(R=R, K=K, V=V, SG=SG, A=A, GT=GT, KM=KM, CUM=CUM, E1=E1, E2=E2, BT=BT, BON=BON, t1=t1, t2=t2, WL=WL, AL=AL, GLr=GLr, tw=tw, sg=sg, VT=VT, AHT=AHT, BPT=BPT, KPT=KPT, AtT=AtT, WT=WT, Yk=Yk, Yd=Yd, Ysq=Ysq, Nm=Nm, NTm=NTm, MTm=MTm, Pm=Pm, Qm=Qm, Pb=Pb, PbT=PbT, Ta=Ta, Tb=Tb, X1=X1, Zm=Zm, Gs=Gs, DG=DG, Rt=Rt, sm=sm, sv=sv, OB=OB, OF=OF)
        sets = [mkset(0), mkset(1)]
        STs = [[W("ST%d_%d" % (g, i), [64, 4, 64]) for i in range(2)] for g in range(3)]
        def bc_h(v, n):
            return v.unsqueeze(2).broadcast_to([64, 4, n])

        def dve(fn, reads, writes):
            S.op("dve", fn, reads=[b.key for b in reads], writes=[b.key for b in writes])

        def act(fn, reads, writes):
            S.op("act", fn, reads=[b.key for b in reads], writes=[b.key for b in writes])

        def tt_(out, in0, in1, op, reads, writes):
            dve(lambda e: e.tensor_tensor(out=out, in0=in0, in1=in1, op=op), reads, writes)

        def pemm(ps, sl, lhsT, rhs, reads, start=True, stop=True):
            S.op("pe", lambda e: e.matmul(sl, lhsT=lhsT, rhs=rhs, start=start, stop=stop),
                 reads=[b.key for b in reads], writes=[ps.key])

        fl = lambda b: b.ap.rearrange("p a b -> p (a b)")
        mU, mUi, mL = msk.ap[:, 0, :], msk.ap[:, 1, :], msk.ap[:, 2, :]
        b4 = lambda m: m.unsqueeze(1).broadcast_to([128, 4, 128])
        idb = ident.ap.unsqueeze(1).broadcast_to([128, 4, 128])
        id64b = ident.ap[0:64, 0:64].unsqueeze(1).broadcast_to([64, 4, 64])

        def chain(hg, B, bank0):
            R, K, V, SG, A, GT, KM, CUM, E1, E2, BT, BON, t1, t2, WL, AL, GLr, tw, sg, VT, AHT, BPT, KPT, AtT, WT, Yk, Yd, Ysq, Nm, NTm, MTm, Pm, Qm, Pb, PbT, Ta, Tb, X1, Zm, Gs, DG, Rt, sm, sv, OB, OF = [B[n] for n in ['R', 'K', 'V', 'SG', 'A', 'GT', 'KM', 'CUM', 'E1', 'E2', 'BT', 'BON', 't1', 't2', 'WL', 'AL', 'GLr', 'tw', 'sg', 'VT', 'AHT', 'BPT', 'KPT', 'AtT', 'WT', 'Yk', 'Yd', 'Ysq', 'Nm', 'NTm', 'MTm', 'Pm', 'Qm', 'Pb', 'PbT', 'Ta', 'Tb', 'X1', 'Zm', 'Gs', 'DG', 'Rt', 'sm', 'sv', 'OB', 'OF']]
            rot = [0]

            def nb():
                b_ = S.banks[bank0 + rot[0] % 4]
                rot[0] += 1
                return b_
            h0 = hg * 4
            vh = lambda i: vec.ap[:, i, h0:h0 + 4]
            S.op("dve", lambda e: e.memset(STs[hg][0].ap, 0.0), writes=[STs[hg][0].key])
            for c in range(NCH):
                t0 = c * 128
                STc, STn = STs[hg][c % 2], STs[hg][(c + 1) % 2]
                fm = self.FMT
                for (buf, base) in ((R, 0), (K, 768), (V, 1536)):
                    S.dma("sp", buf.ap, fm.ap[base + h0 * 64:base + (h0 + 4) * 64, t0:t0 + 128].rearrange("(h c) t -> c h t", c=64),
                          reads=[fm.key], writes=[buf.key])
                S.dma("sp", WL.ap, fm.ap[2304:2368, t0:t0 + 128], reads=[fm.key], writes=[WL.key])
                S.dma("sp", AL.ap, fm.ap[2368:2432, t0:t0 + 128], reads=[fm.key], writes=[AL.key])
                S.dma("sp", GLr.ap, fm.ap[2432:2560, t0:t0 + 128], reads=[fm.key], writes=[GLr.key])
                act(lambda e: e.activation(out=tw.ap, in_=WL.ap, func=AF.Tanh), [WL], [tw])
                act(lambda e: e.activation(out=sg.ap, in_=GLr.ap, func=AF.Sigmoid), [GLr], [sg])
                yield
                p1, p2, p3 = nb(), nb(), nb()
                for j in range(4):
                    h = h0 + j
                    pemm(p1, p1.ap[0:64, j * 128:(j + 1) * 128], w2.ap[:, h * 64:(h + 1) * 64], tw.ap, [w2, tw])
                    pemm(p2, p2.ap[0:64, j * 128:(j + 1) * 128], a2.ap[:, h * 64:(h + 1) * 64], AL.ap, [a2, AL])
                    pemm(p3, p3.ap[0:64, j * 128:(j + 1) * 128], g2.ap[:, h * 64:(h + 1) * 64], sg.ap, [g2, sg])
                v3 = lambda p: p.ap[0:64, :].rearrange("p (a b) -> p a b", a=4)
                tt_(t1.ap, v3(p1), bc_h(vh(0), 128), ALU.add, [p1, vec], [t1])
                act(lambda e: e.activation(out=SG.ap, in_=t1.ap, func=AF.Sigmoid), [t1], [SG])
                tt_(t2.ap, v3(p2), bc_h(vh(1), 128), ALU.add, [p2, vec], [t2])
                act(lambda e: e.activation(out=A.ap, in_=t2.ap, func=AF.Sigmoid), [t2], [A])
                act(lambda e: e.activation(out=GT.ap, in_=v3(p3), func=AF.Copy), [p3], [GT])
                yield
                tt_(t1.ap, A.ap, bc_h(vh(3), 128), ALU.mult, [A, vec], [t1])
                tt_(t1.ap, t1.ap, bc_h(omka.ap[:, h0:h0 + 4], 128), ALU.add, [t1, omka], [t1])
                tt_(KM.ap, K.ap, t1.ap, ALU.mult, [K, t1], [KM])
                tt_(K.ap, K.ap, bc_h(vh(2), 128), ALU.mult, [K, vec], [K])
                tt_(t1.ap, K.ap, K.ap, ALU.mult, [K], [t1])
                pn = nb()
                pemm(pn, pn.ap[0:64, :], ones.ap, fl(t1), [ones, t1])
                act(lambda e: e.activation(out=fl(t1), in_=pn.ap[0:64, :], func=AF.Sqrt), [pn], [t1])
                dve(lambda e: e.tensor_scalar(out=t1.ap, in0=t1.ap, scalar1=1e-12, scalar2=None, op0=ALU.max), [t1], [t1])
                dve(lambda e: e.reciprocal(out=t1.ap, in_=t1.ap), [t1], [t1])
                tt_(K.ap, K.ap, t1.ap, ALU.mult, [K, t1], [K])
                tt_(A.ap, K.ap, A.ap, ALU.mult, [K, A], [A])
                yield
                tt_(t2.ap, R.ap, KM.ap, ALU.mult, [R, KM], [t2])
                tt_(t2.ap, t2.ap, bc_h(vh(4), 128), ALU.mult, [t2, vec], [t2])
                pb_ = nb()
                pemm(pb_, pb_.ap[0:64, :], ones.ap, fl(t2), [ones, t2])
                tt_(BON.ap, v3(pb_), V.ap, ALU.mult, [pb_, V], [BON])
                yield
                dve(lambda e: e.tensor_tensor_scan(out=fl(CUM), data0=fl(rmask), data1=fl(SG), initial=0.0, op0=ALU.mult,
                                                   op1=ALU.add), [rmask, SG], [CUM])
                act(lambda e: e.activation(out=E1.ap, in_=CUM.ap, func=AF.Exp, scale=-CDEC), [CUM], [E1])
                act(lambda e: e.activation(out=E2.ap, in_=CUM.ap, func=AF.Exp, scale=CDEC), [CUM], [E2])
                tt_(SG.ap, CUM.ap, SG.ap, ALU.subtract, [CUM, SG], [SG])
                act(lambda e: e.activation(out=SG.ap, in_=SG.ap, func=AF.Exp, scale=-CDEC), [SG], [SG])
                tt_(CUM.ap, CUM.ap[:, :, 127:128].broadcast_to([64, 4, 128]), CUM.ap, ALU.subtract, [CUM], [CUM])
                act(lambda e: e.activation(out=CUM.ap, in_=CUM.ap, func=AF.Exp, scale=-CDEC), [CUM], [CUM])
                dve(lambda e: e.scalar_tensor_tensor(out=SG.ap, in0=K.ap, scalar=-1.0, in1=SG.ap, op0=ALU.mult, op1=ALU.mult),
                    [K, SG], [SG])
                tt_(R.ap, R.ap, E1.ap, ALU.mult, [R, E1], [R])
                tt_(BT.ap, A.ap, E2.ap, ALU.mult, [A, E2], [BT])
                tt_(E2.ap, KM.ap, E2.ap, ALU.mult, [KM, E2], [E2])
                tt_(A.ap, A.ap, CUM.ap, ALU.mult, [A, CUM], [A])
                tt_(KM.ap, KM.ap, CUM.ap, ALU.mult, [KM, CUM], [KM])
                yield
                AH, RH, KT, BP, KP = SG, R, E2, A, KM
                for (src_, dst_) in ((V, VT), (AH, AHT), (BP, BPT), (KP, KPT)):
                    ps = nb()
                    for j in range(4):
                        S.op("pe", lambda e, j=j: e.transpose(out=ps.ap[:, j * 64:(j + 1) * 64], in_=src_.ap[:, j, :],
                                                               identity=ident.ap[0:64, 0:64]),
                             reads=[src_.key, ident.key], writes=[ps.key])
                    act(lambda e: e.activation(out=fl(dst_), in_=ps.ap[:, 0:256], func=AF.Copy), [ps], [dst_])
                yield
                def stage(dst_, lh, rh, mask):
                    ps = nb()
                    for j in range(4):
                        pemm(ps, ps.ap[:, j * 128:(j + 1) * 128], lh.ap[:, j, :], rh.ap[:, j, :], [lh, rh])
                    tt_(dst_.ap, ps.ap.rearrange("p (a b) -> p a b", a=4), b4(mask), ALU.mult, [ps, msk], [dst_])
                stage(Nm, BT, AH, mU)
                yield
                stage(NTm, AH, BT, mL)
                stage(MTm, AH, KT, mL)
                yield
                stage(Pm, BT, RH, mUi)
                stage(Qm, KT, RH, mUi)
                tt_(Ta.ap, Nm.ap, idb, ALU.add, [Nm, ident], [Ta])
                yield
                Pp, PpT, Pn, PnT = Nm, NTm, Pb, PbT
                Tp, Tn = Ta, Tb
                for k in range(1, 7):
                    if k < 6:
                        ps = nb()
                        for j in range(4):
                            pemm(ps, ps.ap[:, j * 128:(j + 1) * 128], PpT.ap[:, j, :], Pp.ap[:, j, :], [PpT, Pp])
                        act(lambda e: e.activation(out=fl(Pn), in_=ps.ap, func=AF.Copy), [ps], [Pn])
                    ps2 = nb()
                    for j in range(4):
                        pemm(ps2, ps2.ap[:, j * 128:(j + 1) * 128], Pp.ap[:, j, :], PpT.ap[:, j, :], [Pp, PpT])
                    dve(lambda e: e.tensor_copy(out=fl(PnT), in_=ps2.ap), [ps2], [PnT])
                    ps3 = nb()
                    for j in range(4):
                        pemm(ps3, ps3.ap[:, j * 128:(j + 1) * 128], PnT.ap[:, j, :], Tp.ap[:, j, :], [PnT, Tp])
                    tt_(fl(Tn), ps3.ap, fl(Tp), ALU.add, [ps3, Tp], [Tn])
                    Pp, PpT, Pn, PnT = Pn, PnT, Pp, PpT
                    yield
                    Tp, Tn = Tn, Tp
                Tf = Tp
                yield
                ps = nb()
                for j in range(4):
                    pemm(ps, ps.ap[:, j * 64:(j + 1) * 64], Tf.ap[:, j, :], AHT.ap[:, j, :], [Tf, AHT])
                act(lambda e: e.activation(out=fl(AtT), in_=ps.ap[:, 0:256], func=AF.Copy), [ps], [AtT])
                ps = nb()
                for j in range(4):
                    pemm(ps, ps.ap[:, j * 128:(j + 1) * 128], Tf.ap[:, j, :], MTm.ap[:, j, :], [Tf, MTm])
                dve(lambda e: e.tensor_copy(out=fl(X1), in_=ps.ap), [ps], [X1])
                yield
                ps = nb()
                for j in range(4):
                    pemm(ps, ps.ap[0:64, j * 64:(j + 1) * 64], AtT.ap[:, j, :], BPT.ap[:, j, :], [AtT, BPT])
                tt_(DG.ap, id64b, E1.ap[:, :, 127:128].broadcast_to([64, 4, 64]), ALU.mult, [ident, E1], [DG])
                tt_(fl(Gs), ps.ap[0:64, 0:256], fl(DG), ALU.add, [ps, DG], [Gs])
                ps = nb()
                for j in range(4):
                    pemm(ps, ps.ap[:, j * 64:(j + 1) * 64], X1.ap[:, j, :], BPT.ap[:, j, :], [X1, BPT])
                tt_(fl(WT), ps.ap[:, 0:256], fl(KPT), ALU.add, [ps, KPT], [WT])
                yield
                ps = nb()
                for j in range(4):
                    pemm(ps, ps.ap[0:64, j * 128:(j + 1) * 128], AtT.ap[:, j, :], Pm.ap[:, j, :], [AtT, Pm])
                tt_(fl(Rt), ps.ap[0:64, :], fl(RH), ALU.add, [ps, RH], [Rt])
                ps = nb()
                for j in range(4):
                    pemm(ps, ps.ap[:, j * 128:(j + 1) * 128], X1.ap[:, j, :], Pm.ap[:, j, :], [X1, Pm])
                tt_(fl(Zm), ps.ap, fl(Qm), ALU.add, [ps, Qm], [Zm])
                yield
                py = nb()
                for j in range(4):
                    pemm(py, py.ap[:, j * 64:(j + 1) * 64], Rt.ap[:, j, :], STc.ap[:, j, :], [Rt, STc], True, False)
                    pemm(py, py.ap[:, j * 64:(j + 1) * 64], Zm.ap[:, j, :], VT.ap[:, j, :], [Zm, VT], False, True)
                pst = nb()
                for j in range(4):
                    pemm(pst, pst.ap[0:64, j * 64:(j + 1) * 64], Gs.ap[:, j, :], STc.ap[:, j, :], [Gs, STc], True, False)
                    pemm(pst, pst.ap[0:64, j * 64:(j + 1) * 64], WT.ap[:, j, :], VT.ap[:, j, :], [WT, VT], False, True)
                act(lambda e: e.activation(out=fl(STn), in_=pst.ap[0:64, 0:256], func=AF.Copy), [pst], [STn])
                act(lambda e: e.activation(out=fl(Yk), in_=py.ap[:, 0:256], func=AF.Copy), [py], [Yk])
                yield
                dve(lambda e: e.tensor_reduce(out=sm.ap, in_=Yk.ap, axis=AX.X, op=ALU.add), [Yk], [sm])
                dve(lambda e: e.tensor_scalar(out=sm.ap, in0=sm.ap, scalar1=-1.0 / 64, scalar2=None, op0=ALU.mult), [sm], [sm])
                tt_(Yd.ap, Yk.ap, sm.ap.unsqueeze(2).broadcast_to([128, 4, 64]), ALU.add, [Yk, sm], [Yd])
                tt_(Ysq.ap, Yd.ap, Yd.ap, ALU.mult, [Yd], [Ysq])
                dve(lambda e: e.tensor_reduce(out=sv.ap, in_=Ysq.ap, axis=AX.X, op=ALU.add), [Ysq], [sv])
                act(lambda e: e.activation(out=sv.ap, in_=sv.ap, func=AF.Sqrt, bias=gneps.ap, scale=1.0 / 64), [sv, gneps], [sv])
                dve(lambda e: e.reciprocal(out=sv.ap, in_=sv.ap), [sv], [sv])
                tt_(Yd.ap, Yd.ap, sv.ap.unsqueeze(2).broadcast_to([128, 4, 64]), ALU.mult, [Yd, sv], [Yd])
                po = nb()
                for j in range(4):
                    S.op("pe", lambda e, j=j: e.transpose(out=po.ap[0:64, j * 128:(j + 1) * 128], in_=Yd.ap[:, j, :],
                                                           identity=ident.ap), reads=[Yd.key, ident.key], writes=[po.key])
                tt_(OF.ap, v3(po), bc_h(vh(5), 128), ALU.mult, [po, vec], [OF])
                tt_(OF.ap, OF.ap, bc_h(vh(6), 128), ALU.add, [OF, vec], [OF])
                tt_(OF.ap, OF.ap, BON.ap, ALU.add, [OF, BON], [OF])
                tt_(OB.ap, OF.ap, GT.ap, ALU.mult, [OF, GT], [OB])
                S.dma("sp", self.YT.ap[h0 * 64:(h0 + 4) * 64, t0:t0 + 128].rearrange("(h c) t -> c h t", c=64), OB.ap,
                      reads=[OB.key], writes=[self.YT.key])
                yield

        def run_gens(gs):
            gs = list(gs)
            while gs:
                for g_ in list(gs):
                    try:
                        next(g_)
                    except StopIteration:
                        gs.remove(g_)
        run_gens([chain(0, sets[0], 0), chain(1, sets[1], 4)])
        run_gens([chain(2, sets[0], 0)])

    def nsa(self, l):
        nc, S = self.nc, self.S
        T = self.T
        NQT = T // 128
        NC = T // 16 - 1
        NT = (NC + 127) // 128
        ident = self.ident
        rot = {"i": 0, "a": 0}

        def psr():
            b = S.banks[rot["i"] % 5]
            rot["i"] += 1
            return b

        def psa():
            b = S.banks[5 + rot["a"] % 3]
            rot["a"] += 1
            return b

        def W(name, shape, dt=F32):
            return sb(nc, "ns%d_" % l + name, shape, dt)

        def dve(fn, reads, writes):
            S.op("dve", fn, reads=[b.key for b in reads], writes=[b.key for b in writes])

        def act(fn, reads, writes):
            S.op("act", fn, reads=[b.key for b in reads], writes=[b.key for b in writes])

        def pemm(ps, sl, lhsT, rhs, reads, start=True, stop=True):
            S.op("pe", lambda e: e.matmul(sl, lhsT=lhsT, rhs=rhs, start=start, stop=stop),
                 reads=[b.key for b in reads], writes=[ps.key])

        KCMPT = W("KCMPT", [64, 3, 256], BF16)
        VCMP = W("VCMP", [128, 2, 3, 65], BF16)
        OVL = W("OVL", [128, 2, 65], BF16)
        KST = W("KST", [64, 3, T], BF16)
        KWT = W("KWT", [64, 3, T], BF16)
        VS = W("VS", [128, NQT, 3, 65], BF16)
        VW = W("VW", [128, NQT, 3, 65], BF16)
        EXPD = W("EXPD", [64, 32, 128], BF16)
        identb = W("identb", [128, 128], BF16)
        CAUS = W("CAUS", [128, 4, 128], BF16)
        BAND = W("BAND", [128, 4, 128], BF16)
        S.op("dve", lambda e: e.memset(KCMPT.ap, 0.0), writes=[KCMPT.key])
        S.op("dve", lambda e: e.memset(VCMP.ap, 0.0), writes=[VCMP.key])
        S.op("dve", lambda e: e.memset(VCMP.ap[:, :, :, 64:65], 1.0), writes=[VCMP.key])
        S.op("dve", lambda e: e.memset(VS.ap[:, :, :, 64:65], 1.0), writes=[VS.key])
        S.op("dve", lambda e: e.memset(VW.ap[:, :, :, 64:65], 1.0), writes=[VW.key])
        S.op("dve", lambda e: e.tensor_copy(out=identb.ap, in_=ident.ap), reads=[ident.key], writes=[identb.key])
        S.dma("pool", OVL.ap, self.c_ovl.ap.rearrange("a p c -> p a c"), reads=[self.c_ovl.key], writes=[OVL.key])
        S.dma("pool", EXPD.ap, self.c_expand.ap, reads=[self.c_expand.key], writes=[EXPD.key])
        S.dma("pool", CAUS.ap, self.c_caus.ap[0].unsqueeze(1).broadcast_to([128, 4, 128]), reads=[self.c_caus.key],
              writes=[CAUS.key])
        S.dma("pool", BAND.ap, self.c_caus.ap[1].unsqueeze(1).broadcast_to([128, 4, 128]), reads=[self.c_caus.key],
              writes=[BAND.key])
        S.dma("sp", KST.ap, self.QKT.ap[768:960, 0:T].rearrange("(h c) t -> c h t", c=64), reads=[self.QKT.key],
              writes=[KST.key])
        S.dma("sp", KWT.ap, self.QKT.ap[960:1152, 0:T].rearrange("(h c) t -> c h t", c=64), reads=[self.QKT.key],
              writes=[KWT.key])
        for k0 in range(0, NQT, 4):
            for h in range(3):
                S.dma("sp", VS.ap[:, k0:k0 + 4, h, 0:64],
                      self.VSW.ap[k0 * 128:(k0 + 4) * 128, h * 64:(h + 1) * 64].rearrange("(k p) c -> p k c", p=128),
                      reads=[self.VSW.key], writes=[VS.key])
                S.dma("sp", VW.ap[:, k0:k0 + 4, h, 0:64],
                      self.VSW.ap[k0 * 128:(k0 + 4) * 128, 192 + h * 64:192 + (h + 1) * 64].rearrange("(k p) c -> p k c", p=128),
                      reads=[self.VSW.key], writes=[VW.key])

        w1 = W("w1", [64, 32, 256]); w2c = W("w2c", [128, 2, 64]); pe = W("pe", [64, 32])
        biasc = W("biasc", [128, 2]); XC = W("XC", [64, T]); Hc = W("Hc", [128, 2, 256])
        hx = W("hx", [128, 256]); hu = W("hu", [128, 256])
        KCt = W("KCt", [128, 2, 3, 64]); CSC = W("CSC", [128, 2, 16]); rtc = W("rtc", [128, 4, 6, 8])
        S.op("dve", lambda e: e.memset(KCt.ap, 0.0), writes=[KCt.key])
        S.dma("sp", CSC.ap, self.c_ropec.ap.rearrange("(a p) c -> p a c", p=128), reads=[self.c_ropec.key], writes=[CSC.key])
        XC3 = XC.ap.rearrange("p (n b) -> p n b", b=16)
        for kv in range(2):
            S.dma("sp", w1.ap, self.cw1.ap[l, kv].rearrange("d (a f) -> d a f", f=256), reads=[self.cw1.key], writes=[w1.key])
            S.dma("sp", w2c.ap, self.cw2.ap[l, kv], reads=[self.cw2.key], writes=[w2c.key])
            S.dma("sp", pe.ap, self.cpe.ap[l, kv], reads=[self.cpe.key], writes=[pe.key])
            for fh in range(2):
                ps = psr()
                for l_ in range(32):
                    pemm(ps, ps.ap[:, 0:1], w1.ap[:, l_, fh * 128:(fh + 1) * 128], pe.ap[:, l_:l_ + 1], [w1, pe], l_ == 0, l_ == 31)
                dve(lambda e: e.tensor_copy(out=biasc.ap[:, fh:fh + 1], in_=ps.ap[:, 0:1]), [ps], [biasc])
            for h in range(3):
                r0 = 3072 + kv * 192 + h * 64
                S.dma("sp", XC.ap, self.FMT.ap[r0:r0 + 64, 0:T], reads=[self.FMT.key], writes=[XC.key])
                for fh in range(2):
                    ps = psr()
                    for l_ in range(32):
                        a_, b_ = l_ // 16, l_ % 16
                        pemm(ps, ps.ap[:, 0:NC], w1.ap[:, l_, fh * 128:(fh + 1) * 128], XC3[:, a_:a_ + NC, b_], [w1, XC],
                             l_ == 0, l_ == 31)
                    x_, u_ = hx.ap[:, 0:NC], hu.ap[:, 0:NC]
                    dve(lambda e: e.tensor_scalar(out=x_, in0=ps.ap[:, 0:NC], scalar1=biasc.ap[:, fh:fh + 1], scalar2=None,
                                                  op0=ALU.add), [ps, biasc], [hx])
                    dve(lambda e: e.tensor_tensor(out=u_, in0=x_, in1=x_, op=ALU.mult), [hx], [hu])
                    dve(lambda e: e.tensor_scalar(out=u_, in0=u_, scalar1=0.044715, scalar2=1.0, op0=ALU.mult, op1=ALU.add),
                        [hu], [hu])
                    dve(lambda e: e.tensor_tensor(out=u_, in0=u_, in1=x_, op=ALU.mult), [hu, hx], [hu])
                    act(lambda e: e.activation(out=u_, in_=u_, func=AF.Tanh, scale=0.7978845608028654), [hu], [hu])
                    dve(lambda e: e.scalar_tensor_tensor(out=u_, in0=u_, scalar=1.0, in1=x_, op0=ALU.add, op1=ALU.mult),
                        [hu, hx], [hu])
                    dve(lambda e: e.tensor_scalar(out=Hc.ap[:, fh, 0:NC], in0=u_, scalar1=0.5, scalar2=None, op0=ALU.mult),
                        [hu], [Hc])
                for nt in range(NT):
                    n0 = nt * 128
                    nn = min(128, NC - n0)
                    ps = psr()
                    for fh in range(2):
                        pemm(ps, ps.ap[0:nn, 0:64], Hc.ap[:, fh, n0:n0 + nn], w2c.ap[:, fh, :], [Hc, w2c], fh == 0, fh == 1)
                    if kv == 1:
                        act(lambda e: e.activation(out=VCMP.ap[0:nn, nt, h, 0:64], in_=ps.ap[0:nn, 0:64], func=AF.Copy),
                            [ps], [VCMP])
                    else:
                        act(lambda e: e.activation(out=KCt.ap[0:nn, nt, h, :], in_=ps.ap[0:nn, 0:64], func=AF.Copy),
                            [ps], [KCt])
            if kv == 0:
                v = KCt.ap.rearrange("p a h c -> p (a h) c")
                x1, x2 = v[:, :, 0:8], v[:, :, 8:16]
                cosb = CSC.ap[:, :, 0:8].unsqueeze(2).broadcast_to([128, 2, 3, 8]).rearrange("p a h c -> p (a h) c") \
                    if False else None
                for nt in range(NT):
                    vv = KCt.ap[:, nt]
                    x1, x2 = vv[:, :, 0:8], vv[:, :, 8:16]
                    cb_ = CSC.ap[:, nt, 0:8].unsqueeze(1).broadcast_to([128, 3, 8])
                    sb_ = CSC.ap[:, nt, 8:16].unsqueeze(1).broadcast_to([128, 3, 8])
                    r_ = rtc.ap
                    dve(lambda e: e.tensor_tensor(out=r_[:, 0, 0:3], in0=x1, in1=cb_, op=ALU.mult), [KCt, CSC], [rtc])
                    dve(lambda e: e.tensor_tensor(out=r_[:, 1, 0:3], in0=x2, in1=sb_, op=ALU.mult), [KCt, CSC], [rtc])
                    dve(lambda e: e.tensor_tensor(out=r_[:, 2, 0:3], in0=x2, in1=cb_, op=ALU.mult), [KCt, CSC], [rtc])
                    dve(lambda e: e.tensor_tensor(out=r_[:, 3, 0:3], in0=x1, in1=sb_, op=ALU.mult), [KCt, CSC], [rtc])
                    dve(lambda e: e.tensor_tensor(out=x1, in0=r_[:, 0, 0:3], in1=r_[:, 1, 0:3], op=ALU.subtract), [rtc], [KCt])
                    dve(lambda e: e.tensor_tensor(out=x2, in0=r_[:, 2, 0:3], in1=r_[:, 3, 0:3], op=ALU.add), [rtc], [KCt])
                    for h in range(3):
                        ps = psr()
                        S.op("pe", lambda e: e.transpose(out=ps.ap[0:64, 0:128], in_=KCt.ap[:, nt, h, :], identity=ident.ap),
                             reads=[KCt.key, ident.key], writes=[ps.key])
                        act(lambda e: e.activation(out=KCMPT.ap[:, h, nt * 128:(nt + 1) * 128], in_=ps.ap[0:64, 0:128],
                                                   func=AF.Copy), [ps], [KCMPT])

        Gt = W("Gt", [128, 36]); FORCE = W("FORCE", [128, 64]); VALID = W("VALID", [128, 64])
        CM = W("CM", [128, 2, 4, 128], BF16)
        QTt = Pool(nc, "ns%dQT" % l, [64, 4, 128], BF16, 2)
        expool = Pool(nc, "ns%dex" % l, [128, 512], BF16, 4)
        rz = W("rz", [128, 4]); coef = W("coef", [128, 4]); imp = W("imp", [128, 64]); sc2 = W("sc2", [128, 64])
        m16 = W("m16", [128, 16]); negm = W("negm", [128, 64]); NEGT = W("NEGT", [64, 4, 128], BF16)
        YN = W("YN", [128, 4, 64]); ytmp = W("ytmp", [128, 4, 64]); OBn = W("OBn", [64, 4, 128], BF16)
        fl2 = lambda b: b.ap.rearrange("p a b -> p (a b)")

        def combine(bank, h, br, first):
            bv = bank.ap[:, 0:260].rearrange("p (g c) -> p g c", g=4)
            dve(lambda e: e.tensor_scalar(out=rz.ap, in0=bv[:, :, 64], scalar1=1e-30, scalar2=None, op0=ALU.max), [bank], [rz])
            dve(lambda e: e.reciprocal(out=rz.ap, in_=rz.ap), [rz], [rz])
            gv = Gt.ap[:, h * 12:(h + 1) * 12].rearrange("p (g b) -> p g b", b=3)[:, :, br]
            dve(lambda e: e.tensor_tensor(out=coef.ap, in0=rz.ap, in1=gv, op=ALU.mult), [rz, Gt], [coef])
            cb_ = coef.ap.unsqueeze(2).broadcast_to([128, 4, 64])
            if first:
                dve(lambda e: e.tensor_tensor(out=YN.ap, in0=bv[:, :, 0:64], in1=cb_, op=ALU.mult), [bank, coef], [YN])
            else:
                dve(lambda e: e.tensor_tensor(out=ytmp.ap, in0=bv[:, :, 0:64], in1=cb_, op=ALU.mult), [bank, coef], [ytmp])
                dve(lambda e: e.tensor_tensor(out=YN.ap, in0=YN.ap, in1=ytmp.ap, op=ALU.add), [YN, ytmp], [YN])

        for qt in range(NQT):
            q0 = qt * 128
            S.dma("sp", Gt.ap, self.GL.ap[q0:q0 + 128, :], reads=[self.GL.key], writes=[Gt.key])
            S.dma("sp", FORCE.ap, self.c_force.ap[qt], reads=[self.c_force.key], writes=[FORCE.key])
            S.dma("sp", VALID.ap, self.c_valid.ap[qt], reads=[self.c_valid.key], writes=[VALID.key])
            nts = [nt for nt in range(NT) if 16 * (nt * 128) + 31 <= q0 + 127]
            for nt in nts:
                S.dma("pool", CM.ap[:, nt], self.c_cmask.ap[qt, nt].unsqueeze(1).broadcast_to([128, 4, 128]),
                      reads=[self.c_cmask.key], writes=[CM.key])
            for h in range(3):
                qb = QTt.next()
                S.dma("sp", qb.ap, self.QKT.ap[h * 256:(h + 1) * 256, q0:q0 + 128].rearrange("(g c) t -> c g t", c=64),
                      reads=[self.QKT.key], writes=[qb.key])
                q2 = fl2(qb)
                po, pi = psa(), psa()
                for ii, nt in enumerate(nts):
                    ps = psr()
                    pemm(ps, ps.ap, KCMPT.ap[:, h, nt * 128:(nt + 1) * 128], q2, [KCMPT, qb], True, False)
                    pemm(ps, ps.ap, identb.ap, CM.ap[:, nt].rearrange("p a b -> p (a b)"), [identb, CM], False, True)
                    ex = expool.next()
                    act(lambda e: e.activation(out=ex.ap, in_=ps.ap, func=AF.Exp), [ps], [ex])
                    for g in range(4):
                        pemm(po, po.ap[:, g * 65:(g + 1) * 65], ex.ap[:, g * 128:(g + 1) * 128], VCMP.ap[:, nt, h, :], [ex, VCMP],
                             ii == 0 and g == 0, ii == len(nts) - 1 and g == 3)
                    for g in range(4):
                        pemm(pi, pi.ap[:, g * 65:(g + 1) * 65], ex.ap[:, g * 128:(g + 1) * 128], OVL.ap[:, nt, :], [ex, OVL],
                             ii == 0 and g == 0, ii == len(nts) - 1 and g == 3)
                piv = pi.ap[:, 0:260].rearrange("p (g c) -> p g c", g=4)
                dve(lambda e: e.tensor_scalar(out=rz.ap, in0=piv[:, :, 64], scalar1=1e-30, scalar2=None, op0=ALU.max), [pi], [rz])
                dve(lambda e: e.reciprocal(out=rz.ap, in_=rz.ap), [rz], [rz])
                dve(lambda e: e.tensor_scalar(out=imp.ap, in0=piv[:, 0, 0:64], scalar1=rz.ap[:, 0:1], scalar2=None, op0=ALU.mult),
                    [pi, rz], [imp])
                for g in range(1, 4):
                    dve(lambda e: e.scalar_tensor_tensor(out=imp.ap, in0=piv[:, g, 0:64], scalar=rz.ap[:, g:g + 1], in1=imp.ap,
                                                         op0=ALU.mult, op1=ALU.add), [pi, rz, imp], [imp])
                dve(lambda e: e.tensor_tensor(out=imp.ap, in0=imp.ap, in1=FORCE.ap, op=ALU.max), [imp, FORCE], [imp])
                dve(lambda e: e.tensor_tensor(out=imp.ap, in0=imp.ap, in1=VALID.ap, op=ALU.add), [imp, VALID], [imp])
                dve(lambda e: e.max(out=m16.ap[:, 0:8], in_=imp.ap), [imp], [m16])
                dve(lambda e: e.match_replace(out=sc2.ap, in_to_replace=m16.ap[:, 0:8], in_values=imp.ap, imm_value=-2e30),
                    [imp, m16], [sc2])
                dve(lambda e: e.max(out=m16.ap[:, 8:16], in_=sc2.ap), [sc2], [m16])
                dve(lambda e: e.tensor_scalar(out=negm.ap, in0=imp.ap, scalar1=m16.ap[:, 15:16], scalar2=None, op0=ALU.is_ge),
                    [imp, m16], [negm])
                dve(lambda e: e.tensor_scalar(out=negm.ap, in0=negm.ap, scalar1=-1.0, scalar2=-NEG, op0=ALU.add, op1=ALU.mult),
                    [negm], [negm])
                pT = psr()
                S.op("pe", lambda e: e.transpose(out=pT.ap[0:64, 0:128], in_=negm.ap, identity=ident.ap),
                     reads=[negm.key, ident.key], writes=[pT.key])
                dve(lambda e: e.tensor_copy(out=NEGT.ap, in_=pT.ap[0:64, 0:128].unsqueeze(1).broadcast_to([64, 4, 128])),
                    [pT], [NEGT])
                combine(po, h, 0, True)
                pss = psa()
                for kt in range(qt + 1):
                    ps = psr()
                    pemm(ps, ps.ap, KST.ap[:, h, kt * 128:(kt + 1) * 128], q2, [KST, qb], True, False)
                    pemm(ps, ps.ap, EXPD.ap[:, kt, :], fl2(NEGT), [EXPD, NEGT], False, kt != qt)
                    if kt == qt:
                        pemm(ps, ps.ap, identb.ap, fl2(CAUS), [identb, CAUS], False, True)
                    ex = expool.next()
                    act(lambda e: e.activation(out=ex.ap, in_=ps.ap, func=AF.Exp), [ps], [ex])
                    for g in range(4):
                        pemm(pss, pss.ap[:, g * 65:(g + 1) * 65], ex.ap[:, g * 128:(g + 1) * 128], VS.ap[:, kt, h, :], [ex, VS],
                             kt == 0 and g == 0, kt == qt and g == 3)
                combine(pss, h, 1, False)
                psw = psa()
                kts = list(range(max(0, qt - 4), qt + 1))
                for kt in kts:
                    ps = psr()
                    masked = (kt == qt) or (kt == qt - 4)
                    pemm(ps, ps.ap, KWT.ap[:, h, kt * 128:(kt + 1) * 128], q2, [KWT, qb], True, not masked)
                    if kt == qt:
                        pemm(ps, ps.ap, identb.ap, fl2(CAUS), [identb, CAUS], False, True)
                    elif kt == qt - 4:
                        pemm(ps, ps.ap, identb.ap, fl2(BAND), [identb, BAND], False, True)
                    ex = expool.next()
                    act(lambda e: e.activation(out=ex.ap, in_=ps.ap, func=AF.Exp), [ps], [ex])
                    for g in range(4):
                        pemm(psw, psw.ap[:, g * 65:(g + 1) * 65], ex.ap[:, g * 128:(g + 1) * 128], VW.ap[:, kt, h, :], [ex, VW],
                             kt == kts[0] and g == 0, kt == qt and g == 3)
                combine(psw, h, 2, False)
                pso = psr()
                for g in range(4):
                    S.op("pe", lambda e, g=g: e.transpose(out=pso.ap[0:64, g * 128:(g + 1) * 128], in_=YN.ap[:, g, :],
                                                           identity=ident.ap), reads=[YN.key, ident.key], writes=[pso.key])
                act(lambda e: e.activation(out=fl2(OBn), in_=pso.ap[0:64, :], func=AF.Copy), [pso], [OBn])
                S.dma("sp", self.YT.ap[1280 + h * 256:1280 + (h + 1) * 256, q0:q0 + 128].rearrange("(g c) t -> c g t", c=64),
                      OBn.ap, reads=[OBn.key], writes=[self.YT.key])

    def loop3(self, l, dst):
        S = self.S
        xtok, xT, hT = self.xtok, self.xT, self.hT
        for tb in range(self.NTB):
            t0 = tb * 512
            for tt in range(4):
                S.dma("sp", xtok.ap[:, tt, :], self.X1.ap[t0 + tt * 128:t0 + (tt + 1) * 128, :], reads=[self.X1.key],
                      writes=[self.kx[tt]])
            S.dma("sp", xT.ap, self.YT.ap[:, t0:t0 + 512].rearrange("(k p) t -> p k t", p=128), reads=[self.YT.key],
                  writes=[xT.key])
            for tt in range(4):
                S.op("pool", lambda e, tt=tt: e.tensor_scalar_mul(out=xtok.ap[:, tt, :], in0=xtok.ap[:, tt, :], scalar1=ALPHA),
                     reads=[self.kx[tt]], writes=[self.kx[tt]])

            def evac(tt, cb, ps, nco):
                S.op("dve", lambda e: e.tensor_tensor(out=xtok.ap[:, tt, cb * 512:(cb + 1) * 512], in0=ps.ap,
                                                      in1=xtok.ap[:, tt, cb * 512:(cb + 1) * 512], op=ALU.add),
                     reads=[ps.key, self.kx[tt]], writes=[self.kx[tt]])
            self.tokmm(xT, 16, self.woutB, 4, 4, evac)
            for tt in range(4):
                self.layernorm(xtok, tt, 1)
            self.make_xT(xtok, xT, 0)
            self.ffn_ln(l, 1, xtok, xT, hT)
            for tt in range(4):
                S.dma("sp", dst.ap[t0 + tt * 128:t0 + (tt + 1) * 128, :], xtok.ap[:, tt, :], reads=[self.kx[tt]],
                      writes=[dst.key])

    def poolmix(self, l):
        nc, S = self.nc, self.S
        T = self.T
        PP = sb(nc, "pPP%d" % l, [128, 16 + T], F32)
        A = sb(nc, "pA%d" % l, [128, 16 + T], F32)
        Bq = sb(nc, "pB%d" % l, [128, 16 + T], F32)
        RC = sb(nc, "pRC%d" % l, [128, T], F32)
        PW = sb(nc, "pPW%d" % l, [128, 4, 128], F32)
        PV = sb(nc, "pPV%d" % l, [128, 8], F32)
        ob = Pool(nc, "pob%d_" % l, [128, 512], BF16, 2)
        S.dma("sp", PW.ap, self.poolw.ap[l].rearrange("g c d -> c g d"), reads=[self.poolw.key], writes=[PW.key])
        S.dma("sp", PV.ap, self.poolv.ap[l], reads=[self.poolv.key], writes=[PV.key])
        for b_ in (PP, A, Bq):
            S.op("dve", lambda e: e.memset(b_.ap[:, 0:16], 0.0), writes=[b_.key])
        for gi, win in enumerate((2, 4, 8, 16)):
            S.dma("sp", PP.ap[:, 16:16 + T], self.FMT.ap[2560 + gi * 128:2560 + (gi + 1) * 128, 0:T], reads=[self.FMT.key],
                  writes=[PP.key])
            S.dma("sp", RC.ap, self.c_rcnt.ap[gi, 0:T].partition_broadcast(128), reads=[self.c_rcnt.key], writes=[RC.key])
            cur, nxt = PP, A
            sh = 1
            while sh < win:
                S.op("dve", lambda e: e.tensor_tensor(out=nxt.ap[:, 16:16 + T], in0=cur.ap[:, 16:16 + T],
                                                      in1=cur.ap[:, 16 - sh:16 - sh + T], op=ALU.add),
                     reads=[cur.key], writes=[nxt.key])
                cur = nxt
                nxt = Bq if cur is A else A
                sh *= 2
            S.op("dve", lambda e: e.tensor_tensor(out=nxt.ap[:, 16:16 + T], in0=cur.ap[:, 16:16 + T], in1=RC.ap, op=ALU.mult),
                 reads=[cur.key, RC.key], writes=[nxt.key])
            z = nxt
            S.op("dve", lambda e: e.tensor_tensor(out=z.ap[:, 16:16 + T], in0=z.ap[:, 16:16 + T], in1=PP.ap[:, 16:16 + T],
                                                  op=ALU.subtract), reads=[z.key, PP.key], writes=[z.key])
            for tb in range(T // 512):
                ps = S.psum()
                S.op("pe", lambda e: e.matmul(ps.ap, lhsT=PW.ap[:, gi, :], rhs=z.ap[:, 16 + tb * 512:16 + (tb + 1) * 512],
                                              start=True, stop=True), reads=[PW.key, z.key], writes=[ps.key])
                o = ob.next()
                S.op("dve", lambda e: e.tensor_scalar(out=o.ap, in0=ps.ap, scalar1=PV.ap[:, gi:gi + 1],
                                                      scalar2=PV.ap[:, 4 + gi:5 + gi], op0=ALU.add, op1=ALU.mult),
                     reads=[ps.key, PV.key], writes=[o.key])
                S.dma("sp", self.YT.ap[768 + gi * 128:768 + (gi + 1) * 128, tb * 512:(tb + 1) * 512], o.ap, reads=[o.key],
                      writes=[self.YT.key])

def wdn_view(wdn, l):
    return Buf(wdn.ap[l], wdn.key)


def tok_layout(w, G):
    K, N = w.shape
    nk = (K + 127) // 128
    ng = (nk + G - 1) // G
    ncb = (N + 511) // 512
    wp = np.zeros((ng * G * 128, ncb * 512), np.float32)
    wp[:K, :N] = w
    wp = wp.reshape(ng, G, 128, ncb, 512).transpose(3, 0, 2, 1, 4)
    return np.ascontiguousarray(wp).reshape(ncb, ng, 128, G * 512)


def host_layout(inp, nlayer):
    L = nlayer
    o = {}
    for i, nm in ((1, "ffn1"), (2, "ffn2")):
        wu = inp[nm + "_w_up"][:L]
        wu = wu.reshape(L, 16, 128, 2, NF, 128).transpose(0, 4, 2, 3, 1, 5)
        o["wup%d" % i] = np.ascontiguousarray(wu).reshape(L, NF, 128, 4096)
        o["wdn%d" % i] = np.stack([tok_layout(inp[nm + "_w_down"][l], 4) for l in range(L)])
    lns = {1: (inp["ln1_g"], inp["ln1_b"]), 2: (inp["ln2_g"], inp["ln2_b"]), 3: (inp["ln3_g"], inp["ln3_b"])}
    for i in (1, 2, 3):
        o["ln%dg" % i] = np.ascontiguousarray(lns[i][0][:L])
        o["ln%db" % i] = np.ascontiguousarray(lns[i][1][:L])
    win = inp["w_in"][:L]
    nb = 3072
    fm_cols = np.concatenate([np.arange(0, 3072), nb + np.arange(768, 1152)])
    tm_cols = np.concatenate([nb + np.arange(0, 768), nb + np.arange(1152, 1344), nb + np.arange(1536, 1728),
                              nb + np.arange(1344, 1536), nb + np.arange(1728, 1920), nb + np.arange(1920, 1956)])
    wfm = win[:, :, fm_cols].reshape(L, 16, 128, NFM, 128).transpose(0, 3, 2, 1, 4)
    o["winfm"] = np.ascontiguousarray(wfm).reshape(L, NFM, 128, 2048)
    o["wintm"] = np.stack([tok_layout(win[l][:, tm_cols], 4) for l in range(L)])
    o["wout"] = np.stack([tok_layout(inp["w_out"][l], 4) for l in range(L)])
    o["rwmu"] = np.ascontiguousarray(inp["rw_mu"][:L].reshape(L, 20, 128).transpose(0, 2, 1))
    o["gateb"] = np.ascontiguousarray(inp["nsa_gate_b"][:L])
    o["rw_w2"] = np.ascontiguousarray(inp["rw_w2"][:L])
    o["rw_a2"] = np.ascontiguousarray(inp["rw_a2"][:L])
    o["rw_g2"] = np.ascontiguousarray(inp["rw_g2"][:L])
    vecs = [inp["rw_w0"], inp["rw_a0"], inp["rw_k_k"], inp["rw_k_a"], inp["rw_r_k"].reshape(-1, 768), inp["rw_gn_g"],
            inp["rw_gn_b"]]
    o["rwvec"] = np.ascontiguousarray(np.stack([v[:L].reshape(L, 12, 64) for v in vecs], axis=1).transpose(0, 3, 1, 2))
    o["poolw"] = np.ascontiguousarray(inp["pool_w"][:L])
    w1s = np.stack([inp["nsa_cmp_k_w1"][:L], inp["nsa_cmp_v_w1"][:L]], axis=1)
    o["cw1"] = np.ascontiguousarray(w1s.transpose(0, 1, 3, 2, 4)).reshape(L, 2, 64, 32 * 256)
    w2s = np.stack([inp["nsa_cmp_k_w2"][:L], inp["nsa_cmp_v_w2"][:L]], axis=1)
    o["cw2"] = np.ascontiguousarray(w2s.reshape(L, 2, 2, 128, 64).transpose(0, 1, 3, 2, 4))
    pes = np.stack([inp["nsa_cmp_pe_k"][:L], inp["nsa_cmp_pe_v"][:L]], axis=1)
    o["cpe"] = np.ascontiguousarray(pes.transpose(0, 1, 3, 2))
    pv = np.concatenate([inp["pool_b"][:L].reshape(L, 4, 128), inp["pool_scale"][:L].reshape(L, 4, 128)], axis=1)
    o["poolv"] = np.ascontiguousarray(pv.transpose(0, 2, 1))
    return o


def host_consts():
    c = {}
    inv_freq = (500000.0 ** (-np.arange(8, dtype=np.float32) / 8)).astype(np.float32)
    c["c_ident"] = np.eye(128, dtype=np.float32)
    ii = np.arange(128)
    c["c_masks"] = np.stack([(ii[:, None] < ii[None, :]), (ii[:, None] <= ii[None, :]), (ii[:, None] > ii[None, :])]).astype(np.float32)
    angc = (np.arange(256, dtype=np.float32) * 16 + 31)[:, None] * inv_freq
    c["c_ropec"] = np.concatenate([np.cos(angc), np.sin(angc)], axis=1).astype(np.float32)
    n_cmp = 255
    cmp_start = np.arange(256) * 16
    sel_start = np.arange(64) * 64
    ov = np.clip(np.minimum(cmp_start[:, None] + 32, sel_start[None, :] + 64) - np.maximum(cmp_start[:, None], sel_start[None, :]),
                 0, None) / 32.0
    ovl = np.concatenate([ov, np.ones((256, 1))], axis=1).astype(np.float32)
    ovl[255] = 0.0
    c["c_ovl"] = ovl.reshape(2, 128, 65)
    jj = np.arange(64)[:, None, None]
    c["c_expand"] = (jj == (2 * np.arange(32)[None, :, None] + np.arange(128)[None, None, :] // 64)).astype(np.float32)
    kk_, qq_ = np.arange(128)[:, None], np.arange(128)[None, :]
    c["c_caus"] = np.stack([np.where(kk_ > qq_, NEG, 0.0), np.where(kk_ <= qq_, NEG, 0.0)]).astype(np.float32)
    qabs = np.arange(SEQ).reshape(32, 128)
    cur = qabs // 64
    jb = np.arange(64)[None, None, :]
    forced = (jb == 0) | (jb == cur[:, :, None]) | (jb == cur[:, :, None] - 1)
    c["c_force"] = np.where(forced, 1e9, 0.0).astype(np.float32)
    c["c_valid"] = np.where(jb > cur[:, :, None], -1e30, 0.0).astype(np.float32)
    nabs = np.arange(256).reshape(2, 128)
    cend = nabs * 16 + 31
    ok = (cend[None, :, :, None] <= qabs[:, None, None, :]) & (nabs[None, :, :, None] < n_cmp)
    c["c_cmask"] = np.where(ok, 0.0, NEG).astype(np.float32)
    t1 = np.arange(1, SEQ + 1, dtype=np.float32)
    c["c_rcnt"] = np.stack([1.0 / np.minimum(t1, float(w)) for w in (2, 4, 8, 16)]).astype(np.float32)
    half = 8
    inv_freq = (500000.0 ** (-np.arange(half, dtype=np.float32) / half)).astype(np.float32)
    ang = np.arange(SEQ, dtype=np.float32)[:, None] * inv_freq
    c["c_rope"] = np.concatenate([np.cos(ang), np.sin(ang)], axis=1).astype(np.float32)
    return c


_CACHE = {}


def run(inputs, nlayer=NLAYER, trun=SEQ, debug=False, stop_after=None, trace=False):
    key = (nlayer, trun, debug, stop_after)
    if key not in _CACHE:
        _CACHE[key] = Prog(nlayer, trun, debug, stop_after)
    prog = _CACHE[key]
    shared = host_layout(inputs, nlayer)
    shared.update(host_consts())
    in_maps = []
    for c in range(8):
        m = dict(shared)
        m["x"] = np.ascontiguousarray(inputs["x"][c % 4])
        m = {k: v for k, v in m.items() if k in prog.din}
        in_maps.append(m)
    res = run_bass_kernel_spmd(prog.nc, in_maps, core_ids=list(range(8)), trace=trace)
    return prog, res


def kernel(**inputs):
    inputs = {k: np.asarray(v) for k, v in inputs.items()}
    prog, res = run(inputs)
    out = np.stack([res.results[b]["out"] for b in range(4)], axis=0)
    return out.astype(np.float32)
```

```python
from contextlib import ExitStack
import numpy as np
import ml_dtypes
import concourse.bass as bass
import concourse.mybir as mybir
from concourse.bass_utils import run_bass_kernel_spmd

F32 = mybir.dt.float32
BF16 = mybir.dt.bfloat16
AF = mybir.ActivationFunctionType
ALU = mybir.AluOpType
AX = mybir.AxisListType

D = 2048
SEQ = 4096
NLAYER = 4
DFF = 5504
NF = 43
ALPHA = float((2 * NLAYER) ** 0.25)
LN_EPS = 1e-5
RWC = 2560
NFM = 27
NTM = 1572
CDEC = float(np.exp(-0.5))
GN_EPS = 64e-5
NEG = -30000.0


class Key:
    __slots__ = ("name", "w", "r")

    def __init__(self, name=""):
        self.name = name
        self.w = None
        self.r = []


class Buf:
    def __init__(self, ap, key):
        self.ap = ap
        self.key = key


class Sched:
    EPOCH = 30000
    NDS = 40

    def __init__(self, nc):
        self.nc = nc
        self.engs = {"pe": nc.tensor, "act": nc.scalar, "dve": nc.vector, "pool": nc.gpsimd, "sp": nc.sync}
        self.cnt = {e: 0 for e in self.engs}
        self.sems = {e: [] for e in self.engs}
        self.known = {e: {} for e in self.engs}
        self.dsems = [nc.alloc_semaphore("dq%d" % i) for i in range(self.NDS)]
        self.dcum = [0] * self.NDS
        self.dnext = 0
        self.banks = []
        for i in range(8):
            t = nc.alloc_psum_tensor("psb%d" % i, [128, 512], F32).ap()
            self.banks.append(Buf(t, Key("ps%d" % i)))
        self.bnext = 0
        self.ninst = 0
        self.pooltok = []

    def psum(self):
        b = self.banks[self.bnext]
        self.bnext = (self.bnext + 1) % 8
        return b

    def _esem(self, e, n):
        i = n // self.EPOCH
        while len(self.sems[e]) <= i:
            self.sems[e].append(self.nc.alloc_semaphore("s_%s_%d" % (e, len(self.sems[e]))))
        return self.sems[e][i], n % self.EPOCH + 1

    def _wait(self, e, tok):
        if tok[0] == "E":
            _, src, n = tok
            if self.known[e].get(("E", src), -1) >= n:
                return
            if src == e and e == "pe":
                return
            sem, val = self._esem(src, n)
            self.engs[e].wait_ge(sem, val)
            self.known[e][("E", src)] = n
        else:
            _, s, val = tok
            if self.known[e].get(("D", s), 0) >= val:
                return
            self.engs[e].wait_ge(self.dsems[s], val)
            self.known[e][("D", s)] = val
        self.ninst += 1

    def _deps(self, e, reads, writes, is_dma):
        deps = []
        for k in reads:
            if k.w is not None:
                deps.append(k.w)
        for k in writes:
            if k.w is not None:
                deps.append(k.w)
            for t in k.r:
                if (not is_dma) and t[0] == "E" and t[1] == e:
                    continue
                deps.append(t)
        for t in deps:
            self._wait(e, t)

    def _commit(self, e, tok, reads, writes):
        for k in reads:
            if tok[0] == "E":
                k.r = [t for t in k.r if not (t[0] == "E" and t[1] == tok[1])]
            k.r.append(tok)
        for k in writes:
            k.w = tok
            k.r = []

    def op(self, e, fn, reads=(), writes=()):
        self._deps(e, reads, writes, False)
        ins = fn(self.engs[e])
        n = self.cnt[e]
        self.cnt[e] += 1
        sem, _ = self._esem(e, n)
        ins.then_inc(sem, 1)
        self._commit(e, ("E", e, n), reads, writes)
        self.ninst += 1
        return ins

    def dma(self, q, out, in_, reads=(), writes=()):
        s = self.dnext
        self.dnext = (self.dnext + 1) % self.NDS
        if self.dcum[s] > 0:
            self._wait(q, ("D", s, self.dcum[s]))
        if q == "pool":
            if len(self.pooltok) >= 4:
                self._wait(q, self.pooltok[-4])
        self._deps(q, reads, writes, True)
        ins = self.engs[q].dma_start(out=out, in_=in_)
        self.dcum[s] += 16
        ins.then_inc(self.dsems[s], 16)
        self._commit(q, ("D", s, self.dcum[s]), reads, writes)
        if q == "pool":
            self.pooltok.append(("D", s, self.dcum[s]))
            self.pooltok = self.pooltok[-8:]
        self.ninst += 1
        return ins

    def barrier(self):
        for e in self.engs:
            for src in ("pe", "act", "dve", "pool"):
                if self.cnt[src] > 0:
                    n = self.cnt[src] - 1
                    if src == e and e == "pe":
                        continue
                    if self.known[e].get(("E", src), -1) < n:
                        sem, val = self._esem(src, n)
                        self.engs[e].wait_ge(sem, val)
                        self.known[e][("E", src)] = n
            for s in range(self.NDS):
                if self.dcum[s] > 0:
                    self._wait(e, ("D", s, self.dcum[s]))


class Pool:
    def __init__(self, nc, name, shape, dtype, n):
        self.bufs = [sb(nc, "%s%d" % (name, i), shape, dtype) for i in range(n)]
        self.i = 0

    def next(self):
        b = self.bufs[self.i]
        self.i = (self.i + 1) % len(self.bufs)
        return b


_STACK = [None]


_CNT = [0]


def sb(nc, name, shape, dtype):
    _CNT[0] += 1
    name = "%s_u%d" % (name, _CNT[0])
    h = _STACK[0].enter_context(nc.sbuf_tensor(name, list(shape), dtype))
    return Buf(h.ap(), Key(name))


class Prog:
    def __init__(self, nlayer=NLAYER, trun=SEQ, debug=False, stop_after=None):
        self.nlayer = nlayer
        self.T = trun
        self.NTB = trun // 512
        self.debug = debug
        self.stop_after = stop_after
        nc = bass.Bass("TRN2", target_bir_lowering=False)
        self.nc = nc
        self.S = Sched(nc)
        self.din = {}
        self._declare_io()
        self.build()

    def inp(self, name, shape, dtype=F32):
        t = nc_t = self.nc.dram_tensor(name, list(shape), dtype, kind="ExternalInput").ap()
        self.din[name] = (tuple(shape), dtype)
        return Buf(t, Key(name))

    def scratch(self, name, shape, dtype=F32):
        kind = "ExternalOutput" if self.debug else "Internal"
        t = self.nc.dram_tensor(name, list(shape), dtype, kind=kind).ap()
        return Buf(t, Key(name))

    def _declare_io(self):
        L = self.nlayer
        T = self.T
        self.x_in = self.inp("x", [SEQ, D])
        self.out = Buf(self.nc.dram_tensor("out", [SEQ, D], F32, kind="ExternalOutput").ap(), Key("out"))
        self.wup = [self.inp("wup%d" % i, [L, NF, 128, 4096]) for i in (1, 2)]
        self.wdn = [self.inp("wdn%d" % i, [L, 4, 11, 128, 2048]) for i in (1, 2)]
        self.lng = [self.inp("ln%dg" % i, [L, D]) for i in (1, 2, 3)]
        self.lnb = [self.inp("ln%db" % i, [L, D]) for i in (1, 2, 3)]
        self.winfm = self.inp("winfm", [L, NFM, 128, 2048])
        self.wintm = self.inp("wintm", [L, 4, 4, 128, 2048])
        self.wout = self.inp("wout", [L, 4, 4, 128, 2048])
        self.mu = self.inp("rwmu", [L, 128, 20])
        self.gateb = self.inp("gateb", [L, 36])
        self.rw_w2 = self.inp("rw_w2", [L, 64, 768])
        self.rw_a2 = self.inp("rw_a2", [L, 64, 768])
        self.rw_g2 = self.inp("rw_g2", [L, 128, 768])
        self.rwvec = self.inp("rwvec", [L, 64, 7, 12])
        self.poolw = self.inp("poolw", [L, 4, 128, 128])
        self.poolv = self.inp("poolv", [L, 128, 8])
        self.cw1 = self.inp("cw1", [L, 2, 64, 32 * 256])
        self.cw2 = self.inp("cw2", [L, 2, 128, 2, 64])
        self.cpe = self.inp("cpe", [L, 2, 64, 32])
        self.c_ropec = self.inp("c_ropec", [256, 16])
        self.c_ovl = self.inp("c_ovl", [2, 128, 65])
        self.c_expand = self.inp("c_expand", [64, 32, 128])
        self.c_caus = self.inp("c_caus", [2, 128, 128])
        self.c_force = self.inp("c_force", [32, 128, 64])
        self.c_valid = self.inp("c_valid", [32, 128, 64])
        self.c_cmask = self.inp("c_cmask", [32, 2, 128, 128])
        self.c_masks = self.inp("c_masks", [3, 128, 128])
        self.c_rcnt = self.inp("c_rcnt", [4, SEQ])
        self.c_ident = self.inp("c_ident", [128, 128])
        self.c_rope = self.inp("c_rope", [SEQ, 16])
        self.X1 = self.scratch("X1", [SEQ, D])
        self.XN = self.scratch("XN", [SEQ, D])
        self.FMT = self.scratch("FMT", [NFM * 128, SEQ])
        self.QKT = self.scratch("QKT", [1152, SEQ], BF16)
        self.VSW = self.scratch("VSW", [SEQ, 384], BF16)
        self.GL = self.scratch("GL", [SEQ, 36])
        self.YT = self.scratch("YT", [D, SEQ], BF16)

        def wscr(name, shape):
            t = self.nc.dram_tensor(name, list(shape), BF16, kind="Internal").ap()
            return t, [Key("%s_%d" % (name, i)) for i in range(shape[0])]
        self.wupB = [wscr("wupB%d" % i, [NF, 128, 4096]) for i in (1, 2)]
        self.wdnB = [wscr("wdnB%d" % i, [44, 128, 2048]) for i in (1, 2)]
        self.winfmB = wscr("winfmB", [NFM, 128, 2048])
        self.wintmB = wscr("wintmB", [16, 128, 2048])
        self.woutB = wscr("woutB", [16, 128, 2048])

    def _consts(self):
        nc, S = self.nc, self.S
        self.ident = sb(nc, "ident", [128, 128], F32)
        S.dma("sp", self.ident.ap, self.c_ident.ap, reads=[self.c_ident.key], writes=[self.ident.key])
        self.epsc = sb(nc, "epsc", [128, 1], F32)
        S.op("dve", lambda e: e.memset(self.epsc.ap, LN_EPS), writes=[self.epsc.key])

    def mm(self, ps, lhsT, rhs, start, stop, reads):
        self.S.op("pe", lambda e: e.matmul(ps.ap if isinstance(ps, Buf) else ps, lhsT=lhsT, rhs=rhs, start=start, stop=stop),
                  reads=reads, writes=[ps.key] if isinstance(ps, Buf) else [])

    def make_xT(self, xtok, xT, tog):
        S = self.S
        for dc in range(16):
            ps = S.psum()
            for tt in range(4):
                S.op("pe", lambda e, tt=tt: e.transpose(out=ps.ap[:, tt * 128:(tt + 1) * 128],
                                                         in_=xtok.ap[:, tt, dc * 128:(dc + 1) * 128],
                                                         identity=self.ident.ap),
                     reads=[self.kx[tt], self.ident.key], writes=[ps.key])
            if (dc + tog) % 2 == 0:
                S.op("act", lambda e: e.activation(out=xT.ap[:, dc, :], in_=ps.ap, func=AF.Copy),
                     reads=[ps.key], writes=[xT.key])
            else:
                S.op("dve", lambda e: e.tensor_copy(out=xT.ap[:, dc, :], in_=ps.ap), reads=[ps.key], writes=[xT.key])

    def tokmm(self, lhsT, nk, wsrc, ncb, G, evac, ncols=None):
        S = self.S
        ng = (nk + G - 1) // G
        for cb in range(ncb):
            nco = 512 if ncols is None else ncols[cb]
            banks = [S.psum() for _ in range(4)]
            for g in range(ng):
                w = self.wpool.next()
                wap, wkeys = wsrc
                S.dma(self.wq(), w.ap[:, 0:G * 512], wap[cb * ng + g], reads=[wkeys[cb * ng + g]], writes=[w.key])
                for tt in range(4):
                    for fi in range(G):
                        f = g * G + fi
                        if f >= nk:
                            continue
                        self.S.op("pe", lambda e, tt=tt, f=f, fi=fi: e.matmul(
                            banks[tt].ap[:, 0:nco], lhsT=lhsT.ap[:, f, tt * 128:(tt + 1) * 128],
                            rhs=w.ap[:, fi * 512:fi * 512 + nco], start=(f == 0), stop=(f == nk - 1)),
                            reads=[lhsT.key, w.key], writes=[banks[tt].key])
            for tt in range(4):
                evac(tt, cb, banks[tt], nco)

    def convert(self, l, which):
        S = self.S

        def cv(dstp, src_ap, src_key):
            dst, keys = dstp
            for i in range(len(keys)):
                S.dma("pool", dst[i], src_ap[i], reads=[src_key], writes=[keys[i]])
        if which == 0:
            cv(self.wupB[0], self.wup[0].ap[l], self.wup[0].key)
            cv(self.wdnB[0], self.wdn[0].ap[l].rearrange("a b p c -> (a b) p c"), self.wdn[0].key)
            cv(self.winfmB, self.winfm.ap[l], self.winfm.key)
            cv(self.wintmB, self.wintm.ap[l].rearrange("a b p c -> (a b) p c"), self.wintm.key)
        else:
            cv(self.woutB, self.wout.ap[l].rearrange("a b p c -> (a b) p c"), self.wout.key)
            cv(self.wupB[1], self.wup[1].ap[l], self.wup[1].key)
            cv(self.wdnB[1], self.wdn[1].ap[l].rearrange("a b p c -> (a b) p c"), self.wdn[1].key)

    def wq(self):
        self._wq = getattr(self, "_wq", 0) + 1
        return "pool"

    def layer_consts(self, l, first=True):
        nc, S = self.nc, self.S
        for i in ((0,) if first else (1, 2)):
            S.dma("sp", self.lnG[i].ap, self.lng[i].ap[l].partition_broadcast(128), reads=[self.lng[i].key],
                  writes=[self.lnG[i].key])
            S.dma("sp", self.lnB[i].ap, self.lnb[i].ap[l].partition_broadcast(128), reads=[self.lnb[i].key],
                  writes=[self.lnB[i].key])
        if not first:
            return
        S.dma("sp", self.MU.ap, self.mu.ap[l], reads=[self.mu.key], writes=[self.MU.key])
        S.dma("sp", self.GATEB.ap, self.gateb.ap[l].partition_broadcast(128), reads=[self.gateb.key],
              writes=[self.GATEB.key])

    def ffn_ln(self, l, which, xtok, xT, hT):
        S = self.S
        wupB, wdnB = self.wupB[which], self.wdnB[which]
        lni = 0 if which == 0 else 2
        for tt in range(4):
            S.op("act", lambda e, tt=tt: e.mul(xtok.ap[:, tt, :], xtok.ap[:, tt, :], ALPHA),
                 reads=[self.kx[tt]], writes=[self.kx[tt]])
        for f in range(NF):
            w = self.wpool.next()
            S.dma(self.wq(), w.ap, wupB[0][f], reads=[wupB[1][f]], writes=[w.key])
            pa, pb = S.psum(), S.psum()
            for kc in range(16):
                self.mm(pa, w.ap[:, kc * 128:(kc + 1) * 128], xT.ap[:, kc, :], kc == 0, kc == 15, [w.key, xT.key])
            for kc in range(16):
                self.mm(pb, w.ap[:, (16 + kc) * 128:(17 + kc) * 128], xT.ap[:, kc, :], kc == 0, kc == 15,
                        [w.key, xT.key])
            sa = self.sapool.next()
            S.op("act", lambda e: e.activation(out=sa.ap, in_=pa.ap, func=AF.Silu), reads=[pa.key], writes=[sa.key])
            S.op("dve", lambda e: e.tensor_tensor(out=hT.ap[:, f, :], in0=pb.ap, in1=sa.ap, op=ALU.mult),
                 reads=[pb.key, sa.key], writes=[hT.key])

        def evac(tt, cb, ps, nco):
            S.op("dve", lambda e: e.scalar_tensor_tensor(out=xtok.ap[:, tt, cb * 512:(cb + 1) * 512], in0=ps.ap,
                                                         scalar=0.5, in1=xtok.ap[:, tt, cb * 512:(cb + 1) * 512],
                                                         op0=ALU.mult, op1=ALU.add),
                 reads=[ps.key, self.kx[tt]], writes=[self.kx[tt]])
        self.tokmm(hT, NF, wdnB, 4, 4, evac)
        for tt in range(4):
            self.layernorm(xtok, tt, lni)

    def layernorm(self, xtok, tt, lni):
        S = self.S
        st, mv, rs = self.lnst, self.lnmv, self.lnrs
        for j in range(4):
            S.op("dve", lambda e, j=j: e.bn_stats(out=st.ap[:, j, :], in_=xtok.ap[:, tt, j * 512:(j + 1) * 512]),
                 reads=[self.kx[tt]], writes=[st.key])
        S.op("dve", lambda e: e.bn_aggr(out=mv.ap, in_=st.ap.rearrange("p a b -> p (a b)")), reads=[st.key], writes=[mv.key])
        S.op("act", lambda e: e.activation(out=rs.ap, in_=mv.ap[:, 1:2], func=AF.Sqrt, bias=self.epsc.ap, scale=1.0),
             reads=[mv.key, self.epsc.key], writes=[rs.key])
        S.op("dve", lambda e: e.reciprocal(out=rs.ap, in_=rs.ap), reads=[rs.key], writes=[rs.key])
        S.op("dve", lambda e: e.tensor_scalar(out=xtok.ap[:, tt, :], in0=xtok.ap[:, tt, :], scalar1=mv.ap[:, 0:1],
                                              scalar2=rs.ap[:, 0:1], op0=ALU.subtract, op1=ALU.mult),
             reads=[self.kx[tt], mv.key, rs.key], writes=[self.kx[tt]])
        S.op("dve", lambda e: e.tensor_tensor(out=xtok.ap[:, tt, :], in0=xtok.ap[:, tt, :], in1=self.lnG[lni].ap, op=ALU.mult),
             reads=[self.kx[tt], self.lnG[lni].key], writes=[self.kx[tt]])
        S.op("dve", lambda e: e.tensor_tensor(out=xtok.ap[:, tt, :], in0=xtok.ap[:, tt, :], in1=self.lnB[lni].ap, op=ALU.add),
             reads=[self.kx[tt], self.lnB[lni].key], writes=[self.kx[tt]])

    def alloc_loop(self):
        nc = self.nc
        self.xtok = sb(nc, "xtok", [128, 4, D], F32)
        self.kx = [Key("xtok%d" % i) for i in range(4)]
        self.xT = sb(nc, "xT", [128, 16, 512], BF16)
        self.hT = sb(nc, "hT", [128, NF, 512], BF16)
        self.wpool = Pool(nc, "wp", [128, 4096], BF16, 6)
        self.sapool = Pool(nc, "sa", [128, 512], F32, 2)
        self.lnst = sb(nc, "lnst", [128, 4, 6], F32)
        self.lnmv = sb(nc, "lnmv", [128, 2], F32)
        self.lnrs = sb(nc, "lnrs", [128, 1], F32)
        self.lnG = [sb(nc, "lnG%d" % i, [128, D], F32) for i in range(2)]
        self.lnB = [sb(nc, "lnB%d" % i, [128, D], F32) for i in range(2)]
        self.lnG.append(self.lnG[0])
        self.lnB.append(self.lnB[0])
        self.MU = sb(nc, "MU", [128, 20], F32)
        self.GATEB = sb(nc, "GATEB", [128, 36], F32)
        self.halo = sb(nc, "halo", [128, 20], F32)
        self.stage = Pool(nc, "stg", [128, 513], F32, 2)
        self.ost = Pool(nc, "ost", [128, 512], F32, 3)
        self.dtmp = sb(nc, "dtmp", [128, 512], F32)
        tmv = self.hT.ap.rearrange("p a b -> p (a b)").bitcast(F32)[:, 0:4 * NTM].rearrange("p (a c) -> p a c", a=4)
        self.TM = Buf(tmv, self.hT.key)
        self.CS = sb(nc, "CS", [128, 4, 16], F32)
        self.rtmp = sb(nc, "rtmp", [128, 4, 18, 8], F32)
        self.bst = Pool(nc, "bst", [128, 512], BF16, 3)
        self.vst = sb(nc, "vst", [128, 4, 384], BF16)
        self.gst = sb(nc, "gst", [128, 4, 36], F32)

    def loop1(self, l, src):
        S = self.S
        xtok, xT, hT = self.xtok, self.xT, self.hT
        S.op("dve", lambda e: e.memset(self.halo.ap, 0.0), writes=[self.halo.key])
        for tb in range(self.NTB):
            t0 = tb * 512
            for tt in range(4):
                S.dma("sp", xtok.ap[:, tt, :], src.ap[t0 + tt * 128:t0 + (tt + 1) * 128, :], reads=[src.key],
                      writes=[self.kx[tt]])
            S.dma("sp", self.CS.ap, self.c_rope.ap[t0:t0 + 512, :].rearrange("(a p) c -> p a c", p=128),
                  reads=[self.c_rope.key], writes=[self.CS.key])
            self.make_xT(xtok, xT, 0)
            self.ffn_ln(l, 0, xtok, xT, hT)
            for tt in range(4):
                S.dma("sp", self.X1.ap[t0 + tt * 128:t0 + (tt + 1) * 128, :], xtok.ap[:, tt, :], reads=[self.kx[tt]],
                      writes=[self.X1.key])
            self.make_xT(xtok, xT, 1)
            self.win_proj(l, t0, xT)

    def win_proj(self, l, t0, xT):
        S = self.S
        for ch in range(NFM):
            w = self.wpool.next()
            S.dma(self.wq(), w.ap[:, 0:2048], self.winfmB[0][ch], reads=[self.winfmB[1][ch]], writes=[w.key])
            ps = S.psum()
            for kc in range(16):
                self.mm(ps, w.ap[:, kc * 128:(kc + 1) * 128], xT.ap[:, kc, :], kc == 0, kc == 15, [w.key, xT.key])
            o = self.ost.next()
            if ch < 20:
                st = self.stage.next()
                d = self.dtmp
                S.op("act", lambda e: e.activation(out=st.ap[:, 1:513], in_=ps.ap, func=AF.Copy), reads=[ps.key],
                     writes=[st.key])
                S.op("dve", lambda e: e.tensor_copy(out=st.ap[:, 0:1], in_=self.halo.ap[:, ch:ch + 1]),
                     reads=[self.halo.key], writes=[st.key])
                S.op("dve", lambda e: e.tensor_sub(out=d.ap, in0=st.ap[:, 0:512], in1=st.ap[:, 1:513]), reads=[st.key],
                     writes=[d.key])
                S.op("dve", lambda e: e.tensor_copy(out=self.halo.ap[:, ch:ch + 1], in_=st.ap[:, 512:513]),
                     reads=[st.key], writes=[self.halo.key])
                S.op("dve", lambda e: e.scalar_tensor_tensor(out=o.ap, in0=d.ap, scalar=self.MU.ap[:, ch:ch + 1],
                                                             in1=st.ap[:, 1:513], op0=ALU.mult, op1=ALU.add),
                     reads=[d.key, st.key, self.MU.key], writes=[o.key])
            else:
                S.op("act", lambda e: e.activation(out=o.ap, in_=ps.ap, func=AF.Copy), reads=[ps.key], writes=[o.key])
            S.dma("sp", self.FMT.ap[ch * 128:(ch + 1) * 128, t0:t0 + 512], o.ap, reads=[o.key], writes=[self.FMT.key])
        TM = self.TM

        def evac(tt, cb, ps, nco):
            if (tt + cb) % 2 == 0:
                S.op("act", lambda e: e.activation(out=TM.ap[:, tt, cb * 512:cb * 512 + nco], in_=ps.ap[:, 0:nco], func=AF.Copy),
                     reads=[ps.key], writes=[TM.key])
            else:
                S.op("dve", lambda e: e.tensor_copy(out=TM.ap[:, tt, cb * 512:cb * 512 + nco], in_=ps.ap[:, 0:nco]),
                     reads=[ps.key], writes=[TM.key])
        self.tokmm(xT, 16, self.wintmB, 4, 4, evac, ncols=[512, 512, 512, 36])
        rt = self.rtmp
        for tt in range(4):
            v = TM.ap[:, tt, 0:1152].rearrange("p (h c) -> p h c", c=64)
            x1, x2 = v[:, :, 0:8], v[:, :, 8:16]
            cosb = self.CS.ap[:, tt, 0:8].unsqueeze(1).broadcast_to([128, 18, 8])
            sinb = self.CS.ap[:, tt, 8:16].unsqueeze(1).broadcast_to([128, 18, 8])
            rk = [TM.key, self.CS.key]
            S.op("dve", lambda e: e.tensor_tensor(out=rt.ap[:, 0], in0=x1, in1=cosb, op=ALU.mult), reads=rk, writes=[rt.key])
            S.op("dve", lambda e: e.tensor_tensor(out=rt.ap[:, 1], in0=x2, in1=sinb, op=ALU.mult), reads=rk, writes=[rt.key])
            S.op("dve", lambda e: e.tensor_tensor(out=rt.ap[:, 2], in0=x2, in1=cosb, op=ALU.mult), reads=rk, writes=[rt.key])
            S.op("dve", lambda e: e.tensor_tensor(out=rt.ap[:, 3], in0=x1, in1=sinb, op=ALU.mult), reads=rk, writes=[rt.key])
            S.op("dve", lambda e: e.tensor_tensor(out=x1, in0=rt.ap[:, 0], in1=rt.ap[:, 1], op=ALU.subtract),
                 reads=[rt.key], writes=[TM.key])
            S.op("dve", lambda e: e.tensor_tensor(out=x2, in0=rt.ap[:, 2], in1=rt.ap[:, 3], op=ALU.add),
                 reads=[rt.key], writes=[TM.key])
        for ch in range(9):
            ps = S.psum()
            for tt in range(4):
                S.op("pe", lambda e, tt=tt: e.transpose(out=ps.ap[:, tt * 128:(tt + 1) * 128],
                                                         in_=TM.ap[:, tt, ch * 128:(ch + 1) * 128], identity=self.ident.ap),
                     reads=[TM.key, self.ident.key], writes=[ps.key])
            o = self.bst.next()
            S.op("act", lambda e: e.mul(o.ap, ps.ap, (0.125 if ch < 6 else 1.0)), reads=[ps.key], writes=[o.key])
            S.dma("sp", self.QKT.ap[ch * 128:(ch + 1) * 128, t0:t0 + 512], o.ap, reads=[o.key], writes=[self.QKT.key])
        for tt in range(4):
            S.op("act", lambda e, tt=tt: e.activation(out=self.vst.ap[:, tt, :], in_=TM.ap[:, tt, 1152:1536], func=AF.Copy),
                 reads=[TM.key], writes=[self.vst.key])
            S.op("dve", lambda e, tt=tt: e.tensor_tensor(out=self.gst.ap[:, tt, :], in0=TM.ap[:, tt, 1536:1572],
                                                         in1=self.GATEB.ap, op=ALU.add),
                 reads=[TM.key, self.GATEB.key], writes=[self.gst.key])
        S.op("act", lambda e: e.activation(out=self.gst.ap, in_=self.gst.ap, func=AF.Sigmoid), reads=[self.gst.key],
             writes=[self.gst.key])
        S.dma("sp", self.VSW.ap[t0:t0 + 512, :].rearrange("(a p) c -> p a c", p=128), self.vst.ap, reads=[self.vst.key],
              writes=[self.VSW.key])
        S.dma("sp", self.GL.ap[t0:t0 + 512, :].rearrange("(a p) c -> p a c", p=128), self.gst.ap, reads=[self.gst.key],
              writes=[self.GL.key])

    def build(self):
        S = self.S
        self.gstack = ExitStack()
        _STACK[0] = self.gstack
        self._consts()
        src = self.x_in
        sa = self.stop_after
        self.convert(0, 0)
        for l in range(self.nlayer):
            last = (l == self.nlayer - 1)
            with ExitStack() as st:
                _STACK[0] = st
                self.alloc_loop()
                self.layer_consts(l, True)
                self.loop1(l, src)
                S.barrier()
            if sa == "loop1":
                break
            with ExitStack() as st:
                _STACK[0] = st
                self.convert(l, 1)
                if not last:
                    self.convert(l + 1, 0)
                self.rwkv(l)
                S.barrier()
            if sa == "rwkv":
                break
            with ExitStack() as st:
                _STACK[0] = st
                self.poolmix(l)
                S.barrier()
            if sa == "pool":
                break
            with ExitStack() as st:
                _STACK[0] = st
                self.nsa(l)
                S.barrier()
            if sa == "nsa":
                break
            with ExitStack() as st:
                _STACK[0] = st
                self.alloc_loop()
                self.layer_consts(l, False)
                self.loop3(l, self.out if last else self.XN)
                S.barrier()
            src = self.XN
        S.barrier()


    def rwkv(self, l):
        nc, S = self.nc, self.S
        T = self.T
        NCH = T // 128
        ident = self.ident

        def W(name, shape, dt=F32):
            return sb(nc, "rk%d_" % l + name, shape, dt)
        w2 = W("w2", [64, 768]); a2 = W("a2", [64, 768]); g2 = W("g2", [128, 768])
        vec = W("vec", [64, 7, 12])
        omka = W("omka", [64, 12])
        ones = W("ones", [64, 64])
        msk = W("msk", [128, 3, 128])
        rmask = W("rmask", [64, 4, 128])
        gneps = W("gneps", [128, 1])
        S.dma("sp", w2.ap, self.rw_w2.ap[l], reads=[self.rw_w2.key], writes=[w2.key])
        S.dma("sp", a2.ap, self.rw_a2.ap[l], reads=[self.rw_a2.key], writes=[a2.key])
        S.dma("sp", g2.ap, self.rw_g2.ap[l], reads=[self.rw_g2.key], writes=[g2.key])
        S.dma("sp", vec.ap, self.rwvec.ap[l], reads=[self.rwvec.key], writes=[vec.key])
        S.dma("sp", msk.ap, self.c_masks.ap.rearrange("m p c -> p m c"), reads=[self.c_masks.key], writes=[msk.key])
        S.op("dve", lambda e: e.memset(ones.ap, 1.0), writes=[ones.key])
        S.op("dve", lambda e: e.memset(gneps.ap, GN_EPS), writes=[gneps.key])
        S.op("dve", lambda e: e.memset(rmask.ap, 1.0), writes=[rmask.key])
        S.op("dve", lambda e: e.memset(rmask.ap[:, :, 0:1], 0.0), writes=[rmask.key])
        S.op("dve", lambda e: e.tensor_scalar(out=omka.ap, in0=vec.ap[:, 3, :], scalar1=-1.0, scalar2=1.0, op0=ALU.mult,
                                              op1=ALU.add), reads=[vec.key], writes=[omka.key])
        R = W("R", [64, 4, 128]); K = W("K", [64, 4, 128]); V = W("V", [64, 4, 128])
        SG = W("SG", [64, 4, 128]); A = W("A", [64, 4, 128]); GT = W("GT", [64, 4, 128])
        KM = W("KM", [64, 4, 128]); CUM = W("CUM", [64, 4, 128]); E1 = W("E1", [64, 4, 128])
        E2 = W("E2", [64, 4, 128]); BT = W("BT", [64, 4, 128]); BON = W("BON", [64, 4, 128])
        t1 = W("t1", [64, 4, 128]); t2 = W("t2", [64, 4, 128])
        WL = W("WL", [64, 128]); AL = W("AL", [64, 128]); GLr = W("GLr", [128, 128])
        tw = W("tw", [64, 128]); sg = W("sg", [128, 128])
        VT = W("VT", [128, 4, 64]); AHT = W("AHT", [128, 4, 64]); BPT = W("BPT", [128, 4, 64]); KPT = W("KPT", [128, 4, 64])
        AtT = W("AtT", [128, 4, 64]); WT = W("WT", [128, 4, 64]); Yk = W("Yk", [128, 4, 64]); Yd = W("Yd", [128, 4, 64])
        Ysq = W("Ysq", [128, 4, 64])
        Nm = W("Nm", [128, 4, 128]); NTm = W("NTm", [128, 4, 128]); MTm = W("MTm", [128, 4, 128])
        Pm = W("Pm", [128, 4, 128]); Qm = W("Qm", [128, 4, 128]); Pb = W("Pb", [128, 4, 128]); PbT = W("PbT", [128, 4, 128])
        Ta = W("Ta", [128, 4, 128]); Tb = W("Tb", [128, 4, 128]); X1 = W("X1", [128, 4, 128]); Zm = W("Zm", [128, 4, 128])
        Gs = W("Gs", [64, 4, 64]); DG = W("DG", [64, 4, 64]); Rt = W("Rt", [64, 4, 128])
        STs = [[W("ST%d_%d" % (g, i), [64, 4, 64]) for i in range(2)] for g in range(3)]
        sm = W("sm", [128, 4]); sv = W("sv", [128, 4])
        OB = W("OB", [64, 4, 128], BF16)
        OF = W("OF", [64, 4, 128])

        def bc_h(v, n):
            return v.unsqueeze(2).broadcast_to([64, 4, n])

        def dve(fn, reads, writes):
            S.op("dve", fn, reads=[b.key for b in reads], writes=[b.key for b in writes])

        def act(fn, reads, writes):
            S.op("act", fn, reads=[b.key for b in reads], writes=[b.key for b in writes])

        def tt_(out, in0, in1, op, reads, writes):
            dve(lambda e: e.tensor_tensor(out=out, in0=in0, in1=in1, op=op), reads, writes)

        def pemm(ps, sl, lhsT, rhs, reads, start=True, stop=True):
            S.op("pe", lambda e: e.matmul(sl, lhsT=lhsT, rhs=rhs, start=start, stop=stop),
                 reads=[b.key for b in reads], writes=[ps.key])

        fl = lambda b: b.ap.rearrange("p a b -> p (a b)")
        mU, mUi, mL = msk.ap[:, 0, :], msk.ap[:, 1, :], msk.ap[:, 2, :]
        b4 = lambda m: m.unsqueeze(1).broadcast_to([128, 4, 128])
        idb = ident.ap.unsqueeze(1).broadcast_to([128, 4, 128])
        id64b = ident.ap[0:64, 0:64].unsqueeze(1).broadcast_to([64, 4, 64])

        for hg in range(3):
            h0 = hg * 4
            vh = lambda i: vec.ap[:, i, h0:h0 + 4]
            S.op("dve", lambda e: e.memset(STs[hg][0].ap, 0.0), writes=[STs[hg][0].key])
            for c in range(NCH):
                t0 = c * 128
                STc, STn = STs[hg][c % 2], STs[hg][(c + 1) % 2]
                fm = self.FMT
                for (buf, base) in ((R, 0), (K, 768), (V, 1536)):
                    S.dma("sp", buf.ap, fm.ap[base + h0 * 64:base + (h0 + 4) * 64, t0:t0 + 128].rearrange("(h c) t -> c h t", c=64),
                          reads=[fm.key], writes=[buf.key])
                S.dma("sp", WL.ap, fm.ap[2304:2368, t0:t0 + 128], reads=[fm.key], writes=[WL.key])
                S.dma("sp", AL.ap, fm.ap[2368:2432, t0:t0 + 128], reads=[fm.key], writes=[AL.key])
                S.dma("sp", GLr.ap, fm.ap[2432:2560, t0:t0 + 128], reads=[fm.key], writes=[GLr.key])
                act(lambda e: e.activation(out=tw.ap, in_=WL.ap, func=AF.Tanh), [WL], [tw])
                act(lambda e: e.activation(out=sg.ap, in_=GLr.ap, func=AF.Sigmoid), [GLr], [sg])
                p1, p2, p3 = S.psum(), S.psum(), S.psum()
                for j in range(4):
                    h = h0 + j
                    pemm(p1, p1.ap[0:64, j * 128:(j + 1) * 128], w2.ap[:, h * 64:(h + 1) * 64], tw.ap, [w2, tw])
                    pemm(p2, p2.ap[0:64, j * 128:(j + 1) * 128], a2.ap[:, h * 64:(h + 1) * 64], AL.ap, [a2, AL])
                    pemm(p3, p3.ap[0:64, j * 128:(j + 1) * 128], g2.ap[:, h * 64:(h + 1) * 64], sg.ap, [g2, sg])
                v3 = lambda p: p.ap[0:64, :].rearrange("p (a b) -> p a b", a=4)
                tt_(t1.ap, v3(p1), bc_h(vh(0), 128), ALU.add, [p1, vec], [t1])
                act(lambda e: e.activation(out=SG.ap, in_=t1.ap, func=AF.Sigmoid), [t1], [SG])
                tt_(t2.ap, v3(p2), bc_h(vh(1), 128), ALU.add, [p2, vec], [t2])
                act(lambda e: e.activation(out=A.ap, in_=t2.ap, func=AF.Sigmoid), [t2], [A])
                act(lambda e: e.activation(out=GT.ap, in_=v3(p3), func=AF.Copy), [p3], [GT])
                tt_(t1.ap, A.ap, bc_h(vh(3), 128), ALU.mult, [A, vec], [t1])
                tt_(t1.ap, t1.ap, bc_h(omka.ap[:, h0:h0 + 4], 128), ALU.add, [t1, omka], [t1])
                tt_(KM.ap, K.ap, t1.ap, ALU.mult, [K, t1], [KM])
                tt_(K.ap, K.ap, bc_h(vh(2), 128), ALU.mult, [K, vec], [K])
                tt_(t1.ap, K.ap, K.ap, ALU.mult, [K], [t1])
                pn = S.psum()
                pemm(pn, pn.ap[0:64, :], ones.ap, fl(t1), [ones, t1])
                act(lambda e: e.activation(out=fl(t1), in_=pn.ap[0:64, :], func=AF.Sqrt), [pn], [t1])
                dve(lambda e: e.tensor_scalar(out=t1.ap, in0=t1.ap, scalar1=1e-12, scalar2=None, op0=ALU.max), [t1], [t1])
                dve(lambda e: e.reciprocal(out=t1.ap, in_=t1.ap), [t1], [t1])
                tt_(K.ap, K.ap, t1.ap, ALU.mult, [K, t1], [K])
                tt_(A.ap, K.ap, A.ap, ALU.mult, [K, A], [A])
                tt_(t2.ap, R.ap, KM.ap, ALU.mult, [R, KM], [t2])
                tt_(t2.ap, t2.ap, bc_h(vh(4), 128), ALU.mult, [t2, vec], [t2])
                pb_ = S.psum()
                pemm(pb_, pb_.ap[0:64, :], ones.ap, fl(t2), [ones, t2])
                tt_(BON.ap, v3(pb_), V.ap, ALU.mult, [pb_, V], [BON])
                dve(lambda e: e.tensor_tensor_scan(out=fl(CUM), data0=fl(rmask), data1=fl(SG), initial=0.0, op0=ALU.mult,
                                                   op1=ALU.add), [rmask, SG], [CUM])
                act(lambda e: e.activation(out=E1.ap, in_=CUM.ap, func=AF.Exp, scale=-CDEC), [CUM], [E1])
                act(lambda e: e.activation(out=E2.ap, in_=CUM.ap, func=AF.Exp, scale=CDEC), [CUM], [E2])
                tt_(SG.ap, CUM.ap, SG.ap, ALU.subtract, [CUM, SG], [SG])
                act(lambda e: e.activation(out=SG.ap, in_=SG.ap, func=AF.Exp, scale=-CDEC), [SG], [SG])
                tt_(CUM.ap, CUM.ap[:, :, 127:128].broadcast_to([64, 4, 128]), CUM.ap, ALU.subtract, [CUM], [CUM])
                act(lambda e: e.activation(out=CUM.ap, in_=CUM.ap, func=AF.Exp, scale=-CDEC), [CUM], [CUM])
                dve(lambda e: e.scalar_tensor_tensor(out=SG.ap, in0=K.ap, scalar=-1.0, in1=SG.ap, op0=ALU.mult, op1=ALU.mult),
                    [K, SG], [SG])
                tt_(R.ap, R.ap, E1.ap, ALU.mult, [R, E1], [R])
                tt_(BT.ap, A.ap, E2.ap, ALU.mult, [A, E2], [BT])
                tt_(E2.ap, KM.ap, E2.ap, ALU.mult, [KM, E2], [E2])
                tt_(A.ap, A.ap, CUM.ap, ALU.mult, [A, CUM], [A])
                tt_(KM.ap, KM.ap, CUM.ap, ALU.mult, [KM, CUM], [KM])
                AH, RH, KT, BP, KP = SG, R, E2, A, KM
                for (src_, dst_) in ((V, VT), (AH, AHT), (BP, BPT), (KP, KPT)):
                    ps = S.psum()
                    for j in range(4):
                        S.op("pe", lambda e, j=j: e.transpose(out=ps.ap[:, j * 64:(j + 1) * 64], in_=src_.ap[:, j, :],
                                                               identity=ident.ap[0:64, 0:64]),
                             reads=[src_.key, ident.key], writes=[ps.key])
                    act(lambda e: e.activation(out=fl(dst_), in_=ps.ap[:, 0:256], func=AF.Copy), [ps], [dst_])
                def stage(dst_, lh, rh, mask):
                    ps = S.psum()
                    for j in range(4):
                        pemm(ps, ps.ap[:, j * 128:(j + 1) * 128], lh.ap[:, j, :], rh.ap[:, j, :], [lh, rh])
                    tt_(dst_.ap, ps.ap.rearrange("p (a b) -> p a b", a=4), b4(mask), ALU.mult, [ps, msk], [dst_])
                stage(Nm, BT, AH, mU)
                stage(NTm, AH, BT, mL)
                stage(MTm, AH, KT, mL)
                stage(Pm, BT, RH, mUi)
                stage(Qm, KT, RH, mUi)
                tt_(Ta.ap, Nm.ap, idb, ALU.add, [Nm, ident], [Ta])
                Pp, PpT, Pn, PnT = Nm, NTm, Pb, PbT
                Tp, Tn = Ta, Tb
                for k in range(1, 7):
                    if k < 6:
                        ps = S.psum()
                        for j in range(4):
                            pemm(ps, ps.ap[:, j * 128:(j + 1) * 128], PpT.ap[:, j, :], Pp.ap[:, j, :], [PpT, Pp])
                        act(lambda e: e.activation(out=fl(Pn), in_=ps.ap, func=AF.Copy), [ps], [Pn])
                    ps2 = S.psum()
                    for j in range(4):
                        pemm(ps2, ps2.ap[:, j * 128:(j + 1) * 128], Pp.ap[:, j, :], PpT.ap[:, j, :], [Pp, PpT])
                    dve(lambda e: e.tensor_copy(out=fl(PnT), in_=ps2.ap), [ps2], [PnT])
                    ps3 = S.psum()
                    for j in range(4):
                        pemm(ps3, ps3.ap[:, j * 128:(j + 1) * 128], PnT.ap[:, j, :], Tp.ap[:, j, :], [PnT, Tp])
                    tt_(fl(Tn), ps3.ap, fl(Tp), ALU.add, [ps3, Tp], [Tn])
                    Pp, PpT, Pn, PnT = Pn, PnT, Pp, PpT
                    Tp, Tn = Tn, Tp
                Tf = Tp
                ps = S.psum()
                for j in range(4):
                    pemm(ps, ps.ap[:, j * 64:(j + 1) * 64], Tf.ap[:, j, :], AHT.ap[:, j, :], [Tf, AHT])
                act(lambda e: e.activation(out=fl(AtT), in_=ps.ap[:, 0:256], func=AF.Copy), [ps], [AtT])
                ps = S.psum()
                for j in range(4):
                    pemm(ps, ps.ap[:, j * 128:(j + 1) * 128], Tf.ap[:, j, :], MTm.ap[:, j, :], [Tf, MTm])
                dve(lambda e: e.tensor_copy(out=fl(X1), in_=ps.ap), [ps], [X1])
                ps = S.psum()
                for j in range(4):
                    pemm(ps, ps.ap[0:64, j * 64:(j + 1) * 64], AtT.ap[:, j, :], BPT.ap[:, j, :], [AtT, BPT])
                tt_(DG.ap, id64b, E1.ap[:, :, 127:128].broadcast_to([64, 4, 64]), ALU.mult, [ident, E1], [DG])
                tt_(fl(Gs), ps.ap[0:64, 0:256], fl(DG), ALU.add, [ps, DG], [Gs])
                ps = S.psum()
                for j in range(4):
                    pemm(ps, ps.ap[:, j * 64:(j + 1) * 64], X1.ap[:, j, :], BPT.ap[:, j, :], [X1, BPT])
                tt_(fl(WT), ps.ap[:, 0:256], fl(KPT), ALU.add, [ps, KPT], [WT])
                ps = S.psum()
                for j in range(4):
                    pemm(ps, ps.ap[0:64, j * 128:(j + 1) * 128], AtT.ap[:, j, :], Pm.ap[:, j, :], [AtT, Pm])
                tt_(fl(Rt), ps.ap[0:64, :], fl(RH), ALU.add, [ps, RH], [Rt])
                ps = S.psum()
                for j in range(4):
                    pemm(ps, ps.ap[:, j * 128:(j + 1) * 128], X1.ap[:, j, :], Pm.ap[:, j, :], [X1, Pm])
                tt_(fl(Zm), ps.ap, fl(Qm), ALU.add, [ps, Qm], [Zm])
                py = S.psum()
                for j in range(4):
                    pemm(py, py.ap[:, j * 64:(j + 1) * 64], Rt.ap[:, j, :], STc.ap[:, j, :], [Rt, STc], True, False)
                    pemm(py, py.ap[:, j * 64:(j + 1) * 64], Zm.ap[:, j, :], VT.ap[:, j, :], [Zm, VT], False, True)
                pst = S.psum()
                for j in range(4):
                    pemm(pst, pst.ap[0:64, j * 64:(j + 1) * 64], Gs.ap[:, j, :], STc.ap[:, j, :], [Gs, STc], True, False)
                    pemm(pst, pst.ap[0:64, j * 64:(j + 1) * 64], WT.ap[:, j, :], VT.ap[:, j, :], [WT, VT], False, True)
                act(lambda e: e.activation(out=fl(STn), in_=pst.ap[0:64, 0:256], func=AF.Copy), [pst], [STn])
                act(lambda e: e.activation(out=fl(Yk), in_=py.ap[:, 0:256], func=AF.Copy), [py], [Yk])
                dve(lambda e: e.tensor_reduce(out=sm.ap, in_=Yk.ap, axis=AX.X, op=ALU.add), [Yk], [sm])
                dve(lambda e: e.tensor_scalar(out=sm.ap, in0=sm.ap, scalar1=-1.0 / 64, scalar2=None, op0=ALU.mult), [sm], [sm])
                tt_(Yd.ap, Yk.ap, sm.ap.unsqueeze(2).broadcast_to([128, 4, 64]), ALU.add, [Yk, sm], [Yd])
                tt_(Ysq.ap, Yd.ap, Yd.ap, ALU.mult, [Yd], [Ysq])
                dve(lambda e: e.tensor_reduce(out=sv.ap, in_=Ysq.ap, axis=AX.X, op=ALU.add), [Ysq], [sv])
                act(lambda e: e.activation(out=sv.ap, in_=sv.ap, func=AF.Sqrt, bias=gneps.ap, scale=1.0 / 64), [sv, gneps], [sv])
                dve(lambda e: e.reciprocal(out=sv.ap, in_=sv.ap), [sv], [sv])
                tt_(Yd.ap, Yd.ap, sv.ap.unsqueeze(2).broadcast_to([128, 4, 64]), ALU.mult, [Yd, sv], [Yd])
                po = S.psum()
                for j in range(4):
                    S.op("pe", lambda e, j=j: e.transpose(out=po.ap[0:64, j * 128:(j + 1) * 128], in_=Yd.ap[:, j, :],
                                                           identity=ident.ap), reads=[Yd.key, ident.key], writes=[po.key])
                tt_(OF.ap, v3(po), bc_h(vh(5), 128), ALU.mult, [po, vec], [OF])
                tt_(OF.ap, OF.ap, bc_h(vh(6), 128), ALU.add, [OF, vec], [OF])
                tt_(OF.ap, OF.ap, BON.ap, ALU.add, [OF, BON], [OF])
                tt_(OB.ap, OF.ap, GT.ap, ALU.mult, [OF, GT], [OB])
                S.dma("sp", self.YT.ap[h0 * 64:(h0 + 4) * 64, t0:t0 + 128].rearrange("(h c) t -> c h t", c=64), OB.ap,
                      reads=[OB.key], writes=[self.YT.key])


    def nsa(self, l):
        nc, S = self.nc, self.S
        T = self.T
        NQT = T // 128
        NC = T // 16 - 1
        NT = (NC + 127) // 128
        ident = self.ident
        rot = {"i": 0, "a": 0}

        def psr():
            b = S.banks[rot["i"] % 5]
            rot["i"] += 1
            return b

        def psa():
            b = S.banks[5 + rot["a"] % 3]
            rot["a"] += 1
            return b

        def W(name, shape, dt=F32):
            return sb(nc, "ns%d_" % l + name, shape, dt)

        def dve(fn, reads, writes):
            S.op("dve", fn, reads=[b.key for b in reads], writes=[b.key for b in writes])

        def act(fn, reads, writes):
            S.op("act", fn, reads=[b.key for b in reads], writes=[b.key for b in writes])

        def pemm(ps, sl, lhsT, rhs, reads, start=True, stop=True):
            S.op("pe", lambda e: e.matmul(sl, lhsT=lhsT, rhs=rhs, start=start, stop=stop),
                 reads=[b.key for b in reads], writes=[ps.key])

        KCMPT = W("KCMPT", [64, 3, 256], BF16)
        VCMP = W("VCMP", [128, 2, 3, 65], BF16)
        OVL = W("OVL", [128, 2, 65], BF16)
        KST = W("KST", [64, 3, T], BF16)
        KWT = W("KWT", [64, 3, T], BF16)
        VS = W("VS", [128, NQT, 3, 65], BF16)
        VW = W("VW", [128, NQT, 3, 65], BF16)
        EXPD = W("EXPD", [64, 32, 128], BF16)
        identb = W("identb", [128, 128], BF16)
        CAUS = W("CAUS", [128, 4, 128], BF16)
        BAND = W("BAND", [128, 4, 128], BF16)
        S.op("dve", lambda e: e.memset(KCMPT.ap, 0.0), writes=[KCMPT.key])
        S.op("dve", lambda e: e.memset(VCMP.ap, 0.0), writes=[VCMP.key])
        S.op("dve", lambda e: e.memset(VCMP.ap[:, :, :, 64:65], 1.0), writes=[VCMP.key])
        S.op("dve", lambda e: e.memset(VS.ap[:, :, :, 64:65], 1.0), writes=[VS.key])
        S.op("dve", lambda e: e.memset(VW.ap[:, :, :, 64:65], 1.0), writes=[VW.key])
        S.op("dve", lambda e: e.tensor_copy(out=identb.ap, in_=ident.ap), reads=[ident.key], writes=[identb.key])
        S.dma("pool", OVL.ap, self.c_ovl.ap.rearrange("a p c -> p a c"), reads=[self.c_ovl.key], writes=[OVL.key])
        S.dma("pool", EXPD.ap, self.c_expand.ap, reads=[self.c_expand.key], writes=[EXPD.key])
        S.dma("pool", CAUS.ap, self.c_caus.ap[0].unsqueeze(1).broadcast_to([128, 4, 128]), reads=[self.c_caus.key],
              writes=[CAUS.key])
        S.dma("pool", BAND.ap, self.c_caus.ap[1].unsqueeze(1).broadcast_to([128, 4, 128]), reads=[self.c_caus.key],
              writes=[BAND.key])
        S.dma("sp", KST.ap, self.QKT.ap[768:960, 0:T].rearrange("(h c) t -> c h t", c=64), reads=[self.QKT.key],
              writes=[KST.key])
        S.dma("sp", KWT.ap, self.QKT.ap[960:1152, 0:T].rearrange("(h c) t -> c h t", c=64), reads=[self.QKT.key],
              writes=[KWT.key])
        for k0 in range(0, NQT, 4):
            for h in range(3):
                S.dma("sp", VS.ap[:, k0:k0 + 4, h, 0:64],
                      self.VSW.ap[k0 * 128:(k0 + 4) * 128, h * 64:(h + 1) * 64].rearrange("(k p) c -> p k c", p=128),
                      reads=[self.VSW.key], writes=[VS.key])
                S.dma("sp", VW.ap[:, k0:k0 + 4, h, 0:64],
                      self.VSW.ap[k0 * 128:(k0 + 4) * 128, 192 + h * 64:192 + (h + 1) * 64].rearrange("(k p) c -> p k c", p=128),
                      reads=[self.VSW.key], writes=[VW.key])

        w1 = W("w1", [64, 32, 256]); w2c = W("w2c", [128, 2, 64]); pe = W("pe", [64, 32])
        biasc = W("biasc", [128, 2]); XC = W("XC", [64, T]); Hc = W("Hc", [128, 2, 256])
        hx = W("hx", [128, 256]); hu = W("hu", [128, 256])
        KCt = W("KCt", [128, 2, 3, 64]); CSC = W("CSC", [128, 2, 16]); rtc = W("rtc", [128, 4, 6, 8])
        S.op("dve", lambda e: e.memset(KCt.ap, 0.0), writes=[KCt.key])
        S.dma("sp", CSC.ap, self.c_ropec.ap.rearrange("(a p) c -> p a c", p=128), reads=[self.c_ropec.key], writes=[CSC.key])
        XC3 = XC.ap.rearrange("p (n b) -> p n b", b=16)
        for kv in range(2):
            S.dma("sp", w1.ap, self.cw1.ap[l, kv].rearrange("d (a f) -> d a f", f=256), reads=[self.cw1.key], writes=[w1.key])
            S.dma("sp", w2c.ap, self.cw2.ap[l, kv], reads=[self.cw2.key], writes=[w2c.key])
            S.dma("sp", pe.ap, self.cpe.ap[l, kv], reads=[self.cpe.key], writes=[pe.key])
            for fh in range(2):
                ps = psr()
                for l_ in range(32):
                    pemm(ps, ps.ap[:, 0:1], w1.ap[:, l_, fh * 128:(fh + 1) * 128], pe.ap[:, l_:l_ + 1], [w1, pe], l_ == 0, l_ == 31)
                dve(lambda e: e.tensor_copy(out=biasc.ap[:, fh:fh + 1], in_=ps.ap[:, 0:1]), [ps], [biasc])
            for h in range(3):
                r0 = 3072 + kv * 192 + h * 64
                S.dma("sp", XC.ap, self.FMT.ap[r0:r0 + 64, 0:T], reads=[self.FMT.key], writes=[XC.key])
                for fh in range(2):
                    ps = psr()
                    for l_ in range(32):
                        a_, b_ = l_ // 16, l_ % 16
                        pemm(ps, ps.ap[:, 0:NC], w1.ap[:, l_, fh * 128:(fh + 1) * 128], XC3[:, a_:a_ + NC, b_], [w1, XC],
                             l_ == 0, l_ == 31)
                    x_, u_ = hx.ap[:, 0:NC], hu.ap[:, 0:NC]
                    dve(lambda e: e.tensor_scalar(out=x_, in0=ps.ap[:, 0:NC], scalar1=biasc.ap[:, fh:fh + 1], scalar2=None,
                                                  op0=ALU.add), [ps, biasc], [hx])
                    dve(lambda e: e.tensor_tensor(out=u_, in0=x_, in1=x_, op=ALU.mult), [hx], [hu])
                    dve(lambda e: e.tensor_scalar(out=u_, in0=u_, scalar1=0.044715, scalar2=1.0, op0=ALU.mult, op1=ALU.add),
                        [hu], [hu])
                    dve(lambda e: e.tensor_tensor(out=u_, in0=u_, in1=x_, op=ALU.mult), [hu, hx], [hu])
                    act(lambda e: e.activation(out=u_, in_=u_, func=AF.Tanh, scale=0.7978845608028654), [hu], [hu])
                    dve(lambda e: e.scalar_tensor_tensor(out=u_, in0=u_, scalar=1.0, in1=x_, op0=ALU.add, op1=ALU.mult),
                        [hu, hx], [hu])
                    dve(lambda e: e.tensor_scalar(out=Hc.ap[:, fh, 0:NC], in0=u_, scalar1=0.5, scalar2=None, op0=ALU.mult),
                        [hu], [Hc])
                for nt in range(NT):
                    n0 = nt * 128
                    nn = min(128, NC - n0)
                    ps = psr()
                    for fh in range(2):
                        pemm(ps, ps.ap[0:nn, 0:64], Hc.ap[:, fh, n0:n0 + nn], w2c.ap[:, fh, :], [Hc, w2c], fh == 0, fh == 1)
                    if kv == 1:
                        act(lambda e: e.activation(out=VCMP.ap[0:nn, nt, h, 0:64], in_=ps.ap[0:nn, 0:64], func=AF.Copy),
                            [ps], [VCMP])
                    else:
                        act(lambda e: e.activation(out=KCt.ap[0:nn, nt, h, :], in_=ps.ap[0:nn, 0:64], func=AF.Copy),
                            [ps], [KCt])
            if kv == 0:
                v = KCt.ap.rearrange("p a h c -> p (a h) c")
                x1, x2 = v[:, :, 0:8], v[:, :, 8:16]
                cosb = CSC.ap[:, :, 0:8].unsqueeze(2).broadcast_to([128, 2, 3, 8]).rearrange("p a h c -> p (a h) c") \
                    if False else None
                for nt in range(NT):
                    vv = KCt.ap[:, nt]
                    x1, x2 = vv[:, :, 0:8], vv[:, :, 8:16]
                    cb_ = CSC.ap[:, nt, 0:8].unsqueeze(1).broadcast_to([128, 3, 8])
                    sb_ = CSC.ap[:, nt, 8:16].unsqueeze(1).broadcast_to([128, 3, 8])
                    r_ = rtc.ap
                    dve(lambda e: e.tensor_tensor(out=r_[:, 0, 0:3], in0=x1, in1=cb_, op=ALU.mult), [KCt, CSC], [rtc])
                    dve(lambda e: e.tensor_tensor(out=r_[:, 1, 0:3], in0=x2, in1=sb_, op=ALU.mult), [KCt, CSC], [rtc])
                    dve(lambda e: e.tensor_tensor(out=r_[:, 2, 0:3], in0=x2, in1=cb_, op=ALU.mult), [KCt, CSC], [rtc])
                    dve(lambda e: e.tensor_tensor(out=r_[:, 3, 0:3], in0=x1, in1=sb_, op=ALU.mult), [KCt, CSC], [rtc])
                    dve(lambda e: e.tensor_tensor(out=x1, in0=r_[:, 0, 0:3], in1=r_[:, 1, 0:3], op=ALU.subtract), [rtc], [KCt])
                    dve(lambda e: e.tensor_tensor(out=x2, in0=r_[:, 2, 0:3], in1=r_[:, 3, 0:3], op=ALU.add), [rtc], [KCt])
                    for h in range(3):
                        ps = psr()
                        S.op("pe", lambda e: e.transpose(out=ps.ap[0:64, 0:128], in_=KCt.ap[:, nt, h, :], identity=ident.ap),
                             reads=[KCt.key, ident.key], writes=[ps.key])
                        act(lambda e: e.activation(out=KCMPT.ap[:, h, nt * 128:(nt + 1) * 128], in_=ps.ap[0:64, 0:128],
                                                   func=AF.Copy), [ps], [KCMPT])

        Gt = W("Gt", [128, 36]); FORCE = W("FORCE", [128, 64]); VALID = W("VALID", [128, 64])
        CM = W("CM", [128, 2, 4, 128], BF16)
        QTt = Pool(nc, "ns%dQT" % l, [64, 4, 128], BF16, 2)
        expool = Pool(nc, "ns%dex" % l, [128, 512], BF16, 4)
        rz = W("rz", [128, 4]); coef = W("coef", [128, 4]); imp = W("imp", [128, 64]); sc2 = W("sc2", [128, 64])
        m16 = W("m16", [128, 16]); negm = W("negm", [128, 64]); NEGT = W("NEGT", [64, 4, 128], BF16)
        YN = W("YN", [128, 4, 64]); ytmp = W("ytmp", [128, 4, 64]); OBn = W("OBn", [64, 4, 128], BF16)
        fl2 = lambda b: b.ap.rearrange("p a b -> p (a b)")

        def combine(bank, h, br, first):
            bv = bank.ap[:, 0:260].rearrange("p (g c) -> p g c", g=4)
            dve(lambda e: e.tensor_scalar(out=rz.ap, in0=bv[:, :, 64], scalar1=1e-30, scalar2=None, op0=ALU.max), [bank], [rz])
            dve(lambda e: e.reciprocal(out=rz.ap, in_=rz.ap), [rz], [rz])
            gv = Gt.ap[:, h * 12:(h + 1) * 12].rearrange("p (g b) -> p g b", b=3)[:, :, br]
            dve(lambda e: e.tensor_tensor(out=coef.ap, in0=rz.ap, in1=gv, op=ALU.mult), [rz, Gt], [coef])
            cb_ = coef.ap.unsqueeze(2).broadcast_to([128, 4, 64])
            if first:
                dve(lambda e: e.tensor_tensor(out=YN.ap, in0=bv[:, :, 0:64], in1=cb_, op=ALU.mult), [bank, coef], [YN])
            else:
                dve(lambda e: e.tensor_tensor(out=ytmp.ap, in0=bv[:, :, 0:64], in1=cb_, op=ALU.mult), [bank, coef], [ytmp])
                dve(lambda e: e.tensor_tensor(out=YN.ap, in0=YN.ap, in1=ytmp.ap, op=ALU.add), [YN, ytmp], [YN])

        for qt in range(NQT):
            q0 = qt * 128
            S.dma("sp", Gt.ap, self.GL.ap[q0:q0 + 128, :], reads=[self.GL.key], writes=[Gt.key])
            S.dma("sp", FORCE.ap, self.c_force.ap[qt], reads=[self.c_force.key], writes=[FORCE.key])
            S.dma("sp", VALID.ap, self.c_valid.ap[qt], reads=[self.c_valid.key], writes=[VALID.key])
            nts = [nt for nt in range(NT) if 16 * (nt * 128) + 31 <= q0 + 127]
            for nt in nts:
                S.dma("pool", CM.ap[:, nt], self.c_cmask.ap[qt, nt].unsqueeze(1).broadcast_to([128, 4, 128]),
                      reads=[self.c_cmask.key], writes=[CM.key])
            for h in range(3):
                qb = QTt.next()
                S.dma("sp", qb.ap, self.QKT.ap[h * 256:(h + 1) * 256, q0:q0 + 128].rearrange("(g c) t -> c g t", c=64),
                      reads=[self.QKT.key], writes=[qb.key])
                q2 = fl2(qb)
                po, pi = psa(), psa()
                for ii, nt in enumerate(nts):
                    ps = psr()
                    pemm(ps, ps.ap, KCMPT.ap[:, h, nt * 128:(nt + 1) * 128], q2, [KCMPT, qb], True, False)
                    pemm(ps, ps.ap, identb.ap, CM.ap[:, nt].rearrange("p a b -> p (a b)"), [identb, CM], False, True)
                    ex = expool.next()
                    act(lambda e: e.activation(out=ex.ap, in_=ps.ap, func=AF.Exp), [ps], [ex])
                    for g in range(4):
                        pemm(po, po.ap[:, g * 65:(g + 1) * 65], ex.ap[:, g * 128:(g + 1) * 128], VCMP.ap[:, nt, h, :], [ex, VCMP],
                             ii == 0 and g == 0, ii == len(nts) - 1 and g == 3)
                    for g in range(4):
                        pemm(pi, pi.ap[:, g * 65:(g + 1) * 65], ex.ap[:, g * 128:(g + 1) * 128], OVL.ap[:, nt, :], [ex, OVL],
                             ii == 0 and g == 0, ii == len(nts) - 1 and g == 3)
                piv = pi.ap[:, 0:260].rearrange("p (g c) -> p g c", g=4)
                dve(lambda e: e.tensor_scalar(out=rz.ap, in0=piv[:, :, 64], scalar1=1e-30, scalar2=None, op0=ALU.max), [pi], [rz])
                dve(lambda e: e.reciprocal(out=rz.ap, in_=rz.ap), [rz], [rz])
                dve(lambda e: e.tensor_scalar(out=imp.ap, in0=piv[:, 0, 0:64], scalar1=rz.ap[:, 0:1], scalar2=None, op0=ALU.mult),
                    [pi, rz], [imp])
                for g in range(1, 4):
                    dve(lambda e: e.scalar_tensor_tensor(out=imp.ap, in0=piv[:, g, 0:64], scalar=rz.ap[:, g:g + 1], in1=imp.ap,
                                                         op0=ALU.mult, op1=ALU.add), [pi, rz, imp], [imp])
                dve(lambda e: e.tensor_tensor(out=imp.ap, in0=imp.ap, in1=FORCE.ap, op=ALU.max), [imp, FORCE], [imp])
                dve(lambda e: e.tensor_tensor(out=imp.ap, in0=imp.ap, in1=VALID.ap, op=ALU.add), [imp, VALID], [imp])
                dve(lambda e: e.max(out=m16.ap[:, 0:8], in_=imp.ap), [imp], [m16])
                dve(lambda e: e.match_replace(out=sc2.ap, in_to_replace=m16.ap[:, 0:8], in_values=imp.ap, imm_value=-2e30),
                    [imp, m16], [sc2])
                dve(lambda e: e.max(out=m16.ap[:, 8:16], in_=sc2.ap), [sc2], [m16])
                dve(lambda e: e.tensor_scalar(out=negm.ap, in0=imp.ap, scalar1=m16.ap[:, 15:16], scalar2=None, op0=ALU.is_ge),
                    [imp, m16], [negm])
                dve(lambda e: e.tensor_scalar(out=negm.ap, in0=negm.ap, scalar1=-1.0, scalar2=-NEG, op0=ALU.add, op1=ALU.mult),
                    [negm], [negm])
                pT = psr()
                S.op("pe", lambda e: e.transpose(out=pT.ap[0:64, 0:128], in_=negm.ap, identity=ident.ap),
                     reads=[negm.key, ident.key], writes=[pT.key])
                dve(lambda e: e.tensor_copy(out=NEGT.ap, in_=pT.ap[0:64, 0:128].unsqueeze(1).broadcast_to([64, 4, 128])),
                    [pT], [NEGT])
                combine(po, h, 0, True)
                pss = psa()
                for kt in range(qt + 1):
                    ps = psr()
                    pemm(ps, ps.ap, KST.ap[:, h, kt * 128:(kt + 1) * 128], q2, [KST, qb], True, False)
                    pemm(ps, ps.ap, EXPD.ap[:, kt, :], fl2(NEGT), [EXPD, NEGT], False, kt != qt)
                    if kt == qt:
                        pemm(ps, ps.ap, identb.ap, fl2(CAUS), [identb, CAUS], False, True)
                    ex = expool.next()
                    act(lambda e: e.activation(out=ex.ap, in_=ps.ap, func=AF.Exp), [ps], [ex])
                    for g in range(4):
                        pemm(pss, pss.ap[:, g * 65:(g + 1) * 65], ex.ap[:, g * 128:(g + 1) * 128], VS.ap[:, kt, h, :], [ex, VS],
                             kt == 0 and g == 0, kt == qt and g == 3)
                combine(pss, h, 1, False)
                psw = psa()
                kts = list(range(max(0, qt - 4), qt + 1))
                for kt in kts:
                    ps = psr()
                    masked = (kt == qt) or (kt == qt - 4)
                    pemm(ps, ps.ap, KWT.ap[:, h, kt * 128:(kt + 1) * 128], q2, [KWT, qb], True, not masked)
                    if kt == qt:
                        pemm(ps, ps.ap, identb.ap, fl2(CAUS), [identb, CAUS], False, True)
                    elif kt == qt - 4:
                        pemm(ps, ps.ap, identb.ap, fl2(BAND), [identb, BAND], False, True)
                    ex = expool.next()
                    act(lambda e: e.activation(out=ex.ap, in_=ps.ap, func=AF.Exp), [ps], [ex])
                    for g in range(4):
                        pemm(psw, psw.ap[:, g * 65:(g + 1) * 65], ex.ap[:, g * 128:(g + 1) * 128], VW.ap[:, kt, h, :], [ex, VW],
                             kt == kts[0] and g == 0, kt == qt and g == 3)
                combine(psw, h, 2, False)
                pso = psr()
                for g in range(4):
                    S.op("pe", lambda e, g=g: e.transpose(out=pso.ap[0:64, g * 128:(g + 1) * 128], in_=YN.ap[:, g, :],
                                                           identity=ident.ap), reads=[YN.key, ident.key], writes=[pso.key])
                act(lambda e: e.activation(out=fl2(OBn), in_=pso.ap[0:64, :], func=AF.Copy), [pso], [OBn])
                S.dma("sp", self.YT.ap[1280 + h * 256:1280 + (h + 1) * 256, q0:q0 + 128].rearrange("(g c) t -> c g t", c=64),
                      OBn.ap, reads=[OBn.key], writes=[self.YT.key])

    def loop3(self, l, dst):
        S = self.S
        xtok, xT, hT = self.xtok, self.xT, self.hT
        for tb in range(self.NTB):
            t0 = tb * 512
            for tt in range(4):
                S.dma("sp", xtok.ap[:, tt, :], self.X1.ap[t0 + tt * 128:t0 + (tt + 1) * 128, :], reads=[self.X1.key],
                      writes=[self.kx[tt]])
            S.dma("sp", xT.ap, self.YT.ap[:, t0:t0 + 512].rearrange("(k p) t -> p k t", p=128), reads=[self.YT.key],
                  writes=[xT.key])
            for tt in range(4):
                S.op("act", lambda e, tt=tt: e.mul(xtok.ap[:, tt, :], xtok.ap[:, tt, :], ALPHA),
                     reads=[self.kx[tt]], writes=[self.kx[tt]])

            def evac(tt, cb, ps, nco):
                S.op("dve", lambda e: e.tensor_tensor(out=xtok.ap[:, tt, cb * 512:(cb + 1) * 512], in0=ps.ap,
                                                      in1=xtok.ap[:, tt, cb * 512:(cb + 1) * 512], op=ALU.add),
                     reads=[ps.key, self.kx[tt]], writes=[self.kx[tt]])
            self.tokmm(xT, 16, self.woutB, 4, 4, evac)
            for tt in range(4):
                self.layernorm(xtok, tt, 1)
            self.make_xT(xtok, xT, 0)
            self.ffn_ln(l, 1, xtok, xT, hT)
            for tt in range(4):
                S.dma("sp", dst.ap[t0 + tt * 128:t0 + (tt + 1) * 128, :], xtok.ap[:, tt, :], reads=[self.kx[tt]],
                      writes=[dst.key])

    def poolmix(self, l):
        nc, S = self.nc, self.S
        T = self.T
        PP = sb(nc, "pPP%d" % l, [128, 16 + T], F32)
        A = sb(nc, "pA%d" % l, [128, 16 + T], F32)
        Bq = sb(nc, "pB%d" % l, [128, 16 + T], F32)
        RC = sb(nc, "pRC%d" % l, [128, T], F32)
        PW = sb(nc, "pPW%d" % l, [128, 4, 128], F32)
        PV = sb(nc, "pPV%d" % l, [128, 8], F32)
        ob = Pool(nc, "pob%d_" % l, [128, 512], BF16, 2)
        S.dma("sp", PW.ap, self.poolw.ap[l].rearrange("g c d -> c g d"), reads=[self.poolw.key], writes=[PW.key])
        S.dma("sp", PV.ap, self.poolv.ap[l], reads=[self.poolv.key], writes=[PV.key])
        for b_ in (PP, A, Bq):
            S.op("dve", lambda e: e.memset(b_.ap[:, 0:16], 0.0), writes=[b_.key])
        for gi, win in enumerate((2, 4, 8, 16)):
            S.dma("sp", PP.ap[:, 16:16 + T], self.FMT.ap[2560 + gi * 128:2560 + (gi + 1) * 128, 0:T], reads=[self.FMT.key],
                  writes=[PP.key])
            S.dma("sp", RC.ap, self.c_rcnt.ap[gi, 0:T].partition_broadcast(128), reads=[self.c_rcnt.key], writes=[RC.key])
            cur, nxt = PP, A
            sh = 1
            while sh < win:
                S.op("dve", lambda e: e.tensor_tensor(out=nxt.ap[:, 16:16 + T], in0=cur.ap[:, 16:16 + T],
                                                      in1=cur.ap[:, 16 - sh:16 - sh + T], op=ALU.add),
                     reads=[cur.key], writes=[nxt.key])
                cur = nxt
                nxt = Bq if cur is A else A
                sh *= 2
            S.op("dve", lambda e: e.tensor_tensor(out=nxt.ap[:, 16:16 + T], in0=cur.ap[:, 16:16 + T], in1=RC.ap, op=ALU.mult),
                 reads=[cur.key, RC.key], writes=[nxt.key])
            z = nxt
            S.op("dve", lambda e: e.tensor_tensor(out=z.ap[:, 16:16 + T], in0=z.ap[:, 16:16 + T], in1=PP.ap[:, 16:16 + T],
                                                  op=ALU.subtract), reads=[z.key, PP.key], writes=[z.key])
            for tb in range(T // 512):
                ps = S.psum()
                S.op("pe", lambda e: e.matmul(ps.ap, lhsT=PW.ap[:, gi, :], rhs=z.ap[:, 16 + tb * 512:16 + (tb + 1) * 512],
                                              start=True, stop=True), reads=[PW.key, z.key], writes=[ps.key])
                o = ob.next()
                S.op("dve", lambda e: e.tensor_scalar(out=o.ap, in0=ps.ap, scalar1=PV.ap[:, gi:gi + 1],
                                                      scalar2=PV.ap[:, 4 + gi:5 + gi], op0=ALU.add, op1=ALU.mult),
                     reads=[ps.key, PV.key], writes=[o.key])
                S.dma("sp", self.YT.ap[768 + gi * 128:768 + (gi + 1) * 128, tb * 512:(tb + 1) * 512], o.ap, reads=[o.key],
                      writes=[self.YT.key])

def wdn_view(wdn, l):
    return Buf(wdn.ap[l], wdn.key)


def tok_layout(w, G):
    K, N = w.shape
    nk = (K + 127) // 128
    ng = (nk + G - 1) // G
    ncb = (N + 511) // 512
    wp = np.zeros((ng * G * 128, ncb * 512), np.float32)
    wp[:K, :N] = w
    wp = wp.reshape(ng, G, 128, ncb, 512).transpose(3, 0, 2, 1, 4)
    return np.ascontiguousarray(wp).reshape(ncb, ng, 128, G * 512)


def host_layout(inp, nlayer):
    L = nlayer
    o = {}
    for i, nm in ((1, "ffn1"), (2, "ffn2")):
        wu = inp[nm + "_w_up"][:L]
        wu = wu.reshape(L, 16, 128, 2, NF, 128).transpose(0, 4, 2, 3, 1, 5)
        o["wup%d" % i] = np.ascontiguousarray(wu).reshape(L, NF, 128, 4096)
        o["wdn%d" % i] = np.stack([tok_layout(inp[nm + "_w_down"][l], 4) for l in range(L)])
    lns = {1: (inp["ln1_g"], inp["ln1_b"]), 2: (inp["ln2_g"], inp["ln2_b"]), 3: (inp["ln3_g"], inp["ln3_b"])}
    for i in (1, 2, 3):
        o["ln%dg" % i] = np.ascontiguousarray(lns[i][0][:L])
        o["ln%db" % i] = np.ascontiguousarray(lns[i][1][:L])
    win = inp["w_in"][:L]
    nb = 3072
    fm_cols = np.concatenate([np.arange(0, 3072), nb + np.arange(768, 1152)])
    tm_cols = np.concatenate([nb + np.arange(0, 768), nb + np.arange(1152, 1344), nb + np.arange(1536, 1728),
                              nb + np.arange(1344, 1536), nb + np.arange(1728, 1920), nb + np.arange(1920, 1956)])
    wfm = win[:, :, fm_cols].reshape(L, 16, 128, NFM, 128).transpose(0, 3, 2, 1, 4)
    o["winfm"] = np.ascontiguousarray(wfm).reshape(L, NFM, 128, 2048)
    o["wintm"] = np.stack([tok_layout(win[l][:, tm_cols], 4) for l in range(L)])
    o["wout"] = np.stack([tok_layout(inp["w_out"][l], 4) for l in range(L)])
    o["rwmu"] = np.ascontiguousarray(inp["rw_mu"][:L].reshape(L, 20, 128).transpose(0, 2, 1))
    o["gateb"] = np.ascontiguousarray(inp["nsa_gate_b"][:L])
    o["rw_w2"] = np.ascontiguousarray(inp["rw_w2"][:L])
    o["rw_a2"] = np.ascontiguousarray(inp["rw_a2"][:L])
    o["rw_g2"] = np.ascontiguousarray(inp["rw_g2"][:L])
    vecs = [inp["rw_w0"], inp["rw_a0"], inp["rw_k_k"], inp["rw_k_a"], inp["rw_r_k"].reshape(-1, 768), inp["rw_gn_g"],
            inp["rw_gn_b"]]
    o["rwvec"] = np.ascontiguousarray(np.stack([v[:L].reshape(L, 12, 64) for v in vecs], axis=1).transpose(0, 3, 1, 2))
    o["poolw"] = np.ascontiguousarray(inp["pool_w"][:L])
    w1s = np.stack([inp["nsa_cmp_k_w1"][:L], inp["nsa_cmp_v_w1"][:L]], axis=1)
    o["cw1"] = np.ascontiguousarray(w1s.transpose(0, 1, 3, 2, 4)).reshape(L, 2, 64, 32 * 256)
    w2s = np.stack([inp["nsa_cmp_k_w2"][:L], inp["nsa_cmp_v_w2"][:L]], axis=1)
    o["cw2"] = np.ascontiguousarray(w2s.reshape(L, 2, 2, 128, 64).transpose(0, 1, 3, 2, 4))
    pes = np.stack([inp["nsa_cmp_pe_k"][:L], inp["nsa_cmp_pe_v"][:L]], axis=1)
    o["cpe"] = np.ascontiguousarray(pes.transpose(0, 1, 3, 2))
    pv = np.concatenate([inp["pool_b"][:L].reshape(L, 4, 128), inp["pool_scale"][:L].reshape(L, 4, 128)], axis=1)
    o["poolv"] = np.ascontiguousarray(pv.transpose(0, 2, 1))
    return o


def host_consts():
    c = {}
    inv_freq = (500000.0 ** (-np.arange(8, dtype=np.float32) / 8)).astype(np.float32)
    c["c_ident"] = np.eye(128, dtype=np.float32)
    ii = np.arange(128)
    c["c_masks"] = np.stack([(ii[:, None] < ii[None, :]), (ii[:, None] <= ii[None, :]), (ii[:, None] > ii[None, :])]).astype(np.float32)
    angc = (np.arange(256, dtype=np.float32) * 16 + 31)[:, None] * inv_freq
    c["c_ropec"] = np.concatenate([np.cos(angc), np.sin(angc)], axis=1).astype(np.float32)
    n_cmp = 255
    cmp_start = np.arange(256) * 16
    sel_start = np.arange(64) * 64
    ov = np.clip(np.minimum(cmp_start[:, None] + 32, sel_start[None, :] + 64) - np.maximum(cmp_start[:, None], sel_start[None, :]),
                 0, None) / 32.0
    ovl = np.concatenate([ov, np.ones((256, 1))], axis=1).astype(np.float32)
    ovl[255] = 0.0
    c["c_ovl"] = ovl.reshape(2, 128, 65)
    jj = np.arange(64)[:, None, None]
    c["c_expand"] = (jj == (2 * np.arange(32)[None, :, None] + np.arange(128)[None, None, :] // 64)).astype(np.float32)
    kk_, qq_ = np.arange(128)[:, None], np.arange(128)[None, :]
    c["c_caus"] = np.stack([np.where(kk_ > qq_, NEG, 0.0), np.where(kk_ <= qq_, NEG, 0.0)]).astype(np.float32)
    qabs = np.arange(SEQ).reshape(32, 128)
    cur = qabs // 64
    jb = np.arange(64)[None, None, :]
    forced = (jb == 0) | (jb == cur[:, :, None]) | (jb == cur[:, :, None] - 1)
    c["c_force"] = np.where(forced, 1e9, 0.0).astype(np.float32)
    c["c_valid"] = np.where(jb > cur[:, :, None], -1e30, 0.0).astype(np.float32)
    nabs = np.arange(256).reshape(2, 128)
    cend = nabs * 16 + 31
    ok = (cend[None, :, :, None] <= qabs[:, None, None, :]) & (nabs[None, :, :, None] < n_cmp)
    c["c_cmask"] = np.where(ok, 0.0, NEG).astype(np.float32)
    t1 = np.arange(1, SEQ + 1, dtype=np.float32)
    c["c_rcnt"] = np.stack([1.0 / np.minimum(t1, float(w)) for w in (2, 4, 8, 16)]).astype(np.float32)
    half = 8
    inv_freq = (500000.0 ** (-np.arange(half, dtype=np.float32) / half)).astype(np.float32)
    ang = np.arange(SEQ, dtype=np.float32)[:, None] * inv_freq
    c["c_rope"] = np.concatenate([np.cos(ang), np.sin(ang)], axis=1).astype(np.float32)
    return c


_CACHE = {}


def run(inputs, nlayer=NLAYER, trun=SEQ, debug=False, stop_after=None, trace=False):
    key = (nlayer, trun, debug, stop_after)
    if key not in _CACHE:
        _CACHE[key] = Prog(nlayer, trun, debug, stop_after)
    prog = _CACHE[key]
    shared = host_layout(inputs, nlayer)
    shared.update(host_consts())
    in_maps = []
    for c in range(8):
        m = dict(shared)
        m["x"] = np.ascontiguousarray(inputs["x"][c % 4])
        m = {k: v for k, v in m.items() if k in prog.din}
        in_maps.append(m)
    res = run_bass_kernel_spmd(prog.nc, in_maps, core_ids=list(range(8)), trace=trace)
    return prog, res


def kernel(**inputs):
    inputs = {k: np.asarray(v) for k, v in inputs.items()}
    prog, res = run(inputs)
    out = np.stack([res.results[b]["out"] for b in range(4)], axis=0)
    return out.astype(np.float32)
```

```python
from contextlib import ExitStack
import numpy as np
import ml_dtypes
import concourse.bass as bass
import concourse.mybir as mybir
from concourse.bass_utils import run_bass_kernel_spmd

F32 = mybir.dt.float32
BF16 = mybir.dt.bfloat16
AF = mybir.ActivationFunctionType
ALU = mybir.AluOpType
AX = mybir.AxisListType

D = 2048
SEQ = 4096
NLAYER = 4
DFF = 5504
NF = 43
ALPHA = float((2 * NLAYER) ** 0.25)
LN_EPS = 1e-5
RWC = 2560
NFM = 27
NTM = 1572
CDEC = float(np.exp(-0.5))
GN_EPS = 64e-5
NEG = -30000.0


class Key:
    __slots__ = ("name", "w", "r")

    def __init__(self, name=""):
        self.name = name
        self.w = None
        self.r = []


class Buf:
    def __init__(self, ap, key):
        self.ap = ap
        self.key = key


class Sched:
    EPOCH = 30000
    NDS = 40

    def __init__(self, nc):
        self.nc = nc
        self.engs = {"pe": nc.tensor, "act": nc.scalar, "dve": nc.vector, "pool": nc.gpsimd, "sp": nc.sync}
        self.cnt = {e: 0 for e in self.engs}
        self.sems = {e: [] for e in self.engs}
        self.known = {e: {} for e in self.engs}
        self.NCS = 8
        self.dsems = [nc.alloc_semaphore("dq%d" % i) for i in range(self.NDS + self.NCS)]
        self.dcum = [0] * (self.NDS + self.NCS)
        self.dnext = 0
        self.cnext = 0
        self.banks = []
        for i in range(8):
            t = nc.alloc_psum_tensor("psb%d" % i, [128, 512], F32).ap()
            self.banks.append(Buf(t, Key("ps%d" % i)))
        self.bnext = 0
        self.ninst = 0
        self.pooltok = []

    def psum(self):
        b = self.banks[self.bnext]
        self.bnext = (self.bnext + 1) % 8
        return b

    def _esem(self, e, n):
        i = n // self.EPOCH
        while len(self.sems[e]) <= i:
            self.sems[e].append(self.nc.alloc_semaphore("s_%s_%d" % (e, len(self.sems[e]))))
        return self.sems[e][i], n % self.EPOCH + 1

    def _wait(self, e, tok):
        if tok[0] == "E":
            _, src, n = tok
            if self.known[e].get(("E", src), -1) >= n:
                return
            if src == e and e == "pe":
                return
            sem, val = self._esem(src, n)
            self.engs[e].wait_ge(sem, val)
            self.known[e][("E", src)] = n
        else:
            _, s, val = tok
            if self.known[e].get(("D", s), 0) >= val:
                return
            self.engs[e].wait_ge(self.dsems[s], val)
            self.known[e][("D", s)] = val
        self.ninst += 1

    def _deps(self, e, reads, writes, is_dma):
        deps = []
        for k in reads:
            if k.w is not None:
                deps.append(k.w)
        for k in writes:
            if k.w is not None:
                deps.append(k.w)
            for t in k.r:
                if (not is_dma) and t[0] == "E" and t[1] == e:
                    continue
                deps.append(t)
        for t in deps:
            self._wait(e, t)

    def _commit(self, e, tok, reads, writes):
        for k in reads:
            if tok[0] == "E":
                k.r = [t for t in k.r if not (t[0] == "E" and t[1] == tok[1])]
            k.r.append(tok)
        for k in writes:
            k.w = tok
            k.r = []

    def op(self, e, fn, reads=(), writes=()):
        self._deps(e, reads, writes, False)
        ins = fn(self.engs[e])
        n = self.cnt[e]
        self.cnt[e] += 1
        sem, _ = self._esem(e, n)
        ins.then_inc(sem, 1)
        self._commit(e, ("E", e, n), reads, writes)
        self.ninst += 1
        return ins

    def dma(self, q, out, in_, reads=(), writes=(), conv=False):
        if conv:
            s = self.NDS + self.cnext
            self.cnext = (self.cnext + 1) % self.NCS
        else:
            s = self.dnext
            self.dnext = (self.dnext + 1) % self.NDS
        if self.dcum[s] > 0:
            self._wait(q, ("D", s, self.dcum[s]))
        if q == "pool":
            if len(self.pooltok) >= 4:
                self._wait(q, self.pooltok[-4])
        self._deps(q, reads, writes, True)
        ins = self.engs[q].dma_start(out=out, in_=in_)
        self.dcum[s] += 16
        ins.then_inc(self.dsems[s], 16)
        self._commit(q, ("D", s, self.dcum[s]), reads, writes)
        if q == "pool":
            self.pooltok.append(("D", s, self.dcum[s]))
            self.pooltok = self.pooltok[-8:]
        self.ninst += 1
        return ins

    def barrier(self):
        for e in self.engs:
            for src in ("pe", "act", "dve", "pool"):
                if self.cnt[src] > 0:
                    n = self.cnt[src] - 1
                    if src == e and e == "pe":
                        continue
                    if self.known[e].get(("E", src), -1) < n:
                        sem, val = self._esem(src, n)
                        self.engs[e].wait_ge(sem, val)
                        self.known[e][("E", src)] = n
            for s in range(len(self.dsems)):
                if self.dcum[s] > 0:
                    self._wait(e, ("D", s, self.dcum[s]))


class Pool:
    def __init__(self, nc, name, shape, dtype, n):
        self.bufs = [sb(nc, "%s%d" % (name, i), shape, dtype) for i in range(n)]
        self.i = 0

    def next(self):
        b = self.bufs[self.i]
        self.i = (self.i + 1) % len(self.bufs)
        return b


_STACK = [None]


_CNT = [0]


def sb(nc, name, shape, dtype):
    _CNT[0] += 1
    name = "%s_u%d" % (name, _CNT[0])
    h = _STACK[0].enter_context(nc.sbuf_tensor(name, list(shape), dtype))
    return Buf(h.ap(), Key(name))


class Prog:
    def __init__(self, nlayer=NLAYER, trun=SEQ, debug=False, stop_after=None):
        self.nlayer = nlayer
        self.T = trun
        self.NTB = trun // 512
        self.debug = debug
        self.stop_after = stop_after
        nc = bass.Bass("TRN2", target_bir_lowering=False)
        self.nc = nc
        self.S = Sched(nc)
        self.din = {}
        self._declare_io()
        self.build()

    def inp(self, name, shape, dtype=F32):
        t = nc_t = self.nc.dram_tensor(name, list(shape), dtype, kind="ExternalInput").ap()
        self.din[name] = (tuple(shape), dtype)
        return Buf(t, Key(name))

    def scratch(self, name, shape, dtype=F32):
        kind = "ExternalOutput" if self.debug else "Internal"
        t = self.nc.dram_tensor(name, list(shape), dtype, kind=kind).ap()
        return Buf(t, Key(name))

    def _declare_io(self):
        L = self.nlayer
        T = self.T
        self.x_in = self.inp("x", [SEQ, D])
        self.out = Buf(self.nc.dram_tensor("out", [SEQ, D], F32, kind="ExternalOutput").ap(), Key("out"))
        self.wup = [self.inp("wup%d" % i, [L, NF, 128, 4096]) for i in (1, 2)]
        self.wdn = [self.inp("wdn%d" % i, [L, 4, 11, 128, 2048]) for i in (1, 2)]
        self.lng = [self.inp("ln%dg" % i, [L, D]) for i in (1, 2, 3)]
        self.lnb = [self.inp("ln%db" % i, [L, D]) for i in (1, 2, 3)]
        self.winfm = self.inp("winfm", [L, NFM, 128, 2048])
        self.wintm = self.inp("wintm", [L, 4, 4, 128, 2048])
        self.wout = self.inp("wout", [L, 4, 4, 128, 2048])
        self.mu = self.inp("rwmu", [L, 128, 20])
        self.gateb = self.inp("gateb", [L, 36])
        self.rw_w2 = self.inp("rw_w2", [L, 64, 768])
        self.rw_a2 = self.inp("rw_a2", [L, 64, 768])
        self.rw_g2 = self.inp("rw_g2", [L, 128, 768])
        self.rwvec = self.inp("rwvec", [L, 64, 7, 12])
        self.poolw = self.inp("poolw", [L, 4, 128, 128])
        self.poolv = self.inp("poolv", [L, 128, 8])
        self.cw1 = self.inp("cw1", [L, 2, 64, 32 * 256])
        self.cw2 = self.inp("cw2", [L, 2, 128, 2, 64])
        self.cpe = self.inp("cpe", [L, 2, 64, 32])
        self.c_ropec = self.inp("c_ropec", [256, 16])
        self.c_ovl = self.inp("c_ovl", [2, 128, 65])
        self.c_expand = self.inp("c_expand", [64, 32, 128])
        self.c_caus = self.inp("c_caus", [2, 128, 128])
        self.c_force = self.inp("c_force", [32, 128, 64])
        self.c_valid = self.inp("c_valid", [32, 128, 64])
        self.c_cmask = self.inp("c_cmask", [32, 2, 128, 128])
        self.c_masks = self.inp("c_masks", [3, 128, 128])
        self.c_rcnt = self.inp("c_rcnt", [4, SEQ])
        self.c_ident = self.inp("c_ident", [128, 128])
        self.c_rope = self.inp("c_rope", [SEQ, 16])
        self.X1 = self.scratch("X1", [SEQ, D])
        self.XN = self.scratch("XN", [SEQ, D])
        self.FMT = self.scratch("FMT", [NFM * 128, SEQ])
        self.QKT = self.scratch("QKT", [1152, SEQ], BF16)
        self.VSW = self.scratch("VSW", [SEQ, 384], BF16)
        self.GL = self.scratch("GL", [SEQ, 36])
        self.YT = self.scratch("YT", [D, SEQ], BF16)

        def wscr(name, shape):
            t = self.nc.dram_tensor(name, list(shape), BF16, kind="Internal").ap()
            return t, [Key("%s_%d" % (name, i)) for i in range(shape[0])]
        self.wupB = [wscr("wupB%d" % i, [NF, 128, 4096]) for i in (1, 2)]
        self.wdnB = [wscr("wdnB%d" % i, [44, 128, 2048]) for i in (1, 2)]
        self.winfmB = wscr("winfmB", [NFM, 128, 2048])
        self.wintmB = wscr("wintmB", [16, 128, 2048])
        self.woutB = wscr("woutB", [16, 128, 2048])

    def _consts(self):
        nc, S = self.nc, self.S
        self.ident = sb(nc, "ident", [128, 128], F32)
        S.dma("sp", self.ident.ap, self.c_ident.ap, reads=[self.c_ident.key], writes=[self.ident.key])
        self.epsc = sb(nc, "epsc", [128, 1], F32)
        S.op("dve", lambda e: e.memset(self.epsc.ap, LN_EPS), writes=[self.epsc.key])

    def mm(self, ps, lhsT, rhs, start, stop, reads):
        self.S.op("pe", lambda e: e.matmul(ps.ap if isinstance(ps, Buf) else ps, lhsT=lhsT, rhs=rhs, start=start, stop=stop),
                  reads=reads, writes=[ps.key] if isinstance(ps, Buf) else [])

    def make_xT(self, xtok, xT, tog):
        S = self.S
        for dc in range(16):
            ps = S.psum()
            for tt in range(4):
                S.op("pe", lambda e, tt=tt: e.transpose(out=ps.ap[:, tt * 128:(tt + 1) * 128],
                                                         in_=xtok.ap[:, tt, dc * 128:(dc + 1) * 128],
                                                         identity=self.ident.ap),
                     reads=[self.kx[tt], self.ident.key], writes=[ps.key])
            if (dc + tog) % 2 == 0:
                S.op("act", lambda e: e.activation(out=xT.ap[:, dc, :], in_=ps.ap, func=AF.Copy),
                     reads=[ps.key], writes=[xT.key])
            else:
                S.op("dve", lambda e: e.tensor_copy(out=xT.ap[:, dc, :], in_=ps.ap), reads=[ps.key], writes=[xT.key])

    def tokmm(self, lhsT, nk, wsrc, ncb, G, evac, ncols=None):
        S = self.S
        ng = (nk + G - 1) // G
        for cb in range(ncb):
            nco = 512 if ncols is None else ncols[cb]
            banks = [S.psum() for _ in range(4)]
            for g in range(ng):
                w = self.wpool.next()
                wap, wkeys = wsrc
                S.dma(self.wq(), w.ap[:, 0:G * 512], wap[cb * ng + g], reads=[wkeys[cb * ng + g]], writes=[w.key])
                for tt in range(4):
                    for fi in range(G):
                        f = g * G + fi
                        if f >= nk:
                            continue
                        self.S.op("pe", lambda e, tt=tt, f=f, fi=fi: e.matmul(
                            banks[tt].ap[:, 0:nco], lhsT=lhsT.ap[:, f, tt * 128:(tt + 1) * 128],
                            rhs=w.ap[:, fi * 512:fi * 512 + nco], start=(f == 0), stop=(f == nk - 1)),
                            reads=[lhsT.key, w.key], writes=[banks[tt].key])
            for tt in range(4):
                evac(tt, cb, banks[tt], nco)

    def convert(self, l, which):
        S = self.S

        def cv(dstp, src_ap, src_key):
            dst, keys = dstp
            for i in range(len(keys)):
                S.dma("pool", dst[i], src_ap[i], reads=[src_key], writes=[keys[i]], conv=True)
        if which == 0:
            cv(self.wupB[0], self.wup[0].ap[l], self.wup[0].key)
            cv(self.wdnB[0], self.wdn[0].ap[l].rearrange("a b p c -> (a b) p c"), self.wdn[0].key)
            cv(self.winfmB, self.winfm.ap[l], self.winfm.key)
            cv(self.wintmB, self.wintm.ap[l].rearrange("a b p c -> (a b) p c"), self.wintm.key)
        else:
            cv(self.woutB, self.wout.ap[l].rearrange("a b p c -> (a b) p c"), self.wout.key)
            cv(self.wupB[1], self.wup[1].ap[l], self.wup[1].key)
            cv(self.wdnB[1], self.wdn[1].ap[l].rearrange("a b p c -> (a b) p c"), self.wdn[1].key)

    def wq(self):
        self._wq = getattr(self, "_wq", 0) + 1
        return "pool"

    def layer_consts(self, l, first=True):
        nc, S = self.nc, self.S
        for i in ((0,) if first else (1, 2)):
            S.dma("sp", self.lnG[i].ap, self.lng[i].ap[l].partition_broadcast(128), reads=[self.lng[i].key],
                  writes=[self.lnG[i].key])
            S.dma("sp", self.lnB[i].ap, self.lnb[i].ap[l].partition_broadcast(128), reads=[self.lnb[i].key],
                  writes=[self.lnB[i].key])
        if not first:
            return
        S.dma("sp", self.MU.ap, self.mu.ap[l], reads=[self.mu.key], writes=[self.MU.key])
        S.dma("sp", self.GATEB.ap, self.gateb.ap[l].partition_broadcast(128), reads=[self.gateb.key],
              writes=[self.GATEB.key])

    def ffn_ln(self, l, which, xtok, xT, hT):
        S = self.S
        wupB, wdnB = self.wupB[which], self.wdnB[which]
        lni = 0 if which == 0 else 2
        for tt in range(4):
            S.op("act", lambda e, tt=tt: e.mul(xtok.ap[:, tt, :], xtok.ap[:, tt, :], ALPHA),
                 reads=[self.kx[tt]], writes=[self.kx[tt]])
        for f in range(NF):
            w = self.wpool.next()
            S.dma(self.wq(), w.ap, wupB[0][f], reads=[wupB[1][f]], writes=[w.key])
            pa, pb = S.psum(), S.psum()
            for kc in range(16):
                self.mm(pa, w.ap[:, kc * 128:(kc + 1) * 128], xT.ap[:, kc, :], kc == 0, kc == 15, [w.key, xT.key])
            for kc in range(16):
                self.mm(pb, w.ap[:, (16 + kc) * 128:(17 + kc) * 128], xT.ap[:, kc, :], kc == 0, kc == 15,
                        [w.key, xT.key])
            sa = self.sapool.next()
            S.op("act", lambda e: e.activation(out=sa.ap, in_=pa.ap, func=AF.Silu), reads=[pa.key], writes=[sa.key])
            S.op("dve", lambda e: e.tensor_tensor(out=hT.ap[:, f, :], in0=pb.ap, in1=sa.ap, op=ALU.mult),
                 reads=[pb.key, sa.key], writes=[hT.key])

        def evac(tt, cb, ps, nco):
            S.op("dve", lambda e: e.scalar_tensor_tensor(out=xtok.ap[:, tt, cb * 512:(cb + 1) * 512], in0=ps.ap,
                                                         scalar=0.5, in1=xtok.ap[:, tt, cb * 512:(cb + 1) * 512],
                                                         op0=ALU.mult, op1=ALU.add),
                 reads=[ps.key, self.kx[tt]], writes=[self.kx[tt]])
        self.tokmm(hT, NF, wdnB, 4, 4, evac)
        for tt in range(4):
            self.layernorm(xtok, tt, lni)

    def layernorm(self, xtok, tt, lni):
        S = self.S
        st, mv, rs = self.lnst, self.lnmv, self.lnrs
        for j in range(4):
            S.op("dve", lambda e, j=j: e.bn_stats(out=st.ap[:, j, :], in_=xtok.ap[:, tt, j * 512:(j + 1) * 512]),
                 reads=[self.kx[tt]], writes=[st.key])
        S.op("dve", lambda e: e.bn_aggr(out=mv.ap, in_=st.ap.rearrange("p a b -> p (a b)")), reads=[st.key], writes=[mv.key])
        S.op("act", lambda e: e.activation(out=rs.ap, in_=mv.ap[:, 1:2], func=AF.Sqrt, bias=self.epsc.ap, scale=1.0),
             reads=[mv.key, self.epsc.key], writes=[rs.key])
        S.op("dve", lambda e: e.reciprocal(out=rs.ap, in_=rs.ap), reads=[rs.key], writes=[rs.key])
        S.op("dve", lambda e: e.tensor_scalar(out=xtok.ap[:, tt, :], in0=xtok.ap[:, tt, :], scalar1=mv.ap[:, 0:1],
                                              scalar2=rs.ap[:, 0:1], op0=ALU.subtract, op1=ALU.mult),
             reads=[self.kx[tt], mv.key, rs.key], writes=[self.kx[tt]])
        S.op("dve", lambda e: e.tensor_tensor(out=xtok.ap[:, tt, :], in0=xtok.ap[:, tt, :], in1=self.lnG[lni].ap, op=ALU.mult),
             reads=[self.kx[tt], self.lnG[lni].key], writes=[self.kx[tt]])
        S.op("dve", lambda e: e.tensor_tensor(out=xtok.ap[:, tt, :], in0=xtok.ap[:, tt, :], in1=self.lnB[lni].ap, op=ALU.add),
             reads=[self.kx[tt], self.lnB[lni].key], writes=[self.kx[tt]])

    def alloc_loop(self):
        nc = self.nc
        self.xtok = sb(nc, "xtok", [128, 4, D], F32)
        self.kx = [Key("xtok%d" % i) for i in range(4)]
        self.xT = sb(nc, "xT", [128, 16, 512], BF16)
        self.hT = sb(nc, "hT", [128, NF, 512], BF16)
        self.wpool = Pool(nc, "wp", [128, 4096], BF16, 6)
        self.sapool = Pool(nc, "sa", [128, 512], F32, 2)
        self.lnst = sb(nc, "lnst", [128, 4, 6], F32)
        self.lnmv = sb(nc, "lnmv", [128, 2], F32)
        self.lnrs = sb(nc, "lnrs", [128, 1], F32)
        self.lnG = [sb(nc, "lnG%d" % i, [128, D], F32) for i in range(2)]
        self.lnB = [sb(nc, "lnB%d" % i, [128, D], F32) for i in range(2)]
        self.lnG.append(self.lnG[0])
        self.lnB.append(self.lnB[0])
        self.MU = sb(nc, "MU", [128, 20], F32)
        self.GATEB = sb(nc, "GATEB", [128, 36], F32)
        self.halo = sb(nc, "halo", [128, 20], F32)
        self.stage = Pool(nc, "stg", [128, 513], F32, 2)
        self.ost = Pool(nc, "ost", [128, 512], F32, 3)
        self.dtmp = sb(nc, "dtmp", [128, 512], F32)
        tmv = self.hT.ap.rearrange("p a b -> p (a b)").bitcast(F32)[:, 0:4 * NTM].rearrange("p (a c) -> p a c", a=4)
        self.TM = Buf(tmv, self.hT.key)
        self.CS = sb(nc, "CS", [128, 4, 16], F32)
        self.rtmp = sb(nc, "rtmp", [128, 4, 18, 8], F32)
        self.bst = Pool(nc, "bst", [128, 512], BF16, 3)
        self.vst = sb(nc, "vst", [128, 4, 384], BF16)
        self.gst = sb(nc, "gst", [128, 4, 36], F32)

    def loop1(self, l, src):
        S = self.S
        xtok, xT, hT = self.xtok, self.xT, self.hT
        S.op("dve", lambda e: e.memset(self.halo.ap, 0.0), writes=[self.halo.key])
        for tb in range(self.NTB):
            t0 = tb * 512
            for tt in range(4):
                S.dma("sp", xtok.ap[:, tt, :], src.ap[t0 + tt * 128:t0 + (tt + 1) * 128, :], reads=[src.key],
                      writes=[self.kx[tt]])
            S.dma("sp", self.CS.ap, self.c_rope.ap[t0:t0 + 512, :].rearrange("(a p) c -> p a c", p=128),
                  reads=[self.c_rope.key], writes=[self.CS.key])
            self.make_xT(xtok, xT, 0)
            self.ffn_ln(l, 0, xtok, xT, hT)
            for tt in range(4):
                S.dma("sp", self.X1.ap[t0 + tt * 128:t0 + (tt + 1) * 128, :], xtok.ap[:, tt, :], reads=[self.kx[tt]],
                      writes=[self.X1.key])
            self.make_xT(xtok, xT, 1)
            self.win_proj(l, t0, xT)

    def win_proj(self, l, t0, xT):
        S = self.S
        for ch in range(NFM):
            w = self.wpool.next()
            S.dma(self.wq(), w.ap[:, 0:2048], self.winfmB[0][ch], reads=[self.winfmB[1][ch]], writes=[w.key])
            ps = S.psum()
            for kc in range(16):
                self.mm(ps, w.ap[:, kc * 128:(kc + 1) * 128], xT.ap[:, kc, :], kc == 0, kc == 15, [w.key, xT.key])
            o = self.ost.next()
            if ch < 20:
                st = self.stage.next()
                d = self.dtmp
                S.op("act", lambda e: e.activation(out=st.ap[:, 1:513], in_=ps.ap, func=AF.Copy), reads=[ps.key],
                     writes=[st.key])
                S.op("dve", lambda e: e.tensor_copy(out=st.ap[:, 0:1], in_=self.halo.ap[:, ch:ch + 1]),
                     reads=[self.halo.key], writes=[st.key])
                S.op("dve", lambda e: e.tensor_sub(out=d.ap, in0=st.ap[:, 0:512], in1=st.ap[:, 1:513]), reads=[st.key],
                     writes=[d.key])
                S.op("dve", lambda e: e.tensor_copy(out=self.halo.ap[:, ch:ch + 1], in_=st.ap[:, 512:513]),
                     reads=[st.key], writes=[self.halo.key])
                S.op("dve", lambda e: e.scalar_tensor_tensor(out=o.ap, in0=d.ap, scalar=self.MU.ap[:, ch:ch + 1],
                                                             in1=st.ap[:, 1:513], op0=ALU.mult, op1=ALU.add),
                     reads=[d.key, st.key, self.MU.key], writes=[o.key])
            else:
                S.op("act", lambda e: e.activation(out=o.ap, in_=ps.ap, func=AF.Copy), reads=[ps.key], writes=[o.key])
            S.dma("sp", self.FMT.ap[ch * 128:(ch + 1) * 128, t0:t0 + 512], o.ap, reads=[o.key], writes=[self.FMT.key])
        TM = self.TM

        def evac(tt, cb, ps, nco):
            if (tt + cb) % 2 == 0:
                S.op("act", lambda e: e.activation(out=TM.ap[:, tt, cb * 512:cb * 512 + nco], in_=ps.ap[:, 0:nco], func=AF.Copy),
                     reads=[ps.key], writes=[TM.key])
            else:
                S.op("dve", lambda e: e.tensor_copy(out=TM.ap[:, tt, cb * 512:cb * 512 + nco], in_=ps.ap[:, 0:nco]),
                     reads=[ps.key], writes=[TM.key])
        self.tokmm(xT, 16, self.wintmB, 4, 4, evac, ncols=[512, 512, 512, 36])
        rt = self.rtmp
        for tt in range(4):
            v = TM.ap[:, tt, 0:1152].rearrange("p (h c) -> p h c", c=64)
            x1, x2 = v[:, :, 0:8], v[:, :, 8:16]
            cosb = self.CS.ap[:, tt, 0:8].unsqueeze(1).broadcast_to([128, 18, 8])
            sinb = self.CS.ap[:, tt, 8:16].unsqueeze(1).broadcast_to([128, 18, 8])
            rk = [TM.key, self.CS.key]
            S.op("dve", lambda e: e.tensor_tensor(out=rt.ap[:, 0], in0=x1, in1=cosb, op=ALU.mult), reads=rk, writes=[rt.key])
            S.op("dve", lambda e: e.tensor_tensor(out=rt.ap[:, 1], in0=x2, in1=sinb, op=ALU.mult), reads=rk, writes=[rt.key])
            S.op("dve", lambda e: e.tensor_tensor(out=rt.ap[:, 2], in0=x2, in1=cosb, op=ALU.mult), reads=rk, writes=[rt.key])
            S.op("dve", lambda e: e.tensor_tensor(out=rt.ap[:, 3], in0=x1, in1=sinb, op=ALU.mult), reads=rk, writes=[rt.key])
            S.op("dve", lambda e: e.tensor_tensor(out=x1, in0=rt.ap[:, 0], in1=rt.ap[:, 1], op=ALU.subtract),
                 reads=[rt.key], writes=[TM.key])
            S.op("dve", lambda e: e.tensor_tensor(out=x2, in0=rt.ap[:, 2], in1=rt.ap[:, 3], op=ALU.add),
                 reads=[rt.key], writes=[TM.key])
        for ch in range(9):
            ps = S.psum()
            for tt in range(4):
                S.op("pe", lambda e, tt=tt: e.transpose(out=ps.ap[:, tt * 128:(tt + 1) * 128],
                                                         in_=TM.ap[:, tt, ch * 128:(ch + 1) * 128], identity=self.ident.ap),
                     reads=[TM.key, self.ident.key], writes=[ps.key])
            o = self.bst.next()
            S.op("act", lambda e: e.mul(o.ap, ps.ap, (0.125 if ch < 6 else 1.0)), reads=[ps.key], writes=[o.key])
            S.dma("sp", self.QKT.ap[ch * 128:(ch + 1) * 128, t0:t0 + 512], o.ap, reads=[o.key], writes=[self.QKT.key])
        for tt in range(4):
            S.op("act", lambda e, tt=tt: e.activation(out=self.vst.ap[:, tt, :], in_=TM.ap[:, tt, 1152:1536], func=AF.Copy),
                 reads=[TM.key], writes=[self.vst.key])
            S.op("dve", lambda e, tt=tt: e.tensor_tensor(out=self.gst.ap[:, tt, :], in0=TM.ap[:, tt, 1536:1572],
                                                         in1=self.GATEB.ap, op=ALU.add),
                 reads=[TM.key, self.GATEB.key], writes=[self.gst.key])
        S.op("act", lambda e: e.activation(out=self.gst.ap, in_=self.gst.ap, func=AF.Sigmoid), reads=[self.gst.key],
             writes=[self.gst.key])
        S.dma("sp", self.VSW.ap[t0:t0 + 512, :].rearrange("(a p) c -> p a c", p=128), self.vst.ap, reads=[self.vst.key],
              writes=[self.VSW.key])
        S.dma("sp", self.GL.ap[t0:t0 + 512, :].rearrange("(a p) c -> p a c", p=128), self.gst.ap, reads=[self.gst.key],
              writes=[self.GL.key])

    def build(self):
        S = self.S
        self.gstack = ExitStack()
        _STACK[0] = self.gstack
        self._consts()
        src = self.x_in
        sa = self.stop_after
        self.convert(0, 0)
        for l in range(self.nlayer):
            last = (l == self.nlayer - 1)
            with ExitStack() as st:
                _STACK[0] = st
                self.alloc_loop()
                self.layer_consts(l, True)
                self.loop1(l, src)
                S.barrier()
            if sa == "loop1":
                break
            with ExitStack() as st:
                _STACK[0] = st
                self.convert(l, 1)
                if not last:
                    self.convert(l + 1, 0)
                self.rwkv(l)
                S.barrier()
            if sa == "rwkv":
                break
            with ExitStack() as st:
                _STACK[0] = st
                self.poolmix(l)
                S.barrier()
            if sa == "pool":
                break
            with ExitStack() as st:
                _STACK[0] = st
                self.nsa(l)
                S.barrier()
            if sa == "nsa":
                break
            with ExitStack() as st:
                _STACK[0] = st
                self.alloc_loop()
                self.layer_consts(l, False)
                self.loop3(l, self.out if last else self.XN)
                S.barrier()
            src = self.XN
        S.barrier()


    def rwkv(self, l):
        nc, S = self.nc, self.S
        T = self.T
        NCH = T // 128
        ident = self.ident

        def W(name, shape, dt=F32):
            return sb(nc, "rk%d_" % l + name, shape, dt)
        w2 = W("w2", [64, 768]); a2 = W("a2", [64, 768]); g2 = W("g2", [128, 768])
        vec = W("vec", [64, 7, 12])
        omka = W("omka", [64, 12])
        ones = W("ones", [64, 64])
        msk = W("msk", [128, 3, 128])
        rmask = W("rmask", [64, 4, 128])
        gneps = W("gneps", [128, 1])
        S.dma("sp", w2.ap, self.rw_w2.ap[l], reads=[self.rw_w2.key], writes=[w2.key])
        S.dma("sp", a2.ap, self.rw_a2.ap[l], reads=[self.rw_a2.key], writes=[a2.key])
        S.dma("sp", g2.ap, self.rw_g2.ap[l], reads=[self.rw_g2.key], writes=[g2.key])
        S.dma("sp", vec.ap, self.rwvec.ap[l], reads=[self.rwvec.key], writes=[vec.key])
        S.dma("sp", msk.ap, self.c_masks.ap.rearrange("m p c -> p m c"), reads=[self.c_masks.key], writes=[msk.key])
        S.op("dve", lambda e: e.memset(ones.ap, 1.0), writes=[ones.key])
        S.op("dve", lambda e: e.memset(gneps.ap, GN_EPS), writes=[gneps.key])
        S.op("dve", lambda e: e.memset(rmask.ap, 1.0), writes=[rmask.key])
        S.op("dve", lambda e: e.memset(rmask.ap[:, :, 0:1], 0.0), writes=[rmask.key])
        S.op("dve", lambda e: e.tensor_scalar(out=omka.ap, in0=vec.ap[:, 3, :], scalar1=-1.0, scalar2=1.0, op0=ALU.mult,
                                              op1=ALU.add), reads=[vec.key], writes=[omka.key])
        R = W("R", [64, 4, 128]); K = W("K", [64, 4, 128]); V = W("V", [64, 4, 128])
        SG = W("SG", [64, 4, 128]); A = W("A", [64, 4, 128]); GT = W("GT", [64, 4, 128])
        KM = W("KM", [64, 4, 128]); CUM = W("CUM", [64, 4, 128]); E1 = W("E1", [64, 4, 128])
        E2 = W("E2", [64, 4, 128]); BT = W("BT", [64, 4, 128]); BON = W("BON", [64, 4, 128])
        t1 = W("t1", [64, 4, 128]); t2 = W("t2", [64, 4, 128])
        WL = W("WL", [64, 128]); AL = W("AL", [64, 128]); GLr = W("GLr", [128, 128])
        tw = W("tw", [64, 128]); sg = W("sg", [128, 128])
        VT = W("VT", [128, 4, 64]); AHT = W("AHT", [128, 4, 64]); BPT = W("BPT", [128, 4, 64]); KPT = W("KPT", [128, 4, 64])
        AtT = W("AtT", [128, 4, 64]); WT = W("WT", [128, 4, 64]); Yk = W("Yk", [128, 4, 64]); Yd = W("Yd", [128, 4, 64])
        Ysq = W("Ysq", [128, 4, 64])
        Nm = W("Nm", [128, 4, 128]); NTm = W("NTm", [128, 4, 128]); MTm = W("MTm", [128, 4, 128])
        Pm = W("Pm", [128, 4, 128]); Qm = W("Qm", [128, 4, 128]); Pb = W("Pb", [128, 4, 128]); PbT = W("PbT", [128, 4, 128])
        Ta = W("Ta", [128, 4, 128]); Tb = W("Tb", [128, 4, 128]); X1 = W("X1", [128, 4, 128]); Zm = W("Zm", [128, 4, 128])
        Gs = W("Gs", [64, 4, 64]); DG = W("DG", [64, 4, 64]); Rt = W("Rt", [64, 4, 128])
        STs = [[W("ST%d_%d" % (g, i), [64, 4, 64]) for i in range(2)] for g in range(3)]
        sm = W("sm", [128, 4]); sv = W("sv", [128, 4])
        OB = W("OB", [64, 4, 128], BF16)
        OF = W("OF", [64, 4, 128])

        def bc_h(v, n):
            return v.unsqueeze(2).broadcast_to([64, 4, n])

        def dve(fn, reads, writes):
            S.op("dve", fn, reads=[b.key for b in reads], writes=[b.key for b in writes])

        def act(fn, reads, writes):
            S.op("act", fn, reads=[b.key for b in reads], writes=[b.key for b in writes])

        def tt_(out, in0, in1, op, reads, writes):
            dve(lambda e: e.tensor_tensor(out=out, in0=in0, in1=in1, op=op), reads, writes)

        def pemm(ps, sl, lhsT, rhs, reads, start=True, stop=True):
            S.op("pe", lambda e: e.matmul(sl, lhsT=lhsT, rhs=rhs, start=start, stop=stop),
                 reads=[b.key for b in reads], writes=[ps.key])

        fl = lambda b: b.ap.rearrange("p a b -> p (a b)")
        mU, mUi, mL = msk.ap[:, 0, :], msk.ap[:, 1, :], msk.ap[:, 2, :]
        b4 = lambda m: m.unsqueeze(1).broadcast_to([128, 4, 128])
        idb = ident.ap.unsqueeze(1).broadcast_to([128, 4, 128])
        id64b = ident.ap[0:64, 0:64].unsqueeze(1).broadcast_to([64, 4, 64])

        for hg in range(3):
            h0 = hg * 4
            vh = lambda i: vec.ap[:, i, h0:h0 + 4]
            S.op("dve", lambda e: e.memset(STs[hg][0].ap, 0.0), writes=[STs[hg][0].key])
            for c in range(NCH):
                t0 = c * 128
                STc, STn = STs[hg][c % 2], STs[hg][(c + 1) % 2]
                fm = self.FMT
                for (buf, base) in ((R, 0), (K, 768), (V, 1536)):
                    S.dma("sp", buf.ap, fm.ap[base + h0 * 64:base + (h0 + 4) * 64, t0:t0 + 128].rearrange("(h c) t -> c h t", c=64),
                          reads=[fm.key], writes=[buf.key])
                S.dma("sp", WL.ap, fm.ap[2304:2368, t0:t0 + 128], reads=[fm.key], writes=[WL.key])
                S.dma("sp", AL.ap, fm.ap[2368:2432, t0:t0 + 128], reads=[fm.key], writes=[AL.key])
                S.dma("sp", GLr.ap, fm.ap[2432:2560, t0:t0 + 128], reads=[fm.key], writes=[GLr.key])
                act(lambda e: e.activation(out=tw.ap, in_=WL.ap, func=AF.Tanh), [WL], [tw])
                act(lambda e: e.activation(out=sg.ap, in_=GLr.ap, func=AF.Sigmoid), [GLr], [sg])
                p1, p2, p3 = S.psum(), S.psum(), S.psum()
                for j in range(4):
                    h = h0 + j
                    pemm(p1, p1.ap[0:64, j * 128:(j + 1) * 128], w2.ap[:, h * 64:(h + 1) * 64], tw.ap, [w2, tw])
                    pemm(p2, p2.ap[0:64, j * 128:(j + 1) * 128], a2.ap[:, h * 64:(h + 1) * 64], AL.ap, [a2, AL])
                    pemm(p3, p3.ap[0:64, j * 128:(j + 1) * 128], g2.ap[:, h * 64:(h + 1) * 64], sg.ap, [g2, sg])
                v3 = lambda p: p.ap[0:64, :].rearrange("p (a b) -> p a b", a=4)
                tt_(t1.ap, v3(p1), bc_h(vh(0), 128), ALU.add, [p1, vec], [t1])
                act(lambda e: e.activation(out=SG.ap, in_=t1.ap, func=AF.Sigmoid), [t1], [SG])
                tt_(t2.ap, v3(p2), bc_h(vh(1), 128), ALU.add, [p2, vec], [t2])
                act(lambda e: e.activation(out=A.ap, in_=t2.ap, func=AF.Sigmoid), [t2], [A])
                act(lambda e: e.activation(out=GT.ap, in_=v3(p3), func=AF.Copy), [p3], [GT])
                tt_(t1.ap, A.ap, bc_h(vh(3), 128), ALU.mult, [A, vec], [t1])
                tt_(t1.ap, t1.ap, bc_h(omka.ap[:, h0:h0 + 4], 128), ALU.add, [t1, omka], [t1])
                tt_(KM.ap, K.ap, t1.ap, ALU.mult, [K, t1], [KM])
                tt_(K.ap, K.ap, bc_h(vh(2), 128), ALU.mult, [K, vec], [K])
                tt_(t1.ap, K.ap, K.ap, ALU.mult, [K], [t1])
                pn = S.psum()
                pemm(pn, pn.ap[0:64, :], ones.ap, fl(t1), [ones, t1])
                act(lambda e: e.activation(out=fl(t1), in_=pn.ap[0:64, :], func=AF.Sqrt), [pn], [t1])
                dve(lambda e: e.tensor_scalar(out=t1.ap, in0=t1.ap, scalar1=1e-12, scalar2=None, op0=ALU.max), [t1], [t1])
                dve(lambda e: e.reciprocal(out=t1.ap, in_=t1.ap), [t1], [t1])
                tt_(K.ap, K.ap, t1.ap, ALU.mult, [K, t1], [K])
                tt_(A.ap, K.ap, A.ap, ALU.mult, [K, A], [A])
                tt_(t2.ap, R.ap, KM.ap, ALU.mult, [R, KM], [t2])
                tt_(t2.ap, t2.ap, bc_h(vh(4), 128), ALU.mult, [t2, vec], [t2])
                pb_ = S.psum()
                pemm(pb_, pb_.ap[0:64, :], ones.ap, fl(t2), [ones, t2])
                tt_(BON.ap, v3(pb_), V.ap, ALU.mult, [pb_, V], [BON])
                dve(lambda e: e.tensor_tensor_scan(out=fl(CUM), data0=fl(rmask), data1=fl(SG), initial=0.0, op0=ALU.mult,
                                                   op1=ALU.add), [rmask, SG], [CUM])
                act(lambda e: e.activation(out=E1.ap, in_=CUM.ap, func=AF.Exp, scale=-CDEC), [CUM], [E1])
                act(lambda e: e.activation(out=E2.ap, in_=CUM.ap, func=AF.Exp, scale=CDEC), [CUM], [E2])
                tt_(SG.ap, CUM.ap, SG.ap, ALU.subtract, [CUM, SG], [SG])
                act(lambda e: e.activation(out=SG.ap, in_=SG.ap, func=AF.Exp, scale=-CDEC), [SG], [SG])
                tt_(CUM.ap, CUM.ap[:, :, 127:128].broadcast_to([64, 4, 128]), CUM.ap, ALU.subtract, [CUM], [CUM])
                act(lambda e: e.activation(out=CUM.ap, in_=CUM.ap, func=AF.Exp, scale=-CDEC), [CUM], [CUM])
                dve(lambda e: e.scalar_tensor_tensor(out=SG.ap, in0=K.ap, scalar=-1.0, in1=SG.ap, op0=ALU.mult, op1=ALU.mult),
                    [K, SG], [SG])
                tt_(R.ap, R.ap, E1.ap, ALU.mult, [R, E1], [R])
                tt_(BT.ap, A.ap, E2.ap, ALU.mult, [A, E2], [BT])
                tt_(E2.ap, KM.ap, E2.ap, ALU.mult, [KM, E2], [E2])
                tt_(A.ap, A.ap, CUM.ap, ALU.mult, [A, CUM], [A])
                tt_(KM.ap, KM.ap, CUM.ap, ALU.mult, [KM, CUM], [KM])
                AH, RH, KT, BP, KP = SG, R, E2, A, KM
                for (src_, dst_) in ((V, VT), (AH, AHT), (BP, BPT), (KP, KPT)):
                    ps = S.psum()
                    for j in range(4):
                        S.op("pe", lambda e, j=j: e.transpose(out=ps.ap[:, j * 64:(j + 1) * 64], in_=src_.ap[:, j, :],
                                                               identity=ident.ap[0:64, 0:64]),
                             reads=[src_.key, ident.key], writes=[ps.key])
                    act(lambda e: e.activation(out=fl(dst_), in_=ps.ap[:, 0:256], func=AF.Copy), [ps], [dst_])
                def stage(dst_, lh, rh, mask):
                    ps = S.psum()
                    for j in range(4):
                        pemm(ps, ps.ap[:, j * 128:(j + 1) * 128], lh.ap[:, j, :], rh.ap[:, j, :], [lh, rh])
                    tt_(dst_.ap, ps.ap.rearrange("p (a b) -> p a b", a=4), b4(mask), ALU.mult, [ps, msk], [dst_])
                stage(Nm, BT, AH, mU)
                stage(NTm, AH, BT, mL)
                stage(MTm, AH, KT, mL)
                stage(Pm, BT, RH, mUi)
                stage(Qm, KT, RH, mUi)
                tt_(Ta.ap, Nm.ap, idb, ALU.add, [Nm, ident], [Ta])
                Pp, PpT, Pn, PnT = Nm, NTm, Pb, PbT
                Tp, Tn = Ta, Tb
                for k in range(1, 7):
                    if k < 6:
                        ps = S.psum()
                        for j in range(4):
                            pemm(ps, ps.ap[:, j * 128:(j + 1) * 128], PpT.ap[:, j, :], Pp.ap[:, j, :], [PpT, Pp])
                        act(lambda e: e.activation(out=fl(Pn), in_=ps.ap, func=AF.Copy), [ps], [Pn])
                    ps2 = S.psum()
                    for j in range(4):
                        pemm(ps2, ps2.ap[:, j * 128:(j + 1) * 128], Pp.ap[:, j, :], PpT.ap[:, j, :], [Pp, PpT])
                    dve(lambda e: e.tensor_copy(out=fl(PnT), in_=ps2.ap), [ps2], [PnT])
                    ps3 = S.psum()
                    for j in range(4):
                        pemm(ps3, ps3.ap[:, j * 128:(j + 1) * 128], PnT.ap[:, j, :], Tp.ap[:, j, :], [PnT, Tp])
                    tt_(fl(Tn), ps3.ap, fl(Tp), ALU.add, [ps3, Tp], [Tn])
                    Pp, PpT, Pn, PnT = Pn, PnT, Pp, PpT
                    Tp, Tn = Tn, Tp
                Tf = Tp
                ps = S.psum()
                for j in range(4):
                    pemm(ps, ps.ap[:, j * 64:(j + 1) * 64], Tf.ap[:, j, :], AHT.ap[:, j, :], [Tf, AHT])
                act(lambda e: e.activation(out=fl(AtT), in_=ps.ap[:, 0:256], func=AF.Copy), [ps], [AtT])
                ps = S.psum()
                for j in range(4):
                    pemm(ps, ps.ap[:, j * 128:(j + 1) * 128], Tf.ap[:, j, :], MTm.ap[:, j, :], [Tf, MTm])
                dve(lambda e: e.tensor_copy(out=fl(X1), in_=ps.ap), [ps], [X1])
                ps = S.psum()
                for j in range(4):
                    pemm(ps, ps.ap[0:64, j * 64:(j + 1) * 64], AtT.ap[:, j, :], BPT.ap[:, j, :], [AtT, BPT])
                tt_(DG.ap, id64b, E1.ap[:, :, 127:128].broadcast_to([64, 4, 64]), ALU.mult, [ident, E1], [DG])
                tt_(fl(Gs), ps.ap[0:64, 0:256], fl(DG), ALU.add, [ps, DG], [Gs])
                ps = S.psum()
                for j in range(4):
                    pemm(ps, ps.ap[:, j * 64:(j + 1) * 64], X1.ap[:, j, :], BPT.ap[:, j, :], [X1, BPT])
                tt_(fl(WT), ps.ap[:, 0:256], fl(KPT), ALU.add, [ps, KPT], [WT])
                ps = S.psum()
                for j in range(4):
                    pemm(ps, ps.ap[0:64, j * 128:(j + 1) * 128], AtT.ap[:, j, :], Pm.ap[:, j, :], [AtT, Pm])
                tt_(fl(Rt), ps.ap[0:64, :], fl(RH), ALU.add, [ps, RH], [Rt])
                ps = S.psum()
                for j in range(4):
                    pemm(ps, ps.ap[:, j * 128:(j + 1) * 128], X1.ap[:, j, :], Pm.ap[:, j, :], [X1, Pm])
                tt_(fl(Zm), ps.ap, fl(Qm), ALU.add, [ps, Qm], [Zm])
                py = S.psum()
                for j in range(4):
                    pemm(py, py.ap[:, j * 64:(j + 1) * 64], Rt.ap[:, j, :], STc.ap[:, j, :], [Rt, STc], True, False)
                    pemm(py, py.ap[:, j * 64:(j + 1) * 64], Zm.ap[:, j, :], VT.ap[:, j, :], [Zm, VT], False, True)
                pst = S.psum()
                for j in range(4):
                    pemm(pst, pst.ap[0:64, j * 64:(j + 1) * 64], Gs.ap[:, j, :], STc.ap[:, j, :], [Gs, STc], True, False)
                    pemm(pst, pst.ap[0:64, j * 64:(j + 1) * 64], WT.ap[:, j, :], VT.ap[:, j, :], [WT, VT], False, True)
                act(lambda e: e.activation(out=fl(STn), in_=pst.ap[0:64, 0:256], func=AF.Copy), [pst], [STn])
                act(lambda e: e.activation(out=fl(Yk), in_=py.ap[:, 0:256], func=AF.Copy), [py], [Yk])
                dve(lambda e: e.tensor_reduce(out=sm.ap, in_=Yk.ap, axis=AX.X, op=ALU.add), [Yk], [sm])
                dve(lambda e: e.tensor_scalar(out=sm.ap, in0=sm.ap, scalar1=-1.0 / 64, scalar2=None, op0=ALU.mult), [sm], [sm])
                tt_(Yd.ap, Yk.ap, sm.ap.unsqueeze(2).broadcast_to([128, 4, 64]), ALU.add, [Yk, sm], [Yd])
                tt_(Ysq.ap, Yd.ap, Yd.ap, ALU.mult, [Yd], [Ysq])
                dve(lambda e: e.tensor_reduce(out=sv.ap, in_=Ysq.ap, axis=AX.X, op=ALU.add), [Ysq], [sv])
                act(lambda e: e.activation(out=sv.ap, in_=sv.ap, func=AF.Sqrt, bias=gneps.ap, scale=1.0 / 64), [sv, gneps], [sv])
                dve(lambda e: e.reciprocal(out=sv.ap, in_=sv.ap), [sv], [sv])
                tt_(Yd.ap, Yd.ap, sv.ap.unsqueeze(2).broadcast_to([128, 4, 64]), ALU.mult, [Yd, sv], [Yd])
                po = S.psum()
                for j in range(4):
                    S.op("pe", lambda e, j=j: e.transpose(out=po.ap[0:64, j * 128:(j + 1) * 128], in_=Yd.ap[:, j, :],
                                                           identity=ident.ap), reads=[Yd.key, ident.key], writes=[po.key])
                tt_(OF.ap, v3(po), bc_h(vh(5), 128), ALU.mult, [po, vec], [OF])
                tt_(OF.ap, OF.ap, bc_h(vh(6), 128), ALU.add, [OF, vec], [OF])
                tt_(OF.ap, OF.ap, BON.ap, ALU.add, [OF, BON], [OF])
                tt_(OB.ap, OF.ap, GT.ap, ALU.mult, [OF, GT], [OB])
                S.dma("sp", self.YT.ap[h0 * 64:(h0 + 4) * 64, t0:t0 + 128].rearrange("(h c) t -> c h t", c=64), OB.ap,
                      reads=[OB.key], writes=[self.YT.key])


    def nsa(self, l):
        nc, S = self.nc, self.S
        T = self.T
        NQT = T // 128
        NC = T // 16 - 1
        NT = (NC + 127) // 128
        ident = self.ident
        rot = {"i": 0, "a": 0}

        def psr():
            b = S.banks[rot["i"] % 5]
            rot["i"] += 1
            return b

        def psa():
            b = S.banks[5 + rot["a"] % 3]
            rot["a"] += 1
            return b

        def W(name, shape, dt=F32):
            return sb(nc, "ns%d_" % l + name, shape, dt)

        def dve(fn, reads, writes):
            S.op("dve", fn, reads=[b.key for b in reads], writes=[b.key for b in writes])

        def act(fn, reads, writes):
            S.op("act", fn, reads=[b.key for b in reads], writes=[b.key for b in writes])

        def pemm(ps, sl, lhsT, rhs, reads, start=True, stop=True):
            S.op("pe", lambda e: e.matmul(sl, lhsT=lhsT, rhs=rhs, start=start, stop=stop),
                 reads=[b.key for b in reads], writes=[ps.key])

        KCMPT = W("KCMPT", [64, 3, 256], BF16)
        VCMP = W("VCMP", [128, 2, 3, 65], BF16)
        OVL = W("OVL", [128, 2, 65], BF16)
        KST = W("KST", [64, 3, T], BF16)
        KWT = W("KWT", [64, 3, T], BF16)
        VS = W("VS", [128, NQT, 3, 65], BF16)
        VW = W("VW", [128, NQT, 3, 65], BF16)
        EXPD = W("EXPD", [64, 32, 128], BF16)
        identb = W("identb", [128, 128], BF16)
        CAUS = W("CAUS", [128, 4, 128], BF16)
        BAND = W("BAND", [128, 4, 128], BF16)
        S.op("dve", lambda e: e.memset(KCMPT.ap, 0.0), writes=[KCMPT.key])
        S.op("dve", lambda e: e.memset(VCMP.ap, 0.0), writes=[VCMP.key])
        S.op("dve", lambda e: e.memset(VCMP.ap[:, :, :, 64:65], 1.0), writes=[VCMP.key])
        S.op("dve", lambda e: e.memset(VS.ap[:, :, :, 64:65], 1.0), writes=[VS.key])
        S.op("dve", lambda e: e.memset(VW.ap[:, :, :, 64:65], 1.0), writes=[VW.key])
        S.op("dve", lambda e: e.tensor_copy(out=identb.ap, in_=ident.ap), reads=[ident.key], writes=[identb.key])
        S.dma("pool", OVL.ap, self.c_ovl.ap.rearrange("a p c -> p a c"), reads=[self.c_ovl.key], writes=[OVL.key])
        S.dma("pool", EXPD.ap, self.c_expand.ap, reads=[self.c_expand.key], writes=[EXPD.key])
        S.dma("pool", CAUS.ap, self.c_caus.ap[0].unsqueeze(1).broadcast_to([128, 4, 128]), reads=[self.c_caus.key],
              writes=[CAUS.key])
        S.dma("pool", BAND.ap, self.c_caus.ap[1].unsqueeze(1).broadcast_to([128, 4, 128]), reads=[self.c_caus.key],
              writes=[BAND.key])
        S.dma("sp", KST.ap, self.QKT.ap[768:960, 0:T].rearrange("(h c) t -> c h t", c=64), reads=[self.QKT.key],
              writes=[KST.key])
        S.dma("sp", KWT.ap, self.QKT.ap[960:1152, 0:T].rearrange("(h c) t -> c h t", c=64), reads=[self.QKT.key],
              writes=[KWT.key])
        for k0 in range(0, NQT, 4):
            for h in range(3):
                S.dma("sp", VS.ap[:, k0:k0 + 4, h, 0:64],
                      self.VSW.ap[k0 * 128:(k0 + 4) * 128, h * 64:(h + 1) * 64].rearrange("(k p) c -> p k c", p=128),
                      reads=[self.VSW.key], writes=[VS.key])
                S.dma("sp", VW.ap[:, k0:k0 + 4, h, 0:64],
                      self.VSW.ap[k0 * 128:(k0 + 4) * 128, 192 + h * 64:192 + (h + 1) * 64].rearrange("(k p) c -> p k c", p=128),
                      reads=[self.VSW.key], writes=[VW.key])

        w1 = W("w1", [64, 32, 256]); w2c = W("w2c", [128, 2, 64]); pe = W("pe", [64, 32])
        biasc = W("biasc", [128, 2]); XC = W("XC", [64, T]); Hc = W("Hc", [128, 2, 256])
        hx = W("hx", [128, 256]); hu = W("hu", [128, 256])
        KCt = W("KCt", [128, 2, 3, 64]); CSC = W("CSC", [128, 2, 16]); rtc = W("rtc", [128, 4, 6, 8])
        S.op("dve", lambda e: e.memset(KCt.ap, 0.0), writes=[KCt.key])
        S.dma("sp", CSC.ap, self.c_ropec.ap.rearrange("(a p) c -> p a c", p=128), reads=[self.c_ropec.key], writes=[CSC.key])
        XC3 = XC.ap.rearrange("p (n b) -> p n b", b=16)
        for kv in range(2):
            S.dma("sp", w1.ap, self.cw1.ap[l, kv].rearrange("d (a f) -> d a f", f=256), reads=[self.cw1.key], writes=[w1.key])
            S.dma("sp", w2c.ap, self.cw2.ap[l, kv], reads=[self.cw2.key], writes=[w2c.key])
            S.dma("sp", pe.ap, self.cpe.ap[l, kv], reads=[self.cpe.key], writes=[pe.key])
            for fh in range(2):
                ps = psr()
                for l_ in range(32):
                    pemm(ps, ps.ap[:, 0:1], w1.ap[:, l_, fh * 128:(fh + 1) * 128], pe.ap[:, l_:l_ + 1], [w1, pe], l_ == 0, l_ == 31)
                dve(lambda e: e.tensor_copy(out=biasc.ap[:, fh:fh + 1], in_=ps.ap[:, 0:1]), [ps], [biasc])
            for h in range(3):
                r0 = 3072 + kv * 192 + h * 64
                S.dma("sp", XC.ap, self.FMT.ap[r0:r0 + 64, 0:T], reads=[self.FMT.key], writes=[XC.key])
                for fh in range(2):
                    ps = psr()
                    for l_ in range(32):
                        a_, b_ = l_ // 16, l_ % 16
                        pemm(ps, ps.ap[:, 0:NC], w1.ap[:, l_, fh * 128:(fh + 1) * 128], XC3[:, a_:a_ + NC, b_], [w1, XC],
                             l_ == 0, l_ == 31)
                    x_, u_ = hx.ap[:, 0:NC], hu.ap[:, 0:NC]
                    dve(lambda e: e.tensor_scalar(out=x_, in0=ps.ap[:, 0:NC], scalar1=biasc.ap[:, fh:fh + 1], scalar2=None,
                                                  op0=ALU.add), [ps, biasc], [hx])
                    dve(lambda e: e.tensor_tensor(out=u_, in0=x_, in1=x_, op=ALU.mult), [hx], [hu])
                    dve(lambda e: e.tensor_scalar(out=u_, in0=u_, scalar1=0.044715, scalar2=1.0, op0=ALU.mult, op1=ALU.add),
                        [hu], [hu])
                    dve(lambda e: e.tensor_tensor(out=u_, in0=u_, in1=x_, op=ALU.mult), [hu, hx], [hu])
                    act(lambda e: e.activation(out=u_, in_=u_, func=AF.Tanh, scale=0.7978845608028654), [hu], [hu])
                    dve(lambda e: e.scalar_tensor_tensor(out=u_, in0=u_, scalar=1.0, in1=x_, op0=ALU.add, op1=ALU.mult),
                        [hu, hx], [hu])
                    dve(lambda e: e.tensor_scalar(out=Hc.ap[:, fh, 0:NC], in0=u_, scalar1=0.5, scalar2=None, op0=ALU.mult),
                        [hu], [Hc])
                for nt in range(NT):
                    n0 = nt * 128
                    nn = min(128, NC - n0)
                    ps = psr()
                    for fh in range(2):
                        pemm(ps, ps.ap[0:nn, 0:64], Hc.ap[:, fh, n0:n0 + nn], w2c.ap[:, fh, :], [Hc, w2c], fh == 0, fh == 1)
                    if kv == 1:
                        act(lambda e: e.activation(out=VCMP.ap[0:nn, nt, h, 0:64], in_=ps.ap[0:nn, 0:64], func=AF.Copy),
                            [ps], [VCMP])
                    else:
                        act(lambda e: e.activation(out=KCt.ap[0:nn, nt, h, :], in_=ps.ap[0:nn, 0:64], func=AF.Copy),
                            [ps], [KCt])
            if kv == 0:
                v = KCt.ap.rearrange("p a h c -> p (a h) c")
                x1, x2 = v[:, :, 0:8], v[:, :, 8:16]
                cosb = CSC.ap[:, :, 0:8].unsqueeze(2).broadcast_to([128, 2, 3, 8]).rearrange("p a h c -> p (a h) c") \
                    if False else None
                for nt in range(NT):
                    vv = KCt.ap[:, nt]
                    x1, x2 = vv[:, :, 0:8], vv[:, :, 8:16]
                    cb_ = CSC.ap[:, nt, 0:8].unsqueeze(1).broadcast_to([128, 3, 8])
                    sb_ = CSC.ap[:, nt, 8:16].unsqueeze(1).broadcast_to([128, 3, 8])
                    r_ = rtc.ap
                    dve(lambda e: e.tensor_tensor(out=r_[:, 0, 0:3], in0=x1, in1=cb_, op=ALU.mult), [KCt, CSC], [rtc])
                    dve(lambda e: e.tensor_tensor(out=r_[:, 1, 0:3], in0=x2, in1=sb_, op=ALU.mult), [KCt, CSC], [rtc])
                    dve(lambda e: e.tensor_tensor(out=r_[:, 2, 0:3], in0=x2, in1=cb_, op=ALU.mult), [KCt, CSC], [rtc])
                    dve(lambda e: e.tensor_tensor(out=r_[:, 3, 0:3], in0=x1, in1=sb_, op=ALU.mult), [KCt, CSC], [rtc])
                    dve(lambda e: e.tensor_tensor(out=x1, in0=r_[:, 0, 0:3], in1=r_[:, 1, 0:3], op=ALU.subtract), [rtc], [KCt])
                    dve(lambda e: e.tensor_tensor(out=x2, in0=r_[:, 2, 0:3], in1=r_[:, 3, 0:3], op=ALU.add), [rtc], [KCt])
                    for h in range(3):
                        ps = psr()
                        S.op("pe", lambda e: e.transpose(out=ps.ap[0:64, 0:128], in_=KCt.ap[:, nt, h, :], identity=ident.ap),
                             reads=[KCt.key, ident.key], writes=[ps.key])
                        act(lambda e: e.activation(out=KCMPT.ap[:, h, nt * 128:(nt + 1) * 128], in_=ps.ap[0:64, 0:128],
                                                   func=AF.Copy), [ps], [KCMPT])

        Gt = W("Gt", [128, 36]); FORCE = W("FORCE", [128, 64]); VALID = W("VALID", [128, 64])
        CM = W("CM", [128, 2, 4, 128], BF16)
        QTt = Pool(nc, "ns%dQT" % l, [64, 4, 128], BF16, 2)
        expool = Pool(nc, "ns%dex" % l, [128, 512], BF16, 4)
        rz = W("rz", [128, 4]); coef = W("coef", [128, 4]); imp = W("imp", [128, 64]); sc2 = W("sc2", [128, 64])
        m16 = W("m16", [128, 16]); negm = W("negm", [128, 64]); NEGT = W("NEGT", [64, 4, 128], BF16)
        YN = W("YN", [128, 4, 64]); ytmp = W("ytmp", [128, 4, 64]); OBn = W("OBn", [64, 4, 128], BF16)
        fl2 = lambda b: b.ap.rearrange("p a b -> p (a b)")

        def combine(bank, h, br, first):
            bv = bank.ap[:, 0:260].rearrange("p (g c) -> p g c", g=4)
            dve(lambda e: e.tensor_scalar(out=rz.ap, in0=bv[:, :, 64], scalar1=1e-30, scalar2=None, op0=ALU.max), [bank], [rz])
            dve(lambda e: e.reciprocal(out=rz.ap, in_=rz.ap), [rz], [rz])
            gv = Gt.ap[:, h * 12:(h + 1) * 12].rearrange("p (g b) -> p g b", b=3)[:, :, br]
            dve(lambda e: e.tensor_tensor(out=coef.ap, in0=rz.ap, in1=gv, op=ALU.mult), [rz, Gt], [coef])
            cb_ = coef.ap.unsqueeze(2).broadcast_to([128, 4, 64])
            if first:
                dve(lambda e: e.tensor_tensor(out=YN.ap, in0=bv[:, :, 0:64], in1=cb_, op=ALU.mult), [bank, coef], [YN])
            else:
                dve(lambda e: e.tensor_tensor(out=ytmp.ap, in0=bv[:, :, 0:64], in1=cb_, op=ALU.mult), [bank, coef], [ytmp])
                dve(lambda e: e.tensor_tensor(out=YN.ap, in0=YN.ap, in1=ytmp.ap, op=ALU.add), [YN, ytmp], [YN])

        for qt in range(NQT):
            q0 = qt * 128
            S.dma("sp", Gt.ap, self.GL.ap[q0:q0 + 128, :], reads=[self.GL.key], writes=[Gt.key])
            S.dma("sp", FORCE.ap, self.c_force.ap[qt], reads=[self.c_force.key], writes=[FORCE.key])
            S.dma("sp", VALID.ap, self.c_valid.ap[qt], reads=[self.c_valid.key], writes=[VALID.key])
            nts = [nt for nt in range(NT) if 16 * (nt * 128) + 31 <= q0 + 127]
            for nt in nts:
                S.dma("pool", CM.ap[:, nt], self.c_cmask.ap[qt, nt].unsqueeze(1).broadcast_to([128, 4, 128]),
                      reads=[self.c_cmask.key], writes=[CM.key])
            for h in range(3):
                qb = QTt.next()
                S.dma("sp", qb.ap, self.QKT.ap[h * 256:(h + 1) * 256, q0:q0 + 128].rearrange("(g c) t -> c g t", c=64),
                      reads=[self.QKT.key], writes=[qb.key])
                q2 = fl2(qb)
                po, pi = psa(), psa()
                for ii, nt in enumerate(nts):
                    ps = psr()
                    pemm(ps, ps.ap, KCMPT.ap[:, h, nt * 128:(nt + 1) * 128], q2, [KCMPT, qb], True, False)
                    pemm(ps, ps.ap, identb.ap, CM.ap[:, nt].rearrange("p a b -> p (a b)"), [identb, CM], False, True)
                    ex = expool.next()
                    act(lambda e: e.activation(out=ex.ap, in_=ps.ap, func=AF.Exp), [ps], [ex])
                    for g in range(4):
                        pemm(po, po.ap[:, g * 65:(g + 1) * 65], ex.ap[:, g * 128:(g + 1) * 128], VCMP.ap[:, nt, h, :], [ex, VCMP],
                             ii == 0 and g == 0, ii == len(nts) - 1 and g == 3)
                    for g in range(4):
                        pemm(pi, pi.ap[:, g * 65:(g + 1) * 65], ex.ap[:, g * 128:(g + 1) * 128], OVL.ap[:, nt, :], [ex, OVL],
                             ii == 0 and g == 0, ii == len(nts) - 1 and g == 3)
                piv = pi.ap[:, 0:260].rearrange("p (g c) -> p g c", g=4)
                dve(lambda e: e.tensor_scalar(out=rz.ap, in0=piv[:, :, 64], scalar1=1e-30, scalar2=None, op0=ALU.max), [pi], [rz])
                dve(lambda e: e.reciprocal(out=rz.ap, in_=rz.ap), [rz], [rz])
                dve(lambda e: e.tensor_scalar(out=imp.ap, in0=piv[:, 0, 0:64], scalar1=rz.ap[:, 0:1], scalar2=None, op0=ALU.mult),
                    [pi, rz], [imp])
                for g in range(1, 4):
                    dve(lambda e: e.scalar_tensor_tensor(out=imp.ap, in0=piv[:, g, 0:64], scalar=rz.ap[:, g:g + 1], in1=imp.ap,
                                                         op0=ALU.mult, op1=ALU.add), [pi, rz, imp], [imp])
                dve(lambda e: e.tensor_tensor(out=imp.ap, in0=imp.ap, in1=FORCE.ap, op=ALU.max), [imp, FORCE], [imp])
                dve(lambda e: e.tensor_tensor(out=imp.ap, in0=imp.ap, in1=VALID.ap, op=ALU.add), [imp, VALID], [imp])
                dve(lambda e: e.max(out=m16.ap[:, 0:8], in_=imp.ap), [imp], [m16])
                dve(lambda e: e.match_replace(out=sc2.ap, in_to_replace=m16.ap[:, 0:8], in_values=imp.ap, imm_value=-2e30),
                    [imp, m16], [sc2])
                dve(lambda e: e.max(out=m16.ap[:, 8:16], in_=sc2.ap), [sc2], [m16])
                dve(lambda e: e.tensor_scalar(out=negm.ap, in0=imp.ap, scalar1=m16.ap[:, 15:16], scalar2=None, op0=ALU.is_ge),
                    [imp, m16], [negm])
                dve(lambda e: e.tensor_scalar(out=negm.ap, in0=negm.ap, scalar1=-1.0, scalar2=-NEG, op0=ALU.add, op1=ALU.mult),
                    [negm], [negm])
                pT = psr()
                S.op("pe", lambda e: e.transpose(out=pT.ap[0:64, 0:128], in_=negm.ap, identity=ident.ap),
                     reads=[negm.key, ident.key], writes=[pT.key])
                dve(lambda e: e.tensor_copy(out=NEGT.ap, in_=pT.ap[0:64, 0:128].unsqueeze(1).broadcast_to([64, 4, 128])),
                    [pT], [NEGT])
                combine(po, h, 0, True)
                pss = psa()
                for kt in range(qt + 1):
                    ps = psr()
                    pemm(ps, ps.ap, KST.ap[:, h, kt * 128:(kt + 1) * 128], q2, [KST, qb], True, False)
                    pemm(ps, ps.ap, EXPD.ap[:, kt, :], fl2(NEGT), [EXPD, NEGT], False, kt != qt)
                    if kt == qt:
                        pemm(ps, ps.ap, identb.ap, fl2(CAUS), [identb, CAUS], False, True)
                    ex = expool.next()
                    act(lambda e: e.activation(out=ex.ap, in_=ps.ap, func=AF.Exp), [ps], [ex])
                    for g in range(4):
                        pemm(pss, pss.ap[:, g * 65:(g + 1) * 65], ex.ap[:, g * 128:(g + 1) * 128], VS.ap[:, kt, h, :], [ex, VS],
                             kt == 0 and g == 0, kt == qt and g == 3)
                combine(pss, h, 1, False)
                psw = psa()
                kts = list(range(max(0, qt - 4), qt + 1))
                for kt in kts:
                    ps = psr()
                    masked = (kt == qt) or (kt == qt - 4)
                    pemm(ps, ps.ap, KWT.ap[:, h, kt * 128:(kt + 1) * 128], q2, [KWT, qb], True, not masked)
                    if kt == qt:
                        pemm(ps, ps.ap, identb.ap, fl2(CAUS), [identb, CAUS], False, True)
                    elif kt == qt - 4:
                        pemm(ps, ps.ap, identb.ap, fl2(BAND), [identb, BAND], False, True)
                    ex = expool.next()
                    act(lambda e: e.activation(out=ex.ap, in_=ps.ap, func=AF.Exp), [ps], [ex])
                    for g in range(4):
                        pemm(psw, psw.ap[:, g * 65:(g + 1) * 65], ex.ap[:, g * 128:(g + 1) * 128], VW.ap[:, kt, h, :], [ex, VW],
                             kt == kts[0] and g == 0, kt == qt and g == 3)
                combine(psw, h, 2, False)
                pso = psr()
                for g in range(4):
                    S.op("pe", lambda e, g=g: e.transpose(out=pso.ap[0:64, g * 128:(g + 1) * 128], in_=YN.ap[:, g, :],
                                                           identity=ident.ap), reads=[YN.key, ident.key], writes=[pso.key])
                act(lambda e: e.activation(out=fl2(OBn), in_=pso.ap[0:64, :], func=AF.Copy), [pso], [OBn])
                S.dma("sp", self.YT.ap[1280 + h * 256:1280 + (h + 1) * 256, q0:q0 + 128].rearrange("(g c) t -> c g t", c=64),
                      OBn.ap, reads=[OBn.key], writes=[self.YT.key])

    def loop3(self, l, dst):
        S = self.S
        xtok, xT, hT = self.xtok, self.xT, self.hT
        for tb in range(self.NTB):
            t0 = tb * 512
            for tt in range(4):
                S.dma("sp", xtok.ap[:, tt, :], self.X1.ap[t0 + tt * 128:t0 + (tt + 1) * 128, :], reads=[self.X1.key],
                      writes=[self.kx[tt]])
            S.dma("sp", xT.ap, self.YT.ap[:, t0:t0 + 512].rearrange("(k p) t -> p k t", p=128), reads=[self.YT.key],
                  writes=[xT.key])
            for tt in range(4):
                S.op("act", lambda e, tt=tt: e.mul(xtok.ap[:, tt, :], xtok.ap[:, tt, :], ALPHA),
                     reads=[self.kx[tt]], writes=[self.kx[tt]])

            def evac(tt, cb, ps, nco):
                S.op("dve", lambda e: e.tensor_tensor(out=xtok.ap[:, tt, cb * 512:(cb + 1) * 512], in0=ps.ap,
                                                      in1=xtok.ap[:, tt, cb * 512:(cb + 1) * 512], op=ALU.add),
                     reads=[ps.key, self.kx[tt]], writes=[self.kx[tt]])
            self.tokmm(xT, 16, self.woutB, 4, 4, evac)
            for tt in range(4):
                self.layernorm(xtok, tt, 1)
            self.make_xT(xtok, xT, 0)
            self.ffn_ln(l, 1, xtok, xT, hT)
            for tt in range(4):
                S.dma("sp", dst.ap[t0 + tt * 128:t0 + (tt + 1) * 128, :], xtok.ap[:, tt, :], reads=[self.kx[tt]],
                      writes=[dst.key])

    def poolmix(self, l):
        nc, S = self.nc, self.S
        T = self.T
        PP = sb(nc, "pPP%d" % l, [128, 16 + T], F32)
        A = sb(nc, "pA%d" % l, [128, 16 + T], F32)
        Bq = sb(nc, "pB%d" % l, [128, 16 + T], F32)
        RC = sb(nc, "pRC%d" % l, [128, T], F32)
        PW = sb(nc, "pPW%d" % l, [128, 4, 128], F32)
        PV = sb(nc, "pPV%d" % l, [128, 8], F32)
        ob = Pool(nc, "pob%d_" % l, [128, 512], BF16, 2)
        S.dma("sp", PW.ap, self.poolw.ap[l].rearrange("g c d -> c g d"), reads=[self.poolw.key], writes=[PW.key])
        S.dma("sp", PV.ap, self.poolv.ap[l], reads=[self.poolv.key], writes=[PV.key])
        for b_ in (PP, A, Bq):
            S.op("dve", lambda e: e.memset(b_.ap[:, 0:16], 0.0), writes=[b_.key])
        for gi, win in enumerate((2, 4, 8, 16)):
            S.dma("sp", PP.ap[:, 16:16 + T], self.FMT.ap[2560 + gi * 128:2560 + (gi + 1) * 128, 0:T], reads=[self.FMT.key],
                  writes=[PP.key])
            S.dma("sp", RC.ap, self.c_rcnt.ap[gi, 0:T].partition_broadcast(128), reads=[self.c_rcnt.key], writes=[RC.key])
            cur, nxt = PP, A
            sh = 1
            while sh < win:
                S.op("dve", lambda e: e.tensor_tensor(out=nxt.ap[:, 16:16 + T], in0=cur.ap[:, 16:16 + T],
                                                      in1=cur.ap[:, 16 - sh:16 - sh + T], op=ALU.add),
                     reads=[cur.key], writes=[nxt.key])
                cur = nxt
                nxt = Bq if cur is A else A
                sh *= 2
            S.op("dve", lambda e: e.tensor_tensor(out=nxt.ap[:, 16:16 + T], in0=cur.ap[:, 16:16 + T], in1=RC.ap, op=ALU.mult),
                 reads=[cur.key, RC.key], writes=[nxt.key])
            z = nxt
            S.op("dve", lambda e: e.tensor_tensor(out=z.ap[:, 16:16 + T], in0=z.ap[:, 16:16 + T], in1=PP.ap[:, 16:16 + T],
                                                  op=ALU.subtract), reads=[z.key, PP.key], writes=[z.key])
            for tb in range(T // 512):
                ps = S.psum()
                S.op("pe", lambda e: e.matmul(ps.ap, lhsT=PW.ap[:, gi, :], rhs=z.ap[:, 16 + tb * 512:16 + (tb + 1) * 512],
                                              start=True, stop=True), reads=[PW.key, z.key], writes=[ps.key])
                o = ob.next()
                S.op("dve", lambda e: e.tensor_scalar(out=o.ap, in0=ps.ap, scalar1=PV.ap[:, gi:gi + 1],
                                                      scalar2=PV.ap[:, 4 + gi:5 + gi], op0=ALU.add, op1=ALU.mult),
                     reads=[ps.key, PV.key], writes=[o.key])
                S.dma("sp", self.YT.ap[768 + gi * 128:768 + (gi + 1) * 128, tb * 512:(tb + 1) * 512], o.ap, reads=[o.key],
                      writes=[self.YT.key])

def wdn_view(wdn, l):
    return Buf(wdn.ap[l], wdn.key)


def tok_layout(w, G):
    K, N = w.shape
    nk = (K + 127) // 128
    ng = (nk + G - 1) // G
    ncb = (N + 511) // 512
    wp = np.zeros((ng * G * 128, ncb * 512), np.float32)
    wp[:K, :N] = w
    wp = wp.reshape(ng, G, 128, ncb, 512).transpose(3, 0, 2, 1, 4)
    return np.ascontiguousarray(wp).reshape(ncb, ng, 128, G * 512)


def host_layout(inp, nlayer):
    L = nlayer
    o = {}
    for i, nm in ((1, "ffn1"), (2, "ffn2")):
        wu = inp[nm + "_w_up"][:L]
        wu = wu.reshape(L, 16, 128, 2, NF, 128).transpose(0, 4, 2, 3, 1, 5)
        o["wup%d" % i] = np.ascontiguousarray(wu).reshape(L, NF, 128, 4096)
        o["wdn%d" % i] = np.stack([tok_layout(inp[nm + "_w_down"][l], 4) for l in range(L)])
    lns = {1: (inp["ln1_g"], inp["ln1_b"]), 2: (inp["ln2_g"], inp["ln2_b"]), 3: (inp["ln3_g"], inp["ln3_b"])}
    for i in (1, 2, 3):
        o["ln%dg" % i] = np.ascontiguousarray(lns[i][0][:L])
        o["ln%db" % i] = np.ascontiguousarray(lns[i][1][:L])
    win = inp["w_in"][:L]
    nb = 3072
    fm_cols = np.concatenate([np.arange(0, 3072), nb + np.arange(768, 1152)])
    tm_cols = np.concatenate([nb + np.arange(0, 768), nb + np.arange(1152, 1344), nb + np.arange(1536, 1728),
                              nb + np.arange(1344, 1536), nb + np.arange(1728, 1920), nb + np.arange(1920, 1956)])
    wfm = win[:, :, fm_cols].reshape(L, 16, 128, NFM, 128).transpose(0, 3, 2, 1, 4)
    o["winfm"] = np.ascontiguousarray(wfm).reshape(L, NFM, 128, 2048)
    o["wintm"] = np.stack([tok_layout(win[l][:, tm_cols], 4) for l in range(L)])
    o["wout"] = np.stack([tok_layout(inp["w_out"][l], 4) for l in range(L)])
    o["rwmu"] = np.ascontiguousarray(inp["rw_mu"][:L].reshape(L, 20, 128).transpose(0, 2, 1))
    o["gateb"] = np.ascontiguousarray(inp["nsa_gate_b"][:L])
    o["rw_w2"] = np.ascontiguousarray(inp["rw_w2"][:L])
    o["rw_a2"] = np.ascontiguousarray(inp["rw_a2"][:L])
    o["rw_g2"] = np.ascontiguousarray(inp["rw_g2"][:L])
    vecs = [inp["rw_w0"], inp["rw_a0"], inp["rw_k_k"], inp["rw_k_a"], inp["rw_r_k"].reshape(-1, 768), inp["rw_gn_g"],
            inp["rw_gn_b"]]
    o["rwvec"] = np.ascontiguousarray(np.stack([v[:L].reshape(L, 12, 64) for v in vecs], axis=1).transpose(0, 3, 1, 2))
    o["poolw"] = np.ascontiguousarray(inp["pool_w"][:L])
    w1s = np.stack([inp["nsa_cmp_k_w1"][:L], inp["nsa_cmp_v_w1"][:L]], axis=1)
    o["cw1"] = np.ascontiguousarray(w1s.transpose(0, 1, 3, 2, 4)).reshape(L, 2, 64, 32 * 256)
    w2s = np.stack([inp["nsa_cmp_k_w2"][:L], inp["nsa_cmp_v_w2"][:L]], axis=1)
    o["cw2"] = np.ascontiguousarray(w2s.reshape(L, 2, 2, 128, 64).transpose(0, 1, 3, 2, 4))
    pes = np.stack([inp["nsa_cmp_pe_k"][:L], inp["nsa_cmp_pe_v"][:L]], axis=1)
    o["cpe"] = np.ascontiguousarray(pes.transpose(0, 1, 3, 2))
    pv = np.concatenate([inp["pool_b"][:L].reshape(L, 4, 128), inp["pool_scale"][:L].reshape(L, 4, 128)], axis=1)
    o["poolv"] = np.ascontiguousarray(pv.transpose(0, 2, 1))
    return o


def host_consts():
    c = {}
    inv_freq = (500000.0 ** (-np.arange(8, dtype=np.float32) / 8)).astype(np.float32)
    c["c_ident"] = np.eye(128, dtype=np.float32)
    ii = np.arange(128)
    c["c_masks"] = np.stack([(ii[:, None] < ii[None, :]), (ii[:, None] <= ii[None, :]), (ii[:, None] > ii[None, :])]).astype(np.float32)
    angc = (np.arange(256, dtype=np.float32) * 16 + 31)[:, None] * inv_freq
    c["c_ropec"] = np.concatenate([np.cos(angc), np.sin(angc)], axis=1).astype(np.float32)
    n_cmp = 255
    cmp_start = np.arange(256) * 16
    sel_start = np.arange(64) * 64
    ov = np.clip(np.minimum(cmp_start[:, None] + 32, sel_start[None, :] + 64) - np.maximum(cmp_start[:, None], sel_start[None, :]),
                 0, None) / 32.0
    ovl = np.concatenate([ov, np.ones((256, 1))], axis=1).astype(np.float32)
    ovl[255] = 0.0
    c["c_ovl"] = ovl.reshape(2, 128, 65)
    jj = np.arange(64)[:, None, None]
    c["c_expand"] = (jj == (2 * np.arange(32)[None, :, None] + np.arange(128)[None, None, :] // 64)).astype(np.float32)
    kk_, qq_ = np.arange(128)[:, None], np.arange(128)[None, :]
    c["c_caus"] = np.stack([np.where(kk_ > qq_, NEG, 0.0), np.where(kk_ <= qq_, NEG, 0.0)]).astype(np.float32)
    qabs = np.arange(SEQ).reshape(32, 128)
    cur = qabs // 64
    jb = np.arange(64)[None, None, :]
    forced = (jb == 0) | (jb == cur[:, :, None]) | (jb == cur[:, :, None] - 1)
    c["c_force"] = np.where(forced, 1e9, 0.0).astype(np.float32)
    c["c_valid"] = np.where(jb > cur[:, :, None], -1e30, 0.0).astype(np.float32)
    nabs = np.arange(256).reshape(2, 128)
    cend = nabs * 16 + 31
    ok = (cend[None, :, :, None] <= qabs[:, None, None, :]) & (nabs[None, :, :, None] < n_cmp)
    c["c_cmask"] = np.where(ok, 0.0, NEG).astype(np.float32)
    t1 = np.arange(1, SEQ + 1, dtype=np.float32)
    c["c_rcnt"] = np.stack([1.0 / np.minimum(t1, float(w)) for w in (2, 4, 8, 16)]).astype(np.float32)
    half = 8
    inv_freq = (500000.0 ** (-np.arange(half, dtype=np.float32) / half)).astype(np.float32)
    ang = np.arange(SEQ, dtype=np.float32)[:, None] * inv_freq
    c["c_rope"] = np.concatenate([np.cos(ang), np.sin(ang)], axis=1).astype(np.float32)
    return c


_CACHE = {}


def run(inputs, nlayer=NLAYER, trun=SEQ, debug=False, stop_after=None, trace=False):
    key = (nlayer, trun, debug, stop_after)
    if key not in _CACHE:
        _CACHE[key] = Prog(nlayer, trun, debug, stop_after)
    prog = _CACHE[key]
    shared = host_layout(inputs, nlayer)
    shared.update(host_consts())
    in_maps = []
    for c in range(8):
        m = dict(shared)
        m["x"] = np.ascontiguousarray(inputs["x"][c % 4])
        m = {k: v for k, v in m.items() if k in prog.din}
        in_maps.append(m)
    res = run_bass_kernel_spmd(prog.nc, in_maps, core_ids=list(range(8)), trace=trace)
    return prog, res


def kernel(**inputs):
    inputs = {k: np.asarray(v) for k, v in inputs.items()}
    prog, res = run(inputs)
    out = np.stack([res.results[b]["out"] for b in range(4)], axis=0)
    return out.astype(np.float32)
```
